# Optimizing a Trainium2 kernel written in Bass

```python
import jax, jax.numpy as jnp
from jax import lax
import numpy as np

D_MODEL = 1024
BATCH = 4
SEQ = 4096
DEPTH = 1

GRID_W = 64
CHUNK = 128
A_WIDTH = 512
A_GROUPS = 4
A_GROUP_DIM = A_WIDTH // A_GROUPS
B_HEADS = 8
B_HEAD_DIM = 64
B_WIDTH = B_HEADS * B_HEAD_DIM
WIN_ROWS_MAX = 8
WIN_COLS = 16
N_BRANCH = 2
D_FF = 4 * D_MODEL
IN_WIDTH = 2 * A_WIDTH + 3 * B_WIDTH + N_BRANCH * D_MODEL
RMS_EPS = 1e-6
LN_EPS = 1e-5

kernel_name = "hybrid_gmlp_natten_gated_encoder_block"


def rms_norm(x, g):
    xf = x.astype(jnp.float32)
    y = xf * lax.rsqrt(jnp.mean(xf * xf, axis=-1, keepdims=True) + RMS_EPS)
    return (y * g.astype(jnp.float32)).astype(x.dtype)


def layer_norm(x, g, b):
    xf = x.astype(jnp.float32)
    mu = jnp.mean(xf, axis=-1, keepdims=True)
    xc = xf - mu
    y = xc * lax.rsqrt(jnp.mean(xc * xc, axis=-1, keepdims=True) + LN_EPS)
    return (y * g.astype(jnp.float32) + b.astype(jnp.float32)).astype(x.dtype)


def gmlp_spatial_gating(u, v, ln_g, ln_b, w_s, b_s):
    bsz, s, _ = u.shape
    n_chunks = s // CHUNK
    vn = layer_norm(v, ln_g, ln_b).reshape(bsz, n_chunks, CHUNK, A_GROUPS, A_GROUP_DIM)
    mixed = jnp.einsum('gpq,bnqgc->bnpgc', w_s, vn) + b_s.T[:, :, None]
    return u * mixed.reshape(bsz, s, A_WIDTH)


def neighbourhood_attention_2d(q, k, v, q_g, k_g, rpb):
    bsz, s, n_h, hd = q.shape
    rows = s // GRID_W
    kh = min(WIN_ROWS_MAX, rows)
    q = rms_norm(q, q_g) * jnp.asarray(hd ** -0.5, dtype=q.dtype)
    k = rms_norm(k, k_g)
    to_grid = lambda t: t.reshape(bsz, rows, GRID_W, n_h, hd).transpose(0, 3, 1, 2, 4)
    qg, kg, vg = to_grid(q), to_grid(k), to_grid(v)
    cols = np.arange(GRID_W)
    col_start = np.clip(cols - WIN_COLS // 2, 0, GRID_W - WIN_COLS)
    col_idx = col_start[:, None] + np.arange(WIN_COLS)[None, :]
    dc = col_idx - cols[:, None] + (WIN_COLS - 1)
    col_bias = rpb[:, :, dc]

    def row_block(r):
        r0 = jnp.clip(r - kh // 2, 0, rows - kh)
        q_r = lax.dynamic_index_in_dim(qg, r, axis=2, keepdims=False)
        k_rows = lax.dynamic_slice_in_dim(kg, r0, kh, axis=2)
        v_rows = lax.dynamic_slice_in_dim(vg, r0, kh, axis=2)
        k_win = k_rows[:, :, :, col_idx]
        v_win = v_rows[:, :, :, col_idx]
        sc = jnp.einsum('bhqd,bhiqjd->bhqij', q_r, k_win).astype(jnp.float32)
        dr = r0 + jnp.arange(kh) - r + (WIN_ROWS_MAX - 1)
        bias = col_bias[:, dr].transpose(0, 2, 1, 3).astype(jnp.float32)
        sc = sc + bias[None]
        p = jax.nn.softmax(sc.reshape(bsz, n_h, GRID_W, kh * WIN_COLS), axis=-1)
        p = p.reshape(bsz, n_h, GRID_W, kh, WIN_COLS).astype(v.dtype)
        return jnp.einsum('bhqij,bhiqjd->bhqd', p, v_win)

    out = lax.map(row_block, jnp.arange(rows))
    return out.transpose(1, 0, 3, 2, 4).reshape(bsz, s, n_h * hd)


def setup_inputs(seed: int = 0) -> dict:
    key = jax.random.key(seed)
    ks = jax.random.split(key, 20)
    f32 = jnp.float32
    nrm = lambda k, shape, scale: jax.random.normal(k, shape, f32) * scale
    return {
        "x": jax.random.normal(ks[0], (BATCH, SEQ, D_MODEL), f32),
        "norm1_g": 1.0 + nrm(ks[1], (DEPTH, D_MODEL), 0.02),
        "w_in": nrm(ks[2], (DEPTH, D_MODEL, IN_WIDTH), D_MODEL ** -0.5),
        "b_gate": nrm(ks[3], (DEPTH, N_BRANCH * D_MODEL), 0.02),
        "gmlp_ln_g": 1.0 + nrm(ks[4], (DEPTH, A_WIDTH), 0.02),
        "gmlp_ln_b": nrm(ks[5], (DEPTH, A_WIDTH), 0.02),
        "gmlp_w_s": nrm(ks[6], (DEPTH, A_GROUPS, CHUNK, CHUNK), CHUNK ** -0.5),
        "gmlp_b_s": 1.0 + nrm(ks[7], (DEPTH, A_GROUPS, CHUNK), 0.02),
        "na_q_g": 1.0 + nrm(ks[8], (DEPTH, B_HEAD_DIM), 0.02),
        "na_k_g": 1.0 + nrm(ks[9], (DEPTH, B_HEAD_DIM), 0.02),
        "na_rpb": nrm(ks[10], (DEPTH, B_HEADS, 2 * WIN_ROWS_MAX - 1, 2 * WIN_COLS - 1), 0.02),
        "w_o_a": nrm(ks[11], (DEPTH, A_WIDTH, D_MODEL), A_WIDTH ** -0.5),
        "w_o_b": nrm(ks[12], (DEPTH, B_WIDTH, D_MODEL), B_WIDTH ** -0.5),
        "w_out": nrm(ks[13], (DEPTH, D_MODEL, D_MODEL), D_MODEL ** -0.5),
        "norm2_g": 1.0 + nrm(ks[14], (DEPTH, D_MODEL), 0.02),
        "w_ff1": nrm(ks[15], (DEPTH, D_MODEL, D_FF), D_MODEL ** -0.5),
        "w_ff2": nrm(ks[16], (DEPTH, D_FF, D_MODEL), D_FF ** -0.5),
    }


def reference(x, norm1_g, w_in, b_gate, gmlp_ln_g, gmlp_ln_b, gmlp_w_s, gmlp_b_s,
              na_q_g, na_k_g, na_rpb, w_o_a, w_o_b, w_out, norm2_g, w_ff1, w_ff2):
    bsz, s, _ = x.shape
    for l in range(DEPTH):
        h = rms_norm(x, norm1_g[l])
        proj = h @ w_in[l]
        z_a = proj[..., :2 * A_WIDTH]
        qkv = proj[..., 2 * A_WIDTH:2 * A_WIDTH + 3 * B_WIDTH]
        gate_pre = proj[..., 2 * A_WIDTH + 3 * B_WIDTH:]
        z_a = jax.nn.gelu(z_a, approximate=False)
        y_a = gmlp_spatial_gating(z_a[..., :A_WIDTH], z_a[..., A_WIDTH:],
                                  gmlp_ln_g[l], gmlp_ln_b[l], gmlp_w_s[l], gmlp_b_s[l])
        q = qkv[..., :B_WIDTH].reshape(bsz, s, B_HEADS, B_HEAD_DIM)
        k = qkv[..., B_WIDTH:2 * B_WIDTH].reshape(bsz, s, B_HEADS, B_HEAD_DIM)
        v = qkv[..., 2 * B_WIDTH:].reshape(bsz, s, B_HEADS, B_HEAD_DIM)
        y_b = neighbourhood_attention_2d(q, k, v, na_q_g[l], na_k_g[l], na_rpb[l])
        gates = jax.nn.sigmoid((gate_pre + b_gate[l]).astype(jnp.float32)).astype(x.dtype)
        merged = gates[..., :D_MODEL] * (y_a @ w_o_a[l]) + gates[..., D_MODEL:] * (y_b @ w_o_b[l])
        x = x + merged @ w_out[l]
        h2 = rms_norm(x, norm2_g[l])
        x = x + jnp.square(jax.nn.relu(h2 @ w_ff1[l])) @ w_ff2[l]
    return x
```

```python
import numpy as np
import concourse.bass as bass
import concourse.mybir as mybir
from concourse.bass_utils import run_bass_kernel_spmd

F32 = mybir.dt.float32
BF16 = mybir.dt.bfloat16
AF = mybir.ActivationFunctionType
ALU = mybir.AluOpType

D = 1024
NTOK = 2048
NT = 16
NSLOT = 20
RMS_EPS = 1e-6
LN_EPS = 1e-5
NEG = -30000.0
DBG_MP = 4
NPAT = 29

C_U, C_V, C_Q, C_K, C_VA, C_G0, C_G1 = 0, 512, 1024, 1536, 2048, 2560, 3584


def _compact(ops):
    out, last = [], {}
    for p in ops:
        if p.is_dma:
            out.append(p)
        elif p.eng not in last or last[p.eng].idx < p.idx:
            last[p.eng] = p
    return out + list(last.values())


class Buf:
    registry = {"sb": [], "ps": []}

    def __init__(self, name, space, start, size, parent=None):
        self.name, self.space, self.start, self.size = name, space, start, size
        self.last_w = None
        self.readers = []
        self.inherit = list(parent.inherit) if parent is not None else []
        self.dead = False
        for o in Buf.registry[space]:
            if o.start < start + size and start < o.start + o.size:
                if o.last_w is not None:
                    self.inherit.append(o.last_w)
                self.inherit.extend(o.readers)
                self.inherit.extend(o.inherit)
                o.dead = True
        self.inherit = _compact(self.inherit)
        Buf.registry[space].append(self)


def join_region(space, start, size):
    j = Buf("join", space, start, size)
    j.dead = True
    Buf.registry[space] = [o for o in Buf.registry[space] if o is not j]
    return j


class SemGroup:
    all_groups = []

    def __init__(self, name, kind):
        self.name, self.kind = name, kind
        self.n = 0
        self.sem = None
        SemGroup.all_groups.append(self)


class Op:
    __slots__ = ("eng", "fn", "is_dma", "group", "gidx", "waits", "need_inc", "count", "idx")


class Sched:
    ENGS = ("pe", "act", "dve", "pool", "sp")

    def __init__(self, nc):
        self.nc = nc
        self.ops = []
        self.eng_obj = {"pe": nc.tensor, "act": nc.scalar, "dve": nc.vector, "pool": nc.gpsimd, "sp": nc.sync}

    def add(self, eng, fn, reads=(), writes=(), dma=None):
        if getattr(self, "frozen", False):
            return None
        op = Op()
        op.eng, op.fn, op.is_dma, op.group = eng, fn, dma is not None, dma
        op.idx = len(self.ops)
        op.need_inc = False
        op.count = None
        if dma is not None:
            assert getattr(dma, "eng", eng) == eng, "a DMA semaphore group must be fed by a single queue"
            dma.eng = eng
            op.gidx = dma.n
            dma.n += 1
        deps = []
        for b in reads:
            assert not b.dead, f"read of dead buf {b.name}"
            if b.last_w is not None:
                deps.append(b.last_w)
            deps.extend(b.inherit) if b.last_w is None else None
        for b in writes:
            assert not b.dead, f"write of dead buf {b.name}"
            if b.last_w is not None:
                deps.append(b.last_w)
            deps.extend(b.readers)
            deps.extend(b.inherit)
        w = []
        seen = set()
        if dma is not None and dma.kind == "all":
            assert all(not (p.is_dma and p.group is dma) for p in deps), "intra-'all'-group dependency"
        for p in deps:
            if p.idx in seen or p is op:
                continue
            seen.add(p.idx)
            if (not p.is_dma) and (not op.is_dma) and p.eng == "pe" and op.eng == "pe":
                continue
            w.append(p)
        op.waits = w
        for b in reads:
            b.readers.append(op)
            if len(b.readers) > 1:
                b.readers = _compact(b.readers)
        for b in writes:
            b.last_w = op
            b.readers = []
            b.inherit = []
        self.ops.append(op)
        return op

    def emit(self, final_groups):
        nc = self.nc
        for op in self.ops:
            for p in op.waits:
                p.need_inc = True
        sems = {e: nc.alloc_semaphore("s_" + e) for e in ("pe", "act", "dve", "pool")}
        cnt = {e: 0 for e in sems}
        waited = {e: {} for e in self.ENGS}
        nwait = 0
        for op in self.ops:
            eo = self.eng_obj[op.eng]
            need = {}
            for p in op.waits:
                if p.is_dma:
                    g = p.group
                    sem = g.sem
                    val = 16 * (p.gidx + 1) if g.kind == "slot" else 16 * g.n
                else:
                    sem = sems[p.eng]
                    val = p.count
                key = id(sem)
                if key not in need or need[key][1] < val:
                    need[key] = (sem, val)
            for key, (sem, val) in need.items():
                if waited[op.eng].get(key, 0) >= val:
                    continue
                waited[op.eng][key] = val
                eo.wait_ge(sem, val)
                nwait += 1
            inst = op.fn()
            if op.is_dma:
                g = op.group
                if g.sem is None:
                    g.sem = nc.alloc_semaphore("d_" + g.name)
                inst.then_inc(g.sem, 16)
            elif op.need_inc:
                cnt[op.eng] += 1
                op.count = cnt[op.eng]
                inst.then_inc(sems[op.eng], 1)
        for g in final_groups:
            if g.sem is not None:
                nc.sync.wait_ge(g.sem, 16 * g.n)
        self.stats = dict(n_ops=len(self.ops), n_waits=nwait, counts=dict(cnt))


def build_program(stop=None):
    Buf.registry = {"sb": [], "ps": []}
    SemGroup.all_groups = []
    nc = bass.Bass("TRN2", target_bir_lowering=False)
    S = Sched(nc)
    g_dbg = SemGroup("dbg", "all")

    def ck(k, items):
        if stop != k or getattr(S, "frozen", False):
            return
        for name, ap, buf in items:
            dt_ = nc.dram_tensor("dbg_" + name, list(ap.shape), F32, kind="ExternalOutput").ap()
            S.add("pool", lambda dt_=dt_, ap=ap: nc.gpsimd.dma_start(out=dt_, in_=ap, max_dma_last_dim=2048), reads=[buf], dma=g_dbg)
        S.frozen = True

    def din(name, shape):
        return nc.dram_tensor(name, list(shape), F32, kind="ExternalInput").ap()

    x_ext = din("x_ext", [NSLOT * 128, D])
    w_in = din("w_in", [D, 4608])
    w_o_a = din("w_o_a", [512, D])
    w_o_b = din("w_o_b", [512, D])
    w_out = din("w_out", [D, D])
    w_ff1 = din("w_ff1", [D, 4096])
    w_ff2 = din("w_ff2", [4096, D])
    g1_d = din("norm1_g", [1, D])
    g2_d = din("norm2_g", [1, D])
    lng_d = din("ln_g", [1, 512])
    lnb_d = din("ln_b", [1, 512])
    bs_d = din("b_s", [1, 512])
    wst_d = din("w_sT", [128, 512])
    smalls_d = din("smalls", [128, 20])
    ident_d = din("ident", [128, 128])
    bones_d = din("bones", [128, 128])
    btab_d = din("btab", [128, NPAT * 8 * 128])
    out_d = nc.dram_tensor("out", [NTOK, D], F32, kind="ExternalOutput").ap()

    ARENA_ELEMS = 105472
    arena = nc.alloc_sbuf_tensor("arena", [128, ARENA_ELEMS], BF16)
    psum = nc.alloc_psum_tensor("psum", [128, 4096], F32)

    def sb(name, off, nbytes, dtype=BF16, parent=None):
        assert off % 4 == 0 and off + nbytes <= ARENA_ELEMS * 2, (name, off, nbytes)
        ap = arena[:, off // 2:(off + nbytes) // 2]
        if dtype == F32:
            ap = ap.bitcast(F32)
        return ap, Buf(name, "sb", off, nbytes, parent=parent)

    def pbank(name, bank, nbanks=1, dtype=F32, boff=0, nbytes=None):
        start = bank * 2048 + boff
        nb = nbanks * 2048 - boff if nbytes is None else nbytes
        ap = psum[:, start // 4:(start + nb) // 4]
        if dtype == BF16:
            ap = ap.bitcast(BF16)
        return ap, Buf(name, "ps", start, nb)

    R_A = 0
    R_B = 65536
    R_C = 98304
    R_D = 163840
    R_E = 208896
    END = ARENA_ELEMS * 2

    g_const = SemGroup("const_sw", "all")
    g_const_h = SemGroup("const_hw", "all")
    ident, b_ident = sb("ident", R_E, 256)
    bones, b_bones = sb("bones", R_E + 256, 256)
    smalls, b_smalls = sb("smalls", R_E + 1296, 80, F32)
    hbg, b_hbg = sb("hbg", R_E + 584, 64, F32)
    ss1, b_ss1 = sb("ss1", R_E + 648, 80, F32)
    std1, b_std1 = sb("std1", R_E + 728, 80, F32)
    rstd1, b_rstd1 = sb("rstd1", R_E + 808, 80, F32)
    ss2, b_ss2 = sb("ss2", R_E + 888, 64, F32)
    rstd2, b_rstd2 = sb("rstd2", R_E + 952, 64, F32)
    mhalf, b_mhalf = sb("mhalf", R_E + 1016, 64, F32)
    lnst, b_lnst = sb("lnst", R_E + 1080, 24 * 2, F32)
    lnmv, b_lnmv = sb("lnmv", R_E + 1128, 8 * 2, F32)
    lnr, b_lnr = sb("lnr", R_E + 1144, 4 * 2, F32)
    rden, b_rden_ = sb("rden", R_E + 1152, 32 * 2, F32)
    v2t, b_v2t = sb("v2t", R_E + 1216, 64, F32)
    epsc, b_epsc = sb("epsc", R_E + 1280, 16, F32)
    assert R_E + 1376 <= END
    S.add("pool", lambda: nc.gpsimd.memset(epsc[:, 0:1], RMS_EPS), writes=[b_epsc])
    S.add("pool", lambda: nc.gpsimd.memset(epsc[:, 1:2], 64 * RMS_EPS), writes=[b_epsc])

    S.add("pool", lambda: nc.gpsimd.dma_start(out=ident, in_=ident_d), writes=[b_ident], dma=g_const)
    S.add("pool", lambda: nc.gpsimd.dma_start(out=bones, in_=bones_d), writes=[b_bones], dma=g_const)
    S.add("sp", lambda: nc.sync.dma_start(out=smalls, in_=smalls_d), writes=[b_smalls], dma=g_const_h)
    S.add("pool", lambda: nc.gpsimd.memset(mhalf, -0.5), writes=[b_mhalf])
    S.add("dve", lambda: nc.vector.tensor_scalar(out=hbg, in0=smalls[:, 2:18], scalar1=0.5, scalar2=None,
                                                  op0=ALU.mult), reads=[b_smalls], writes=[b_hbg])

    hT = arena[:, R_A // 2:(R_A + 32768) // 2]
    b_hTg = [Buf(f"hTg{g}", "sb", R_A + g * 8192, 8192) for g in range(4)]
    hT3 = hT.rearrange("p (k t) -> p k t", k=8)
    hTh, b_hTh = sb("hTh", R_D, 8192)
    hTh3 = hTh.rearrange("p (k t) -> p k t", k=8)
    b_hT_g = [Buf(f"hT_g{g}", "sb", R_A + 0, 0) for g in range(0)]

    o = R_D + 8192
    NXR = 4
    xr = []
    for i in range(NXR):
        xr.append(sb(f"xr{i}", o, 4096, F32)); o += 4096
    hb = []
    for i in range(2):
        hb.append(sb(f"hb{i}", o, 2048)); o += 2048
    g1t, b_g1t = sb("g1t", o, 4096, F32); o += 4096
    junks = []
    for i in range(2):
        junks.append(sb(f"junk{i}", o, 2048)); o += 2048
    wrKa, b_wrK = sb("wrK", o, 8192); o += 8192
    wrK3 = wrKa.rearrange("p (k f) -> p k f", k=8)
    assert o <= R_E
    g_xr = [SemGroup(f"xr{i}", "slot") for i in range(NXR)]
    S.add("sp", lambda: nc.sync.dma_start(out=g1t, in_=g1_d.partition_broadcast(128)), writes=[b_g1t], dma=g_const_h)

    pT = [pbank(f"pT{i}", i, dtype=BF16) for i in range(2)]

    hT_gb = [Buf(f"hTg{g}", "sb", R_A + g * 1024, 1024) for g in range(0)]

    b_ss1c = [Buf(f"ss1_{t}", "sb", R_E + 648 + 4 * t, 4) for t in range(NSLOT)]
    b_std1c = [Buf(f"std1_{t}", "sb", R_E + 728 + 4 * t, 4) for t in range(NSLOT)]
    b_rstd1c = [Buf(f"rstd1_{t}", "sb", R_E + 808 + 4 * t, 4) for t in range(NSLOT)]
    p0_order = list(range(NT, NSLOT)) + list(range(NT))

    def p0_A(ti):
        t = p0_order[ti]
        xa, xb_ = xr[ti % NXR]
        ha, hb_ = hb[ti % 2]
        ja, jb_ = junks[ti % 2]
        S.add("sp", lambda: nc.sync.dma_start(out=xa, in_=x_ext[t * 128:(t + 1) * 128, :]), writes=[xb_], dma=g_xr[ti % NXR])
        S.add("act", lambda: nc.scalar.activation(out=ja, in_=xa, func=AF.Square, accum_out=ss1[:, t:t + 1]),
              reads=[xb_], writes=[jb_, b_ss1c[t]])
        S.add("pool", lambda: nc.gpsimd.tensor_scalar(out=std1[:, t:t + 1], in0=ss1[:, t:t + 1], scalar1=1.0 / D, scalar2=RMS_EPS,
                                                      op0=ALU.mult, op1=ALU.add), reads=[b_ss1c[t]], writes=[b_std1c[t]])
        S.add("pool", lambda: nc.gpsimd.tensor_tensor(out=rstd1[:, t:t + 1], in0=std1[:, t:t + 1], in1=mhalf[:, 0:1], op=ALU.pow),
              reads=[b_std1c[t], b_mhalf], writes=[b_rstd1c[t]])

    def p0_A2(ti):
        t = p0_order[ti]
        xa, xb_ = xr[ti % NXR]
        ha, hb_ = hb[ti % 2]
        S.add("dve", lambda: nc.vector.scalar_tensor_tensor(out=ha, in0=xa, scalar=rstd1[:, t:t + 1], in1=g1t, op0=ALU.mult, op1=ALU.mult),
              reads=[xb_, b_rstd1c[t], b_g1t], writes=[hb_])

    def p0_B(ti):
        t = p0_order[ti]
        ha, hb_ = hb[ti % 2]
        pa, pb_ = pT[ti % 2]
        pa3 = pa.rearrange("p (k t) -> p k t", k=8)
        for kc in range(8):
            S.add("pe", lambda kc=kc: nc.tensor.transpose(pa3[:, kc, :], ha[:, kc * 128:(kc + 1) * 128], ident),
                  reads=[hb_, b_ident], writes=[pb_])
        if t < NT:
            dst, dbuf = hT3[:, :, t * 128:(t + 1) * 128], b_hTg[t // 4]
        else:
            dst, dbuf = hTh3[:, :, (t - NT) * 128:(t - NT + 1) * 128], b_hTh
        S.add("dve", lambda: nc.vector.tensor_copy(out=dst, in_=pa3), reads=[pb_], writes=[dbuf])

    qm, b_qm = sb("qm", R_B, 32768)
    qm4 = qm.rearrange("p (h c t) -> p h c t", h=2, c=4)
    kT, b_kT = sb("kT", R_C, 20480)
    kT3 = kT.rearrange("p (c t) -> p c t", c=4)
    Va, b_Va = sb("Vaug", R_C + 20480, 20800)
    Va4 = Va.rearrange("p (s h d) -> p s h d", s=NSLOT, h=8)
    o = R_A + 32768
    sq = []
    for i in range(2):
        sq.append(sb(f"sq{i}", o, 1024)); o += 1024
    stdb = []
    for i in range(2):
        stdb.append(sb(f"std{i}", o, 2048, F32)); o += 2048
    rsb = []
    for i in range(2):
        rsb.append(sb(f"rs{i}", o, 2048, F32)); o += 2048
    assert o <= R_A + 49152
    wkey = {}
    g_wrK = SemGroup("wrK", "slot")
    g_wv = SemGroup("wv", "slot")
    w_in3 = w_in.rearrange("(k p) f -> p k f", p=128)

    S.add("pool", lambda: nc.gpsimd.memset(Va4[:, :, :, 64:65], 1.0), writes=[b_Va])

    mm_ps = [pbank(f"mm{i}", 2 + i) for i in range(4)]
    ss_ps = [pbank(f"ssp{i}", 6 + i) for i in range(2)]

    def load_w256(key, col0):
        (wa, wb), grp = wkey[key]
        wa3 = wa.rearrange("p (k f) -> p k f", k=8)
        S.add("pool", lambda: nc.gpsimd.dma_start(out=wa3, in_=w_in3[:, :, col0:col0 + 256]),
              writes=[wb], dma=grp)
        return wa3, wb

    def rhs_group(tg):
        if tg < 4:
            return (lambda kc: hT3[:, kc, tg * 512:(tg + 1) * 512]), b_hTg[tg]
        return (lambda kc: hTh3[:, kc, :]), b_hTh

    units_qk = []
    for tg in [4, 0, 1, 2, 3]:
        for cp in range(2):
            for cc in range(2):
                units_qk.append(("k", C_K, cp, cc, tg))
    for cp in range(2):
        for cc in range(2):
            for tg in [0, 1, 2, 3]:
                units_qk.append(("q", C_Q, cp, cc, tg))
    wcache = {}

    def qk_mm(u):
        kind, cbase, cp, cc, tg = units_qk[u]
        key = (kind, cp)
        if key not in wcache:
            wcache[key] = load_w256(key, cbase + cp * 256)
        wa3, wb = wcache[key]
        rf, rbuf = rhs_group(tg)
        pa, pb_ = mm_ps[u % 4]
        qa, qb_ = sq[u % 2]
        for kc in range(8):
            S.add("pe", lambda kc=kc: nc.tensor.matmul(pa, lhsT=wa3[:, kc, cc * 128:(cc + 1) * 128], rhs=rf(kc),
                                                       start=(kc == 0), stop=(kc == 7)), reads=[wb, rbuf], writes=[pb_])
        S.add("act", lambda: nc.scalar.activation(out=qa, in_=pa, func=AF.Square), reads=[pb_], writes=[qb_])

    def qk_fin(u):
        kind, cbase, cp, cc, tg = units_qk[u]
        c = cp * 2 + cc
        pa, pb_ = mm_ps[u % 4]
        sa, sb_ = ss_ps[u % 2]
        qa, qb_ = sq[u % 2]
        sta, stb_ = stdb[u % 2]
        ra, rb_ = rsb[u % 2]
        S.add("pe", lambda: nc.tensor.matmul(sa, lhsT=bones, rhs=qa, start=True, stop=True), reads=[qb_, b_bones], writes=[sb_])
        if kind == "k":
            S.add("act", lambda: nc.scalar.activation(out=sta, in_=sa, func=AF.Ln, scale=1.0 / 64, bias=RMS_EPS), reads=[sb_], writes=[stb_])
        else:
            S.add("act", lambda: nc.scalar.activation(out=sta, in_=sa, func=AF.Ln, scale=1.0, bias=64 * RMS_EPS), reads=[sb_], writes=[stb_])
        S.add("act", lambda: nc.scalar.activation(out=ra, in_=sta, func=AF.Exp, scale=-0.5), reads=[stb_], writes=[rb_])
        if kind == "k":
            tok0 = tg * 512
            S.add("dve", lambda: nc.vector.scalar_tensor_tensor(out=kT3[:, c, tok0:tok0 + 512], in0=pa, scalar=smalls[:, 1:2], in1=ra,
                                                                op0=ALU.mult, op1=ALU.mult), reads=[pb_, rb_, b_smalls], writes=[b_kT])
        else:
            for hp in range(2):
                S.add("dve", lambda hp=hp: nc.vector.scalar_tensor_tensor(
                    out=qm4[:, hp, c, tg * 512:(tg + 1) * 512], in0=pa, scalar=smalls[:, 18 + hp:19 + hp],
                    in1=ra, op0=ALU.mult, op1=ALU.mult), reads=[pb_, rb_, b_smalls], writes=[b_qm])

    qk_next = [0]

    def qk_step():
        u = qk_next[0]
        if u < len(units_qk):
            qk_mm(u)
        if u >= 1:
            qk_fin(u - 1)
        qk_next[0] = u + 1

    S.add("pool", lambda: nc.gpsimd.dma_start(out=wrK3, in_=w_in3[:, :, C_K:C_K + 512]), writes=[b_wrK], dma=g_wrK)
    wcache[("k", 0)] = (wrK3[:, :, 0:256], b_wrK)
    wcache[("k", 1)] = (wrK3[:, :, 256:512], b_wrK)
    p0_A(0)
    p0_A(1)
    p0_A2(0)
    for ti in range(NSLOT):
        if ti + 2 < NSLOT:
            p0_A(ti + 2)
        if ti + 1 < NSLOT:
            p0_A2(ti + 1)
        p0_B(ti)
        if ti >= 4:
            qk_step()
    ck(0, [("hT", hT, b_hTg[3]), ("hTh", hTh, b_hTh)])
    wkey[("q", 0)] = (sb("wrQ0", R_D + 8192, 4096), SemGroup("wrQ0", "slot"))
    wkey[("q", 1)] = (sb("wrQ1", R_D + 12288, 4096), SemGroup("wrQ1", "slot"))
    wv, b_wv = sb("wv", R_D + 16384, 8192)
    while qk_next[0] <= len(units_qk):
        qk_step()
    unit = len(units_qk)

    wv3 = wv.rearrange("p (k f) -> p k f", k=8)
    S.add("pool", lambda: nc.gpsimd.dma_start(out=wv3, in_=w_in3[:, :, C_VA:C_VA + 512]), writes=[b_wv], dma=g_wv)
    for j in range(NSLOT):
        pa, pb_ = mm_ps[unit % 4]
        unit += 1
        if j < NT:
            lf, lbuf = (lambda kc, j=j: hT3[:, kc, j * 128:(j + 1) * 128]), b_hTg[j // 4]
        else:
            lf, lbuf = (lambda kc, j=j: hTh3[:, kc, (j - NT) * 128:(j - NT + 1) * 128]), b_hTh
        for kc in range(8):
            S.add("pe", lambda pa=pa, lf=lf, kc=kc: nc.tensor.matmul(pa, lhsT=lf(kc), rhs=wv3[:, kc, :],
                                                                     start=(kc == 0), stop=(kc == 7)),
                  reads=[b_wv, lbuf], writes=[pb_])
        src = pa.rearrange("p (h d) -> p h d", h=8)
        dst = Va4[:, j, :, 0:64]
        if j % 2 == 0:
            S.add("act", lambda dst=dst, src=src: nc.scalar.copy(out=dst, in_=src), reads=[pb_], writes=[b_Va])
        else:
            S.add("dve", lambda dst=dst, src=src: nc.vector.tensor_copy(out=dst, in_=src), reads=[pb_], writes=[b_Va])

    ck(1, [("kT", kT, b_kT), ("qm", qm, b_qm), ("Va", Va, b_Va)])
    tin, b_tin = sb("tab_int", R_C + 41472, 10240)
    tin4 = tin.rearrange("p (a h q) -> p a h q", a=5, h=8)
    tsa, b_tsa = sb("tab_sa", R_C + 51712, 12288)
    tsa4 = tsa.rearrange("p (a h q) -> p a h q", a=6, h=8)
    assert R_C + 51712 + 12288 <= R_D
    o = R_D
    tsb, b_tsb = sb("tab_sb", R_A + 49152, 12288)
    tsb4 = tsb.rearrange("p (a h q) -> p a h q", a=6, h=8)
    PTb = []
    for i in range(3):
        PTb.append(sb(f"PT{i}", o, 3072)); o += 3072
    Sbb = []
    for i in range(2):
        Sbb.append(sb(f"Sb{i}", o, 6144, F32)); o += 6144
    ybt = []
    for i in range(2):
        ybt.append(sb(f"ybt{i}", o, 1024)); o += 1024
    assert o <= R_E
    y_bT, b_ybT = sb("y_bT", R_A + 32768, 16384)
    y_bT3 = y_bT.rearrange("p (c t) -> p c t", c=4)
    g_tin = SemGroup("tin", "slot")
    g_tsa = SemGroup("tsa", "slot")
    g_tsb = SemGroup("tsb", "slot")
    PW = 8 * 128

    def load_tab(dst4, dbuf, grp, p0, npat, eng="pool"):
        S.add("pool", lambda: nc.gpsimd.dma_start(
            out=dst4[:, 0:npat, :, :].rearrange("p a h q -> p a (h q)"),
            in_=btab_d[:, p0 * PW:(p0 + npat) * PW].rearrange("p (a x) -> p a x", a=npat)),
            writes=[dbuf], dma=grp)

    load_tab(tsa4, b_tsa, g_tsa, 0, 6)
    load_tab(tin4, b_tin, g_tin, 12, 5)
    load_tab(tsb4, b_tsb, g_tsb, 6, 6)

    S_ps = [pbank(f"S{i}", 3 * i, nbytes=6144) for i in range(2)]
    _pv = pbank("PV", 6, nbytes=4 * 65 * 4)
    PV_ps = [_pv, _pv]
    yTp_a, yTp_b = pbank("yTp", 7, dtype=BF16, nbytes=1024)
    yTp3 = yTp_a.rearrange("p (c t) -> p c t", c=4)

    def kst(s):
        if 2 <= s < 18:
            return s - 2
        return 16 + s if s < 2 else s

    def na_unit_info(il):
        if il == 0:
            offs = list(range(-2, 4)); tab = (tsa4, b_tsa)
        elif il == 1:
            offs = list(range(-2, 3)); tab = (tsb4, b_tsb)
        elif il == 14:
            offs = list(range(-2, 3)); tab = (tsa4, b_tsa)
        elif il == 15:
            offs = list(range(-3, 3)); tab = (tsb4, b_tsb)
        else:
            offs = list(range(-2, 3)); tab = (tin4, b_tin)
        slots = [il + 2 + o_ for o_ in offs]
        return slots, tab

    def na_S(il, c, u):
        slots, (t4, tbuf) = na_unit_info(il)
        ns = len(slots)
        sa, sb_ = S_ps[u % 2]
        pa, pb_ = PTb[u % 3]
        for n, s in enumerate(slots):
            j = kst(s)
            S.add("pe", lambda n=n, j=j: nc.tensor.matmul(
                sa[:, n * 256:(n + 1) * 256], lhsT=kT3[:, c, j * 128:(j + 1) * 128],
                rhs=qm4[:, :, c, il * 128:(il + 1) * 128], start=True, stop=False),
                reads=[b_kT, b_qm], writes=[sb_])
            S.add("pe", lambda n=n: nc.tensor.matmul(
                sa[:, n * 256:(n + 1) * 256], lhsT=ident,
                rhs=t4[:, n, 2 * c:2 * c + 2, :], start=False, stop=True),
                reads=[tbuf, b_ident], writes=[sb_])
        S.add("act", lambda: nc.scalar.activation(out=pa[:, 0:ns * 256], in_=sa[:, 0:ns * 256], func=AF.Exp),
              reads=[sb_], writes=[pb_])

    def na_PV(il, c, u):
        slots, _ = na_unit_info(il)
        pa, pb_ = PTb[u % 3]
        for hp in range(2):
            h = 2 * c + hp
            va, vb_ = PV_ps[h // 4]
            va3 = va.rearrange("p (h d) -> p h d", h=4)
            for n, s in enumerate(slots):
                j = kst(s)
                S.add("pe", lambda va3=va3, n=n, j=j, h=h, hp=hp, last=(n == len(slots) - 1): nc.tensor.matmul(
                    va3[:, h % 4, :], lhsT=pa[:, n * 256 + hp * 128:n * 256 + hp * 128 + 128], rhs=Va4[:, j, h, :],
                    start=(n == 0), stop=last), reads=[pb_, b_Va], writes=[vb_])
            if h % 4 == 3:
                hb4 = h // 4
                ya, yb_ = ybt[il % 2]
                ya3 = ya.rearrange("p (h d) -> p h d", h=8)
                rd = rden[:, (il % 2) * 8 + hb4 * 4:(il % 2) * 8 + hb4 * 4 + 4]
                S.add("dve", lambda rd=rd, va3=va3: nc.vector.reciprocal(out=rd, in_=va3[:, :, 64]),
                      reads=[vb_], writes=[b_rden_])
                S.add("dve", lambda ya3=ya3, va3=va3, rd=rd, hb4=hb4: nc.vector.tensor_tensor(
                    out=ya3[:, hb4 * 4:(hb4 + 1) * 4, :], in0=va3[:, :, 0:64],
                    in1=rd.unsqueeze(2).to_broadcast([128, 4, 64]), op=ALU.mult),
                    reads=[vb_, b_rden_], writes=[yb_])

    def na_T(il):
        ya, yb_ = ybt[il % 2]
        for c in range(4):
            S.add("pe", lambda c=c: nc.tensor.transpose(yTp3[:, c, :], ya[:, c * 128:(c + 1) * 128], ident),
                  reads=[yb_, b_ident], writes=[yTp_b])
        dst = y_bT3[:, :, il * 128:(il + 1) * 128]
        S.add("dve", lambda dst=dst: nc.vector.tensor_copy(out=dst, in_=yTp3), reads=[yTp_b], writes=[b_ybT])

    units = [(il, c) for il in range(NT) for c in range(4)]
    LAG = 2

    def na_issue_S(u):
        il2, c2 = units[u]
        if il2 == 3 and c2 == 0:
            load_tab(tsa4, b_tsa, g_tsa, 17, 6)
            load_tab(tsb4, b_tsb, g_tsb, 23, 6)
        na_S(il2, c2, u)

    for u in range(min(LAG, len(units))):
        na_issue_S(u)
    for u, (il, c) in enumerate(units):
        if u + LAG < len(units):
            na_issue_S(u + LAG)
        na_PV(il, c, u)
        if c == 0 and il > 0:
            na_T(il - 1)
    na_T(NT - 1)

    ck(2, [("ybT", y_bT, b_ybT)])
    uT = arena[:, (R_A + 49152) // 2:(R_A + 65536) // 2]
    j_uT = join_region("sb", R_A + 49152, 16384)
    b_uTg = [Buf(f"uTg{g}", "sb", R_A + 49152 + g * 4096, 4096, parent=j_uT) for g in range(4)]
    uT3 = uT.rearrange("p (c t) -> p c t", c=4)
    gtmp = []
    o = R_D
    for i in range(2):
        gtmp.append(sb(f"gtmp{i}", o, 2048, F32)); o += 2048
    o = R_D + 12288
    wst, b_wst = sb("wst", o, 1024); o += 1024
    bst, b_bst = sb("bst", o, 2048, F32); o += 2048
    lngt, b_lngt = sb("lngt", o, 2048, F32); o += 2048
    lnbt, b_lnbt = sb("lnbt", o, 2048, F32); o += 2048
    vtmp = []
    for i in range(3):
        vtmp.append(sb(f"vtmp{i}", o, 2048, F32)); o += 2048
    lnsm = []
    for i in range(3):
        lnsm.append(sb(f"lnsm{i}", o, 48, F32)); o += 48
    assert o <= R_D + 28672, o
    o = R_D + 28672
    wr2 = []
    for i in range(2):
        wr2.append(sb(f"wrb{i}", o, 4096)); o += 4096
    wvv, b_wvv = sb("wvv", o, 8192); o += 8192
    assert o <= R_E, o
    vn = arena[:, R_B // 2:(R_B + 16384) // 2]
    j_vn = join_region("sb", R_B, 16384)
    b_vnj = [Buf(f"vn{j}", "sb", R_B + j * 1024, 1024, parent=j_vn) for j in range(NT)]
    vn3 = vn.rearrange("p (j f) -> p j f", j=NT)
    g_c2 = SemGroup("const2", "all")
    S.add("sp", lambda: nc.sync.dma_start(out=lngt, in_=lng_d.partition_broadcast(128)), writes=[b_lngt], dma=g_c2)
    S.add("sp", lambda: nc.sync.dma_start(out=lnbt, in_=lnb_d.partition_broadcast(128)), writes=[b_lnbt], dma=g_c2)
    S.add("sp", lambda: nc.sync.dma_start(out=bst, in_=bs_d.partition_broadcast(128)), writes=[b_bst], dma=g_c2)
    g_c2p = SemGroup("const2_sw", "all")
    S.add("pool", lambda: nc.gpsimd.dma_start(out=wst, in_=wst_d), writes=[b_wst], dma=g_c2p)
    g_wr2 = [SemGroup(f"wrb{i}", "slot") for i in range(2)]
    g_wvv = SemGroup("wvv", "slot")

    mm2 = [pbank(f"mmb{i}", i) for i in range(4)]
    unit = 0
    u_w = []
    for cp in range(2):
        wa, wb = wr2[cp % 2]
        wa3 = wa.rearrange("p (k f) -> p k f", k=8)
        S.add("pool", lambda wa3=wa3, cp=cp: nc.gpsimd.dma_start(out=wa3, in_=w_in3[:, :, C_U + cp * 256:C_U + (cp + 1) * 256]),
              writes=[wb], dma=g_wr2[cp % 2])
        u_w.append((wa3, wb))

    def u_unit(i):
        tg, c = i // 4, i % 4
        cp, cc = c // 2, c % 2
        wa3, wb = u_w[cp]
        pa, pb_ = mm2[i % 2]
        for kc in range(8):
            S.add("pe", lambda kc=kc: nc.tensor.matmul(pa, lhsT=wa3[:, kc, cc * 128:(cc + 1) * 128], rhs=hT3[:, kc, tg * 512:(tg + 1) * 512],
                                                       start=(kc == 0), stop=(kc == 7)), reads=[wb, b_hTg[tg]], writes=[pb_])
        S.add("act", lambda: nc.scalar.activation(out=uT3[:, c, tg * 512:(tg + 1) * 512], in_=pa, func=AF.Gelu), reads=[pb_], writes=[b_uTg[tg]])

    wvv3 = wvv.rearrange("p (k f) -> p k f", k=8)
    S.add("pool", lambda: nc.gpsimd.dma_start(out=wvv3, in_=w_in3[:, :, C_V:C_V + 512]), writes=[b_wvv], dma=g_wvv)
    def v_A(j):
        pa, pb_ = mm2[2 + j % 2]
        va_, vb_ = vtmp[j % 3]
        sm, smb = lnsm[j % 3]
        st_, mv_, r_ = sm[:, 0:6], sm[:, 6:8], sm[:, 8:9]
        for kc in range(8):
            S.add("pe", lambda kc=kc: nc.tensor.matmul(pa, lhsT=hT3[:, kc, j * 128:(j + 1) * 128], rhs=wvv3[:, kc, :],
                                                       start=(kc == 0), stop=(kc == 7)),
                  reads=[b_wvv, b_hTg[j // 4]], writes=[pb_])
        S.add("act", lambda: nc.scalar.activation(out=va_, in_=pa, func=AF.Gelu), reads=[pb_], writes=[vb_])
        S.add("dve", lambda: nc.vector.bn_stats(out=st_, in_=va_), reads=[vb_], writes=[smb])
        S.add("dve", lambda: nc.vector.bn_aggr(out=mv_, in_=st_), reads=[smb], writes=[smb])
        S.add("pool", lambda: nc.gpsimd.tensor_scalar(out=r_, in0=mv_[:, 1:2], scalar1=LN_EPS, scalar2=None, op0=ALU.add),
              reads=[smb], writes=[smb])
        S.add("pool", lambda: nc.gpsimd.tensor_tensor(out=r_, in0=r_, in1=mhalf[:, 0:1], op=ALU.pow),
              reads=[smb, b_mhalf], writes=[smb])

    def v_B(j):
        va_, vb_ = vtmp[j % 3]
        sm, smb = lnsm[j % 3]
        mv_, r_ = sm[:, 6:8], sm[:, 8:9]
        S.add("dve", lambda: nc.vector.scalar_tensor_tensor(out=va_, in0=va_, scalar=mv_[:, 0:1], in1=lngt, op0=ALU.subtract, op1=ALU.mult),
              reads=[vb_, smb, b_lngt], writes=[vb_])
        S.add("dve", lambda: nc.vector.scalar_tensor_tensor(out=vn3[:, j, :], in0=va_, scalar=r_, in1=lnbt, op0=ALU.mult, op1=ALU.add),
              reads=[vb_, smb, b_lnbt], writes=[b_vnj[j]])

    wst3 = wst.rearrange("p (g q) -> p g q", g=4)
    mg = [pbank(f"mg{i}", 4 + i) for i in range(2)]

    def gmlp(j):
        pa, pb_ = mg[j % 2]
        ga, gb_ = gtmp[j % 2]
        for g in range(4):
            S.add("pe", lambda g=g: nc.tensor.matmul(
                pa[:, g * 128:(g + 1) * 128], lhsT=vn3[:, j, g * 128:(g + 1) * 128], rhs=wst3[:, g, :], start=True, stop=True),
                reads=[b_vnj[j], b_wst], writes=[pb_])
        S.add("dve", lambda: nc.vector.tensor_tensor(out=ga, in0=pa, in1=bst, op=ALU.add), reads=[pb_, b_bst], writes=[gb_])
        uv = uT3[:, :, j * 128:(j + 1) * 128]
        S.add("dve", lambda: nc.vector.tensor_tensor(out=uv, in0=ga.rearrange("p (g t) -> p g t", g=4), in1=uv, op=ALU.mult),
              reads=[gb_, b_uTg[j // 4]], writes=[b_uTg[j // 4]])

    woa, b_woa = sb("woa", R_C + 32768, 8192)
    wob, b_wob = sb("wob", R_C + 32768 + 8192, 8192)
    woa3 = woa.rearrange("p (k f) -> p k f", k=4)
    wob3 = wob.rearrange("p (k f) -> p k f", k=4)
    g_wo = SemGroup("wo", "all")
    wgA = [sb("wgA0", R_C + 0, 4096), sb("wgA1", R_C + 4096, 4096)]
    g_wgA = [SemGroup(f"wgA{i}", "slot") for i in range(2)]
    wgA_loaded = []
    for gi, cb in enumerate((C_G0, C_G1)):
        wa, wb = wgA[gi]
        wa3 = wa.rearrange("p (k f) -> p k f", k=8)
        S.add("pool", lambda wa3=wa3, cb=cb: nc.gpsimd.dma_start(out=wa3, in_=w_in3[:, :, cb:cb + 256]), writes=[wb], dma=g_wgA[gi])
        wgA_loaded.append((wa3, wb))
    S.add("pool", lambda: nc.gpsimd.dma_start(out=woa3, in_=w_o_a.rearrange("(k p) f -> p k f", p=128)), writes=[b_woa], dma=g_wo)
    S.add("pool", lambda: nc.gpsimd.dma_start(out=wob3, in_=w_o_b.rearrange("(k p) f -> p k f", p=128)), writes=[b_wob], dma=g_wo)
    wout_a, b_wout0 = sb("wout0", R_C + 49152, 8192)
    wout_b, b_wout1 = sb("wout1", R_C + 49152 + 8192, 8192)
    wout3 = arena[:, (R_C + 49152) // 2:(R_C + 49152 + 16384) // 2].rearrange("p (k f) -> p k f", k=8)
    g_wout = SemGroup("wout", "all")
    NXS = 4
    xs = [sb(f"xs{t}", R_C + 16384 + t * 4096, 4096, F32) for t in range(NXS)]
    g_xs = [SemGroup(f"xs{t}", "slot") for t in range(NXS)]
    for t in range(NXS):
        S.add("sp", lambda t=t: nc.sync.dma_start(out=xs[t][0], in_=x_ext[t * 128:(t + 1) * 128, :]), writes=[xs[t][1]], dma=g_xs[t])

    v_A(0)
    for j in range(NT):
        u_unit(j)
        if j + 1 < NT:
            v_A(j + 1)
        v_B(j)
        if j >= 4:
            gmlp(j - 4)
    for j in range(NT - 4, NT):
        gmlp(j)

    ck(3, [("uT", uT, b_uTg[3]), ("vn", vn, b_vnj[15])])
    y_aT3 = uT3

    ck(4, [("yaT", uT, b_uTg[3])])
    mT, b_mT = sb("mergedT", R_B, 32768)
    mT3 = mT.rearrange("p (k t) -> p k t", k=8)
    S.add("pool", lambda: nc.gpsimd.dma_start(out=wout3[:, 0:4, :], in_=w_out.rearrange("(k p) f -> p k f", p=128)[:, 0:4, :]),
          writes=[b_wout0], dma=g_wout)
    S.add("pool", lambda: nc.gpsimd.dma_start(out=wout3[:, 4:8, :], in_=w_out.rearrange("(k p) f -> p k f", p=128)[:, 4:8, :]),
          writes=[b_wout1], dma=g_wout)
    wgB = [sb("wgB0", R_D + 0, 4096), sb("wgB1", R_D + 4096, 4096)]
    wgC = [sb("wgC0", R_D + 36864, 4096), sb("wgC1", R_D + 40960, 4096)]
    g_wgB = [SemGroup(f"wgB{i}", "slot") for i in range(2)]
    g_wgC = [SemGroup(f"wgC{i}", "slot") for i in range(2)]
    tt = []
    o = R_D + 8192
    for i in range(4):
        tt.append(sb(f"tt{i}", o, 2048, F32)); o += 2048
    ffw = []
    g_ff = [(SemGroup(f"ff1_{i}", "slot"), SemGroup(f"ff2_{i}", "slot")) for i in range(3)]
    w_ff1_3 = w_ff1.rearrange("(k p) f -> p k f", p=128)
    w_ff2_3 = w_ff2.rearrange("(k p) f -> p k f", p=128)

    def load_ff(part):
        (w1a, w1b), (w2a, w2b) = ffw[part % 3]
        g1_, g2_ = g_ff[part % 3]
        S.add("pool", lambda: nc.gpsimd.dma_start(out=w1a.rearrange("p (k f) -> p k f", k=8),
                                                  in_=w_ff1_3[:, :, part * 512:(part + 1) * 512]), writes=[w1b], dma=g1_)
        S.add("pool", lambda: nc.gpsimd.dma_start(out=w2a.rearrange("p (k f) -> p k f", k=4),
                                                  in_=w_ff2_3[:, part * 4:(part + 1) * 4, :]), writes=[w2b], dma=g2_)

    gps = [[pbank(f"gp{i}_{k}", 4 * i + k) for k in range(4)] for i in range(2)]
    unit = 0
    for mp in range(DBG_MP):
        if mp == 0:
            was = wgA_loaded
        else:
            was = []
            bufs, grps = (wgB, g_wgB) if mp % 2 == 1 else (wgC, g_wgC)
            for gi, cb in enumerate((C_G0, C_G1)):
                wa, wb = bufs[gi]
                wa3 = wa.rearrange("p (k f) -> p k f", k=8)
                S.add("pool", lambda wa3=wa3, cb=cb, mp=mp: nc.gpsimd.dma_start(out=wa3, in_=w_in3[:, :, cb + mp * 256:cb + (mp + 1) * 256]),
                      writes=[wb], dma=grps[gi])
                was.append((wa3, wb))
        if mp == 2:
            ffw.append((sb("ff1_0", R_C, 8192), sb("ff2_0", R_C + 8192, 8192)))
            load_ff(0)
        for mm in range(2):
            m = mp * 2 + mm
            for tg in range(4):
                ps4 = gps[unit % 2]
                t0, t1, ta, tb = tt[0], tt[1], tt[2], tt[3]
                unit += 1
                tsl = slice(tg * 512, (tg + 1) * 512)
                for gi in range(2):
                    wa3, wb = was[gi]
                    pa, pb_ = ps4[gi]
                    for kc in range(8):
                        S.add("pe", lambda pa=pa, wa3=wa3, mm=mm, kc=kc, tsl=tsl: nc.tensor.matmul(
                            pa, lhsT=wa3[:, kc, mm * 128:(mm + 1) * 128], rhs=hT3[:, kc, tsl], start=(kc == 0), stop=(kc == 7)),
                            reads=[wb, b_hTg[tg]], writes=[pb_])
                pa, pb_ = ps4[2]
                for kc in range(4):
                    S.add("pe", lambda pa=pa, kc=kc, m=m, tsl=tsl: nc.tensor.matmul(
                        pa, lhsT=woa3[:, kc, m * 128:(m + 1) * 128], rhs=y_aT3[:, kc, tsl], start=(kc == 0), stop=(kc == 3)),
                        reads=[b_woa, b_uTg[tg]], writes=[pb_])
                pa, pb_ = ps4[3]
                for kc in range(4):
                    S.add("pe", lambda pa=pa, kc=kc, m=m, tsl=tsl: nc.tensor.matmul(
                        pa, lhsT=wob3[:, kc, m * 128:(m + 1) * 128], rhs=y_bT3[:, kc, tsl], start=(kc == 0), stop=(kc == 3)),
                        reads=[b_wob, b_ybT], writes=[pb_])
                S.add("act", lambda t0=t0, ps4=ps4, m=m: nc.scalar.activation(out=t0[0], in_=ps4[0][0], func=AF.Tanh, scale=0.5,
                                                                              bias=hbg[:, m:m + 1]),
                      reads=[ps4[0][1], b_hbg], writes=[t0[1]])
                S.add("act", lambda t1=t1, ps4=ps4, m=m: nc.scalar.activation(out=t1[0], in_=ps4[1][0], func=AF.Tanh, scale=0.5,
                                                                              bias=hbg[:, 8 + m:8 + m + 1]),
                      reads=[ps4[1][1], b_hbg], writes=[t1[1]])
                S.add("dve", lambda ta=ta, t0=t0, ps4=ps4: nc.vector.scalar_tensor_tensor(
                    out=ta[0], in0=t0[0], scalar=1.0, in1=ps4[2][0], op0=ALU.add, op1=ALU.mult),
                    reads=[t0[1], ps4[2][1]], writes=[ta[1]])
                S.add("dve", lambda tb=tb, t1=t1, ps4=ps4: nc.vector.scalar_tensor_tensor(
                    out=tb[0], in0=t1[0], scalar=1.0, in1=ps4[3][0], op0=ALU.add, op1=ALU.mult),
                    reads=[t1[1], ps4[3][1]], writes=[tb[1]])
                S.add("dve", lambda ta=ta, tb=tb, m=m, tsl=tsl: nc.vector.tensor_tensor(
                    out=mT3[:, m, tsl], in0=ta[0], in1=tb[0], op=ALU.add),
                    reads=[ta[1], tb[1]], writes=[b_mT])

    ck(5, [("mT", mT, b_mT)])
    j_RA = join_region("sb", R_A, 65536)
    x1 = []
    for t in range(NT):
        x1.append(sb(f"x1_{t}", R_A + t * 4096, 4096, F32, parent=j_RA))
    g_x1 = [SemGroup(f"x1_{t}", "slot") for t in range(NT)]

    def x_reload(t, after=None):
        xa, xb_ = x1[t]
        S.add("sp", lambda: nc.sync.dma_start(out=xa, in_=x_ext[t * 128:(t + 1) * 128, :]),
              reads=[after] if after is not None else [], writes=[xb_], dma=g_x1[t])

    for t in range(NXS, NXS + 3):
        x_reload(t)
    h2T = arena[:, R_D // 2:(R_D + 32768) // 2]
    j_RD = join_region("sb", R_D, 32768)
    b_h2Tg = [Buf(f"h2Tg{g}", "sb", R_D + g * 8192, 8192, parent=j_RD) for g in range(4)]
    h2T3 = h2T.rearrange("p (k t) -> p k t", k=8)
    o = R_D + 32768
    h2b = []
    for i in range(2):
        h2b.append(sb(f"h2b{i}", o, 2048)); o += 2048
    g2t, b_g2t = sb("g2t", o, 4096, F32); o += 4096
    junk2s = []
    for i in range(2):
        junk2s.append(sb(f"junk2_{i}", o, 2048)); o += 2048
    assert o <= R_E, o
    b_ss2c = [Buf(f"ss2_{t}", "sb", R_E + 888 + 4 * t, 4) for t in range(NT)]
    b_v2c = [Buf(f"v2_{t}", "sb", R_E + 1216 + 4 * t, 4) for t in range(NT)]
    b_rstd2c = [Buf(f"rstd2_{t}", "sb", R_E + 952 + 4 * t, 4) for t in range(NT)]
    g_c3 = SemGroup("const3", "all")
    S.add("sp", lambda: nc.sync.dma_start(out=g2t, in_=g2_d.partition_broadcast(128)), writes=[b_g2t], dma=g_c3)

    wo_ps = [pbank(f"wo{i}", i) for i in range(4)]
    pT2 = [pbank(f"pTb{i}", 4 + i, dtype=BF16) for i in range(2)]
    unit = 0

    def n2_A(t):
        xa, xb_ = x1[t]
        ha, hb_ = h2b[t % 2]
        ja, jb_ = junk2s[t % 2]
        S.add("act", lambda: nc.scalar.activation(out=ja, in_=xa, func=AF.Square, accum_out=ss2[:, t:t + 1]),
              reads=[xb_], writes=[jb_, b_ss2c[t]])
        S.add("pool", lambda: nc.gpsimd.tensor_scalar(out=v2t[:, t:t + 1], in0=ss2[:, t:t + 1], scalar1=1.0 / D, scalar2=RMS_EPS,
                                                      op0=ALU.mult, op1=ALU.add), reads=[b_ss2c[t]], writes=[b_v2c[t]])
        S.add("pool", lambda: nc.gpsimd.tensor_tensor(out=rstd2[:, t:t + 1], in0=v2t[:, t:t + 1], in1=mhalf[:, 0:1], op=ALU.pow),
              reads=[b_v2c[t], b_mhalf], writes=[b_rstd2c[t]])
        S.add("dve", lambda: nc.vector.scalar_tensor_tensor(out=ha, in0=xa, scalar=rstd2[:, t:t + 1], in1=g2t, op0=ALU.mult, op1=ALU.mult),
              reads=[xb_, b_rstd2c[t], b_g2t], writes=[hb_])

    def n2_B(t):
        ha, hb_ = h2b[t % 2]
        pa, pb_ = pT2[t % 2]
        pa3 = pa.rearrange("p (k t) -> p k t", k=8)
        for kc in range(8):
            S.add("pe", lambda kc=kc: nc.tensor.transpose(pa3[:, kc, :], ha[:, kc * 128:(kc + 1) * 128], ident),
                  reads=[hb_, b_ident], writes=[pb_])
        dst = h2T3[:, :, t * 128:(t + 1) * 128]
        S.add("act", lambda: nc.scalar.copy(out=dst, in_=pa3), reads=[pb_], writes=[b_h2Tg[t // 4]])

    for t in range(NT):
        xa, xb_ = x1[t]
        for dh in range(2):
            pa, pb_ = wo_ps[unit % 4]
            unit += 1
            for kc in range(8):
                S.add("pe", lambda pa=pa, kc=kc, t=t, dh=dh: nc.tensor.matmul(
                    pa, lhsT=mT3[:, kc, t * 128:(t + 1) * 128], rhs=wout3[:, kc, dh * 512:(dh + 1) * 512],
                    start=(kc == 0), stop=(kc == 7)), reads=[b_mT, b_wout0 if kc < 4 else b_wout1], writes=[pb_])
            if t < NXS:
                src, srcb = xs[t]
            else:
                src, srcb = xa, xb_
            S.add("dve", lambda pa=pa, xa=xa, dh=dh, src=src: nc.vector.scalar_tensor_tensor(
                out=xa[:, dh * 512:(dh + 1) * 512], in0=pa, scalar=0.5, in1=src[:, dh * 512:(dh + 1) * 512],
                op0=ALU.mult, op1=ALU.add), reads=[pb_, srcb], writes=[xb_])
        if t >= NXS and t + 3 < NT:
            x_reload(t + 3, after=xb_)
        if t >= 1:
            n2_A(t - 1)
        if t >= 2:
            n2_B(t - 2)
    def n2_tail():
        n2_A(NT - 1)
        n2_B(NT - 2)
        n2_B(NT - 1)

    ffw.append((sb("ff1_1", R_C + 16384, 8192), sb("ff2_1", R_C + 16384 + 8192, 8192)))
    ffw.append((sb("ff1_2", R_C + 2 * 16384, 8192), sb("ff2_2", R_C + 2 * 16384 + 8192, 8192)))
    load_ff(1)
    load_ff(2)
    ck(6, [("x1_%d" % t, x1[t][0], x1[t][1]) for t in range(NT)] + [("h2T", h2T, b_h2Tg[3])])
    o = R_B
    actT = []
    for i in range(2):
        actT.append(sb(f"actT{i}", o, 4096)); o += 4096
    sqf = []
    for i in range(2):
        sqf.append(sb(f"sqf{i}", o, 2048, F32)); o += 2048
    f1_ps = [pbank(f"f1p{i}", i) for i in range(3)]
    f2_ps = [pbank(f"f2p{i}", b) for i, b in enumerate((3, 6, 7))]
    g_out = SemGroup("out", "all")
    u1 = 0
    u2 = 0

    def ff1(part, tg, ui):
        nonlocal u1
        (w1a, w1b), _ = ffw[part % 3]
        w13 = w1a.rearrange("p (k f) -> p k f", k=8)
        aa, ab_ = actT[ui % 2]
        aa3 = aa.rearrange("p (c t) -> p c t", c=4)
        for fc in range(4):
            pa, pb_ = f1_ps[u1 % 3]
            qa, qb_ = sqf[u1 % 2]
            u1 += 1
            for kc in range(8):
                S.add("pe", lambda pa=pa, w13=w13, fc=fc, kc=kc, tg=tg: nc.tensor.matmul(
                    pa, lhsT=w13[:, kc, fc * 128:(fc + 1) * 128], rhs=h2T3[:, kc, tg * 512:(tg + 1) * 512],
                    start=(kc == 0), stop=(kc == 7)), reads=[w1b, b_h2Tg[tg]], writes=[pb_])
            S.add("act", lambda qa=qa, pa=pa: nc.scalar.activation(out=qa, in_=pa, func=AF.Square), reads=[pb_], writes=[qb_])
            S.add("dve", lambda aa3=aa3, fc=fc, pa=pa, qa=qa: nc.vector.scalar_tensor_tensor(
                out=aa3[:, fc, :], in0=pa, scalar=0.0, in1=qa, op0=ALU.is_gt, op1=ALU.mult),
                reads=[pb_, qb_], writes=[ab_])

    def ff2(part, tg, ui):
        nonlocal u2
        _, (w2a, w2b) = ffw[part % 3]
        w23 = w2a.rearrange("p (k f) -> p k f", k=4)
        aa, ab_ = actT[ui % 2]
        aa3 = aa.rearrange("p (c t) -> p c t", c=4)
        for tt_ in range(4):
            t = tg * 4 + tt_
            xa, xb_ = x1[t]
            for dh in range(2):
                pa, pb_ = f2_ps[u2 % 3]
                u2 += 1
                for fc in range(4):
                    S.add("pe", lambda pa=pa, aa3=aa3, w23=w23, fc=fc, tt_=tt_, dh=dh: nc.tensor.matmul(
                        pa, lhsT=aa3[:, fc, tt_ * 128:(tt_ + 1) * 128], rhs=w23[:, fc, dh * 512:(dh + 1) * 512],
                        start=(fc == 0), stop=(fc == 3)), reads=[ab_, w2b], writes=[pb_])
                S.add("dve", lambda pa=pa, xa=xa, dh=dh: nc.vector.tensor_tensor(
                    out=xa[:, dh * 512:(dh + 1) * 512], in0=pa, in1=xa[:, dh * 512:(dh + 1) * 512], op=ALU.add),
                    reads=[pb_, xb_], writes=[xb_])
            if part == 7:
                S.add("sp", lambda xa=xa, t=t: nc.sync.dma_start(out=out_d[t * 128:(t + 1) * 128, :], in_=xa),
                      reads=[xb_], dma=g_out)

    funits = [(p, tg) for p in range(8) for tg in range(4)]
    ff1(*funits[0], 0)
    for ui, (p, tg) in enumerate(funits):
        if ui + 1 < len(funits):
            ff1(*funits[ui + 1], ui + 1)
        ff2(p, tg, ui)
        if ui == 1:
            n2_tail()
        if tg == 3 and p + 3 < 8:
            load_ff(p + 3)

    S.frozen = False if stop is None else S.frozen
    S.emit([g_out] if stop is None else list(SemGroup.all_groups))
    return nc, S


def _bias_tables(rpb, half):
    def pattern(il, o):
        i = 16 * half + il
        j = i + o
        tab = np.full((8, 128, 128), NEG, dtype=np.float32)
        if j < 0 or j > 31:
            return tab
        a = np.arange(2)[:, None]; kc = np.arange(64)[None, :]
        kr = (2 * j + a + 0 * kc).reshape(-1)
        kcc = (0 * a + kc).reshape(-1)
        r = (2 * i + a + 0 * kc).reshape(-1)
        c = kcc.copy()
        r0 = np.clip(r - 4, 0, 56)
        cs = np.clip(c - 8, 0, 48)
        KR, R = kr[:, None], r[None, :]
        KC, C = kcc[:, None], c[None, :]
        valid = (KR >= r0[None, :]) & (KR < r0[None, :] + 8) & (KC >= cs[None, :]) & (KC < cs[None, :] + 16)
        dr = np.clip(KR - R + 7, 0, 14)
        dc = np.clip(KC - C + 15, 0, 30)
        g = rpb[:, dr, dc]
        return np.where(valid[None], g, tab)
    pats = []
    pats += [pattern(0, o) for o in range(-2, 4)]
    pats += [pattern(1, o) for o in range(-2, 3)] + [np.full((8, 128, 128), NEG, np.float32)]
    pats += [pattern(8, o) for o in range(-2, 3)]
    pats += [pattern(14, o) for o in range(-2, 3)] + [np.full((8, 128, 128), NEG, np.float32)]
    pats += [pattern(15, o) for o in range(-3, 3)]
    arr = np.stack(pats, 0)
    arr = arr.transpose(2, 0, 1, 3).reshape(128, NPAT * 8 * 128)
    return np.ascontiguousarray(arr)


_CACHE = {}


def kernel(x, norm1_g, w_in, b_gate, gmlp_ln_g, gmlp_ln_b, gmlp_w_s, gmlp_b_s,
           na_q_g, na_k_g, na_rpb, w_o_a, w_o_b, w_out, norm2_g, w_ff1, w_ff2):
    if "nc" not in _CACHE:
        _CACHE["nc"] = build_program()[0]
    nc = _CACHE["nc"]
    in_maps = make_in_maps(x, norm1_g, w_in, b_gate, gmlp_ln_g, gmlp_ln_b, gmlp_w_s, gmlp_b_s,
                           na_q_g, na_k_g, na_rpb, w_o_a, w_o_b, w_out, norm2_g, w_ff1, w_ff2)
    res = run_bass_kernel_spmd(nc, in_maps, core_ids=list(range(8)))
    out = np.empty((4, 2 * NTOK, D), np.float32)
    for core in range(8):
        b, hf = core // 2, core % 2
        out[b, hf * NTOK:(hf + 1) * NTOK] = res.results[core]["out"]
    return out


def make_in_maps(x, norm1_g, w_in, b_gate, gmlp_ln_g, gmlp_ln_b, gmlp_w_s, gmlp_b_s,
                 na_q_g, na_k_g, na_rpb, w_o_a, w_o_b, w_out, norm2_g, w_ff1, w_ff2):
    f = lambda a: np.ascontiguousarray(np.asarray(a, dtype=np.float32))
    x = f(x)
    shared = {
        "w_in": f(w_in[0]), "w_o_a": f(w_o_a[0]), "w_o_b": f(w_o_b[0]), "w_out": f(w_out[0]),
        "w_ff1": f(w_ff1[0]), "w_ff2": f(w_ff2[0]),
        "norm1_g": f(norm1_g[0]).reshape(1, D), "norm2_g": f(norm2_g[0]).reshape(1, D),
        "ln_g": f(gmlp_ln_g[0]).reshape(1, 512), "ln_b": f(gmlp_ln_b[0]).reshape(1, 512),
        "b_s": f(gmlp_b_s[0]).reshape(1, 512),
        "w_sT": f(np.transpose(np.asarray(gmlp_w_s[0]), (2, 0, 1)).reshape(128, 512)),
        "ident": np.eye(128, dtype=np.float32),
        "bones": np.kron(np.eye(2, dtype=np.float32), np.ones((64, 64), np.float32)),
    }
    qg = np.tile(np.asarray(na_q_g[0], np.float32), 2).reshape(128, 1)
    kg = np.tile(np.asarray(na_k_g[0], np.float32), 2).reshape(128, 1)
    bg = np.asarray(b_gate[0], np.float32).reshape(16, 128).T
    z64 = np.zeros((64, 1), np.float32)
    qg0 = np.concatenate([qg[:64], z64], axis=0)
    qg1 = np.concatenate([z64, qg[64:]], axis=0)
    shared["smalls"] = f(np.concatenate([qg, kg, bg, qg0, qg1], axis=1))
    tabs = [_bias_tables(np.asarray(na_rpb[0], np.float32), hf) for hf in range(2)]
    zeros = np.zeros((256, D), np.float32)
    in_maps = []
    for core in range(8):
        b, hf = core // 2, core % 2
        own = x[b, hf * NTOK:(hf + 1) * NTOK]
        before = x[b, NTOK - 256:NTOK] if hf == 1 else zeros
        after = x[b, NTOK:NTOK + 256] if hf == 0 else zeros
        m = dict(shared)
        m["x_ext"] = np.ascontiguousarray(np.concatenate([own, before, after], axis=0))
        m["btab"] = tabs[hf]
        in_maps.append(m)
    return in_maps
```

```python
import numpy as np
import concourse.bass as bass
import concourse.mybir as mybir
from concourse.bass_utils import run_bass_kernel_spmd

F32 = mybir.dt.float32
BF16 = mybir.dt.bfloat16
AF = mybir.ActivationFunctionType
ALU = mybir.AluOpType

D = 1024
NTOK = 2048
NT = 16
NSLOT = 20
RMS_EPS = 1e-6
LN_EPS = 1e-5
NEG = -30000.0
DBG_MP = 4
NPAT = 29

C_U, C_V, C_Q, C_K, C_VA, C_G0, C_G1 = 0, 512, 1024, 1536, 2048, 2560, 3584


def _compact(ops):
    out, last = [], {}
    for p in ops:
        if p.is_dma:
            out.append(p)
        elif p.eng not in last or last[p.eng].idx < p.idx:
            last[p.eng] = p
    return out + list(last.values())


class Buf:
    registry = {"sb": [], "ps": []}

    def __init__(self, name, space, start, size, parent=None):
        self.name, self.space, self.start, self.size = name, space, start, size
        self.last_w = None
        self.readers = []
        self.inherit = list(parent.inherit) if parent is not None else []
        self.dead = False
        for o in Buf.registry[space]:
            if o.start < start + size and start < o.start + o.size:
                if o.last_w is not None:
                    self.inherit.append(o.last_w)
                self.inherit.extend(o.readers)
                self.inherit.extend(o.inherit)
                o.dead = True
        self.inherit = _compact(self.inherit)
        Buf.registry[space].append(self)


def join_region(space, start, size):
    j = Buf("join", space, start, size)
    j.dead = True
    Buf.registry[space] = [o for o in Buf.registry[space] if o is not j]
    return j


class SemGroup:
    all_groups = []

    def __init__(self, name, kind):
        self.name, self.kind = name, kind
        self.n = 0
        self.sem = None
        SemGroup.all_groups.append(self)


class Op:
    __slots__ = ("eng", "fn", "is_dma", "group", "gidx", "waits", "need_inc", "count", "idx")


class Sched:
    ENGS = ("pe", "act", "dve", "pool", "sp")

    def __init__(self, nc):
        self.nc = nc
        self.ops = []
        self.eng_obj = {"pe": nc.tensor, "act": nc.scalar, "dve": nc.vector, "pool": nc.gpsimd, "sp": nc.sync}

    def add(self, eng, fn, reads=(), writes=(), dma=None):
        if getattr(self, "frozen", False):
            return None
        op = Op()
        op.eng, op.fn, op.is_dma, op.group = eng, fn, dma is not None, dma
        op.idx = len(self.ops)
        op.need_inc = False
        op.count = None
        if dma is not None:
            assert getattr(dma, "eng", eng) == eng, "a DMA semaphore group must be fed by a single queue"
            dma.eng = eng
            op.gidx = dma.n
            dma.n += 1
        deps = []
        for b in reads:
            assert not b.dead, f"read of dead buf {b.name}"
            if b.last_w is not None:
                deps.append(b.last_w)
            deps.extend(b.inherit) if b.last_w is None else None
        for b in writes:
            assert not b.dead, f"write of dead buf {b.name}"
            if b.last_w is not None:
                deps.append(b.last_w)
            deps.extend(b.readers)
            deps.extend(b.inherit)
        w = []
        seen = set()
        if dma is not None and dma.kind == "all":
            assert all(not (p.is_dma and p.group is dma) for p in deps), "intra-'all'-group dependency"
        for p in deps:
            if p.idx in seen or p is op:
                continue
            seen.add(p.idx)
            if (not p.is_dma) and (not op.is_dma) and p.eng == "pe" and op.eng == "pe":
                continue
            w.append(p)
        op.waits = w
        for b in reads:
            b.readers.append(op)
            if len(b.readers) > 1:
                b.readers = _compact(b.readers)
        for b in writes:
            b.last_w = op
            b.readers = []
            b.inherit = []
        self.ops.append(op)
        return op

    def emit(self, final_groups):
        nc = self.nc
        for op in self.ops:
            for p in op.waits:
                p.need_inc = True
        sems = {e: nc.alloc_semaphore("s_" + e) for e in ("pe", "act", "dve", "pool")}
        cnt = {e: 0 for e in sems}
        waited = {e: {} for e in self.ENGS}
        nwait = 0
        for op in self.ops:
            eo = self.eng_obj[op.eng]
            need = {}
            for p in op.waits:
                if p.is_dma:
                    g = p.group
                    sem = g.sem
                    val = 16 * (p.gidx + 1) if g.kind == "slot" else 16 * g.n
                else:
                    sem = sems[p.eng]
                    val = p.count
                key = id(sem)
                if key not in need or need[key][1] < val:
                    need[key] = (sem, val)
            for key, (sem, val) in need.items():
                if waited[op.eng].get(key, 0) >= val:
                    continue
                waited[op.eng][key] = val
                eo.wait_ge(sem, val)
                nwait += 1
            inst = op.fn()
            if op.is_dma:
                g = op.group
                if g.sem is None:
                    g.sem = nc.alloc_semaphore("d_" + g.name)
                inst.then_inc(g.sem, 16)
            elif op.need_inc:
                cnt[op.eng] += 1
                op.count = cnt[op.eng]
                inst.then_inc(sems[op.eng], 1)
        for g in final_groups:
            if g.sem is not None:
                nc.sync.wait_ge(g.sem, 16 * g.n)
        self.stats = dict(n_ops=len(self.ops), n_waits=nwait, counts=dict(cnt))


def build_program(stop=None):
    Buf.registry = {"sb": [], "ps": []}
    SemGroup.all_groups = []
    nc = bass.Bass("TRN2", target_bir_lowering=False)
    S = Sched(nc)
    g_dbg = SemGroup("dbg", "all")

    def ck(k, items):
        if stop != k or getattr(S, "frozen", False):
            return
        for name, ap, buf in items:
            dt_ = nc.dram_tensor("dbg_" + name, list(ap.shape), F32, kind="ExternalOutput").ap()
            S.add("pool", lambda dt_=dt_, ap=ap: nc.gpsimd.dma_start(out=dt_, in_=ap, max_dma_last_dim=2048), reads=[buf], dma=g_dbg)
        S.frozen = True

    def din(name, shape):
        return nc.dram_tensor(name, list(shape), F32, kind="ExternalInput").ap()

    x_ext = din("x_ext", [NSLOT * 128, D])
    w_in = din("w_in", [D, 4608])
    w_o_a = din("w_o_a", [512, D])
    w_o_b = din("w_o_b", [512, D])
    w_out = din("w_out", [D, D])
    w_ff1 = din("w_ff1", [D, 4096])
    w_ff2 = din("w_ff2", [4096, D])
    g1_d = din("norm1_g", [1, D])
    g2_d = din("norm2_g", [1, D])
    lng_d = din("ln_g", [1, 512])
    lnb_d = din("ln_b", [1, 512])
    bs_d = din("b_s", [1, 512])
    wst_d = din("w_sT", [128, 512])
    smalls_d = din("smalls", [128, 20])
    ident_d = din("ident", [128, 128])
    bones_d = din("bones", [128, 128])
    btab_d = din("btab", [128, NPAT * 8 * 128])
    out_d = nc.dram_tensor("out", [NTOK, D], F32, kind="ExternalOutput").ap()

    ARENA_ELEMS = 105472
    arena = nc.alloc_sbuf_tensor("arena", [128, ARENA_ELEMS], BF16)
    psum = nc.alloc_psum_tensor("psum", [128, 4096], F32)

    def sb(name, off, nbytes, dtype=BF16, parent=None):
        assert off % 4 == 0 and off + nbytes <= ARENA_ELEMS * 2, (name, off, nbytes)
        ap = arena[:, off // 2:(off + nbytes) // 2]
        if dtype == F32:
            ap = ap.bitcast(F32)
        return ap, Buf(name, "sb", off, nbytes, parent=parent)

    def pbank(name, bank, nbanks=1, dtype=F32, boff=0, nbytes=None):
        start = bank * 2048 + boff
        nb = nbanks * 2048 - boff if nbytes is None else nbytes
        ap = psum[:, start // 4:(start + nb) // 4]
        if dtype == BF16:
            ap = ap.bitcast(BF16)
        return ap, Buf(name, "ps", start, nb)

    R_A = 0
    R_B = 65536
    R_C = 98304
    R_D = 163840
    R_E = 208896
    END = ARENA_ELEMS * 2

    g_const = SemGroup("const_sw", "all")
    g_const_h = SemGroup("const_hw", "all")
    ident, b_ident = sb("ident", R_E, 256)
    bones, b_bones = sb("bones", R_E + 256, 256)
    smalls, b_smalls = sb("smalls", R_E + 1296, 80, F32)
    hbg, b_hbg = sb("hbg", R_E + 584, 64, F32)
    ss1, b_ss1 = sb("ss1", R_E + 648, 80, F32)
    std1, b_std1 = sb("std1", R_E + 728, 80, F32)
    rstd1, b_rstd1 = sb("rstd1", R_E + 808, 80, F32)
    ss2, b_ss2 = sb("ss2", R_E + 888, 64, F32)
    rstd2, b_rstd2 = sb("rstd2", R_E + 952, 64, F32)
    mhalf, b_mhalf = sb("mhalf", R_E + 1016, 64, F32)
    lnst, b_lnst = sb("lnst", R_E + 1080, 24 * 2, F32)
    lnmv, b_lnmv = sb("lnmv", R_E + 1128, 8 * 2, F32)
    lnr, b_lnr = sb("lnr", R_E + 1144, 4 * 2, F32)
    rden, b_rden_ = sb("rden", R_E + 1152, 32 * 2, F32)
    v2t, b_v2t = sb("v2t", R_E + 1216, 64, F32)
    epsc, b_epsc = sb("epsc", R_E + 1280, 16, F32)
    assert R_E + 1376 <= END
    S.add("pool", lambda: nc.gpsimd.memset(epsc[:, 0:1], RMS_EPS), writes=[b_epsc])
    S.add("pool", lambda: nc.gpsimd.memset(epsc[:, 1:2], 64 * RMS_EPS), writes=[b_epsc])

    S.add("pool", lambda: nc.gpsimd.dma_start(out=ident, in_=ident_d), writes=[b_ident], dma=g_const)
    S.add("pool", lambda: nc.gpsimd.dma_start(out=bones, in_=bones_d), writes=[b_bones], dma=g_const)
    S.add("sp", lambda: nc.sync.dma_start(out=smalls, in_=smalls_d), writes=[b_smalls], dma=g_const_h)
    S.add("pool", lambda: nc.gpsimd.memset(mhalf, -0.5), writes=[b_mhalf])
    S.add("dve", lambda: nc.vector.tensor_scalar(out=hbg, in0=smalls[:, 2:18], scalar1=0.5, scalar2=None,
                                                  op0=ALU.mult), reads=[b_smalls], writes=[b_hbg])

    hT = arena[:, R_A // 2:(R_A + 32768) // 2]
    b_hTg = [Buf(f"hTg{g}", "sb", R_A + g * 8192, 8192) for g in range(4)]
    hT3 = hT.rearrange("p (k t) -> p k t", k=8)
    hTh, b_hTh = sb("hTh", R_D, 8192)
    hTh3 = hTh.rearrange("p (k t) -> p k t", k=8)
    b_hT_g = [Buf(f"hT_g{g}", "sb", R_A + 0, 0) for g in range(0)]

    o = R_D + 8192
    NXR = 4
    xr = []
    for i in range(NXR):
        xr.append(sb(f"xr{i}", o, 4096, F32)); o += 4096
    hb = []
    for i in range(2):
        hb.append(sb(f"hb{i}", o, 2048)); o += 2048
    g1t, b_g1t = sb("g1t", o, 4096, F32); o += 4096
    junks = []
    for i in range(2):
        junks.append(sb(f"junk{i}", o, 2048)); o += 2048
    wrKa, b_wrK = sb("wrK", o, 8192); o += 8192
    wrK3 = wrKa.rearrange("p (k f) -> p k f", k=8)
    assert o <= R_E
    g_xr = [SemGroup(f"xr{i}", "slot") for i in range(NXR)]
    S.add("sp", lambda: nc.sync.dma_start(out=g1t, in_=g1_d.partition_broadcast(128)), writes=[b_g1t], dma=g_const_h)

    pT = [pbank(f"pT{i}", i, dtype=BF16) for i in range(2)]

    hT_gb = [Buf(f"hTg{g}", "sb", R_A + g * 1024, 1024) for g in range(0)]

    b_ss1c = [Buf(f"ss1_{t}", "sb", R_E + 648 + 4 * t, 4) for t in range(NSLOT)]
    b_std1c = [Buf(f"std1_{t}", "sb", R_E + 728 + 4 * t, 4) for t in range(NSLOT)]
    b_rstd1c = [Buf(f"rstd1_{t}", "sb", R_E + 808 + 4 * t, 4) for t in range(NSLOT)]
    p0_order = list(range(NT, NSLOT)) + list(range(NT))

    def p0_A(ti):
        t = p0_order[ti]
        xa, xb_ = xr[ti % NXR]
        ha, hb_ = hb[ti % 2]
        ja, jb_ = junks[ti % 2]
        S.add("sp", lambda: nc.sync.dma_start(out=xa, in_=x_ext[t * 128:(t + 1) * 128, :]), writes=[xb_], dma=g_xr[ti % NXR])
        S.add("act", lambda: nc.scalar.activation(out=ja, in_=xa, func=AF.Square, accum_out=ss1[:, t:t + 1]),
              reads=[xb_], writes=[jb_, b_ss1c[t]])
        S.add("pool", lambda: nc.gpsimd.tensor_scalar(out=std1[:, t:t + 1], in0=ss1[:, t:t + 1], scalar1=1.0 / D, scalar2=RMS_EPS,
                                                      op0=ALU.mult, op1=ALU.add), reads=[b_ss1c[t]], writes=[b_std1c[t]])
        S.add("pool", lambda: nc.gpsimd.tensor_tensor(out=rstd1[:, t:t + 1], in0=std1[:, t:t + 1], in1=mhalf[:, 0:1], op=ALU.pow),
              reads=[b_std1c[t], b_mhalf], writes=[b_rstd1c[t]])

    def p0_A2(ti):
        t = p0_order[ti]
        xa, xb_ = xr[ti % NXR]
        ha, hb_ = hb[ti % 2]
        S.add("dve", lambda: nc.vector.scalar_tensor_tensor(out=ha, in0=xa, scalar=rstd1[:, t:t + 1], in1=g1t, op0=ALU.mult, op1=ALU.mult),
              reads=[xb_, b_rstd1c[t], b_g1t], writes=[hb_])

    def p0_B(ti):
        t = p0_order[ti]
        ha, hb_ = hb[ti % 2]
        pa, pb_ = pT[ti % 2]
        pa3 = pa.rearrange("p (k t) -> p k t", k=8)
        for kc in range(8):
            S.add("pe", lambda kc=kc: nc.tensor.transpose(pa3[:, kc, :], ha[:, kc * 128:(kc + 1) * 128], ident),
                  reads=[hb_, b_ident], writes=[pb_])
        if t < NT:
            dst, dbuf = hT3[:, :, t * 128:(t + 1) * 128], b_hTg[t // 4]
        else:
            dst, dbuf = hTh3[:, :, (t - NT) * 128:(t - NT + 1) * 128], b_hTh
        S.add("dve", lambda: nc.vector.tensor_copy(out=dst, in_=pa3), reads=[pb_], writes=[dbuf])

    qm, b_qm = sb("qm", R_B, 32768)
    qm4 = qm.rearrange("p (h c t) -> p h c t", h=2, c=4)
    kT, b_kT = sb("kT", R_C, 20480)
    kT3 = kT.rearrange("p (c t) -> p c t", c=4)
    Va, b_Va = sb("Vaug", R_C + 20480, 20800)
    Va4 = Va.rearrange("p (s h d) -> p s h d", s=NSLOT, h=8)
    o = R_A + 32768
    sq = []
    for i in range(2):
        sq.append(sb(f"sq{i}", o, 1024)); o += 1024
    stdb = []
    for i in range(2):
        stdb.append(sb(f"std{i}", o, 2048, F32)); o += 2048
    rsb = []
    for i in range(2):
        rsb.append(sb(f"rs{i}", o, 2048, F32)); o += 2048
    assert o <= R_A + 49152
    wkey = {}
    g_wrK = SemGroup("wrK", "slot")
    g_wv = SemGroup("wv", "slot")
    w_in3 = w_in.rearrange("(k p) f -> p k f", p=128)

    S.add("pool", lambda: nc.gpsimd.memset(Va4[:, :, :, 64:65], 1.0), writes=[b_Va])

    mm_ps = [pbank(f"mm{i}", 2 + i) for i in range(4)]
    ss_ps = [pbank(f"ssp{i}", 6 + i) for i in range(2)]

    def load_w256(key, col0):
        (wa, wb), grp = wkey[key]
        wa3 = wa.rearrange("p (k f) -> p k f", k=8)
        S.add("pool", lambda: nc.gpsimd.dma_start(out=wa3, in_=w_in3[:, :, col0:col0 + 256]),
              writes=[wb], dma=grp)
        return wa3, wb

    def rhs_group(tg):
        if tg < 4:
            return (lambda kc: hT3[:, kc, tg * 512:(tg + 1) * 512]), b_hTg[tg]
        return (lambda kc: hTh3[:, kc, :]), b_hTh

    units_qk = []
    for tg in [4, 0, 1, 2, 3]:
        for cp in range(2):
            for cc in range(2):
                units_qk.append(("k", C_K, cp, cc, tg))
    for cp in range(2):
        for cc in range(2):
            for tg in [0, 1, 2, 3]:
                units_qk.append(("q", C_Q, cp, cc, tg))
    wcache = {}

    def qk_mm(u):
        kind, cbase, cp, cc, tg = units_qk[u]
        key = (kind, cp)
        if key not in wcache:
            wcache[key] = load_w256(key, cbase + cp * 256)
        wa3, wb = wcache[key]
        rf, rbuf = rhs_group(tg)
        pa, pb_ = mm_ps[u % 4]
        qa, qb_ = sq[u % 2]
        for kc in range(8):
            S.add("pe", lambda kc=kc: nc.tensor.matmul(pa, lhsT=wa3[:, kc, cc * 128:(cc + 1) * 128], rhs=rf(kc),
                                                       start=(kc == 0), stop=(kc == 7)), reads=[wb, rbuf], writes=[pb_])
        S.add("act", lambda: nc.scalar.activation(out=qa, in_=pa, func=AF.Square), reads=[pb_], writes=[qb_])

    def qk_fin(u):
        kind, cbase, cp, cc, tg = units_qk[u]
        c = cp * 2 + cc
        pa, pb_ = mm_ps[u % 4]
        sa, sb_ = ss_ps[u % 2]
        qa, qb_ = sq[u % 2]
        sta, stb_ = stdb[u % 2]
        ra, rb_ = rsb[u % 2]
        S.add("pe", lambda: nc.tensor.matmul(sa, lhsT=bones, rhs=qa, start=True, stop=True), reads=[qb_, b_bones], writes=[sb_])
        if kind == "k":
            S.add("act", lambda: nc.scalar.activation(out=sta, in_=sa, func=AF.Ln, scale=1.0 / 64, bias=RMS_EPS), reads=[sb_], writes=[stb_])
        else:
            S.add("act", lambda: nc.scalar.activation(out=sta, in_=sa, func=AF.Ln, scale=1.0, bias=64 * RMS_EPS), reads=[sb_], writes=[stb_])
        S.add("act", lambda: nc.scalar.activation(out=ra, in_=sta, func=AF.Exp, scale=-0.5), reads=[stb_], writes=[rb_])
        if kind == "k":
            tok0 = tg * 512
            S.add("dve", lambda: nc.vector.scalar_tensor_tensor(out=kT3[:, c, tok0:tok0 + 512], in0=pa, scalar=smalls[:, 1:2], in1=ra,
                                                                op0=ALU.mult, op1=ALU.mult), reads=[pb_, rb_, b_smalls], writes=[b_kT])
        else:
            for hp in range(2):
                S.add("dve", lambda hp=hp: nc.vector.scalar_tensor_tensor(
                    out=qm4[:, hp, c, tg * 512:(tg + 1) * 512], in0=pa, scalar=smalls[:, 18 + hp:19 + hp],
                    in1=ra, op0=ALU.mult, op1=ALU.mult), reads=[pb_, rb_, b_smalls], writes=[b_qm])

    qk_next = [0]

    def qk_step():
        u = qk_next[0]
        if u < len(units_qk):
            qk_mm(u)
        if u >= 1:
            qk_fin(u - 1)
        qk_next[0] = u + 1

    S.add("pool", lambda: nc.gpsimd.dma_start(out=wrK3, in_=w_in3[:, :, C_K:C_K + 512]), writes=[b_wrK], dma=g_wrK)
    wcache[("k", 0)] = (wrK3[:, :, 0:256], b_wrK)
    wcache[("k", 1)] = (wrK3[:, :, 256:512], b_wrK)
    p0_A(0)
    p0_A(1)
    p0_A2(0)
    for ti in range(NSLOT):
        if ti + 2 < NSLOT:
            p0_A(ti + 2)
        if ti + 1 < NSLOT:
            p0_A2(ti + 1)
        p0_B(ti)
        if ti >= 4:
            qk_step()
    ck(0, [("hT", hT, b_hTg[3]), ("hTh", hTh, b_hTh)])
    wrQa, b_wrQ = sb("wrQ", R_D + 8192, 8192)
    wrQ3 = wrQa.rearrange("p (k f) -> p k f", k=8)
    g_wrQ = SemGroup("wrQ", "slot")
    S.add("pool", lambda: nc.gpsimd.dma_start(out=wrQ3, in_=w_in3[:, :, C_Q:C_Q + 512]), writes=[b_wrQ], dma=g_wrQ)
    wcache[("q", 0)] = (wrQ3[:, :, 0:256], b_wrQ)
    wcache[("q", 1)] = (wrQ3[:, :, 256:512], b_wrQ)
    wv, b_wv = sb("wv", R_D + 16384, 8192)
    while qk_next[0] <= len(units_qk):
        qk_step()
    unit = len(units_qk)

    wv3 = wv.rearrange("p (k f) -> p k f", k=8)
    S.add("pool", lambda: nc.gpsimd.dma_start(out=wv3, in_=w_in3[:, :, C_VA:C_VA + 512]), writes=[b_wv], dma=g_wv)
    for j in range(NSLOT):
        pa, pb_ = mm_ps[unit % 4]
        unit += 1
        if j < NT:
            lf, lbuf = (lambda kc, j=j: hT3[:, kc, j * 128:(j + 1) * 128]), b_hTg[j // 4]
        else:
            lf, lbuf = (lambda kc, j=j: hTh3[:, kc, (j - NT) * 128:(j - NT + 1) * 128]), b_hTh
        for kc in range(8):
            S.add("pe", lambda pa=pa, lf=lf, kc=kc: nc.tensor.matmul(pa, lhsT=lf(kc), rhs=wv3[:, kc, :],
                                                                     start=(kc == 0), stop=(kc == 7)),
                  reads=[b_wv, lbuf], writes=[pb_])
        src = pa.rearrange("p (h d) -> p h d", h=8)
        dst = Va4[:, j, :, 0:64]
        if j % 2 == 0:
            S.add("act", lambda dst=dst, src=src: nc.scalar.copy(out=dst, in_=src), reads=[pb_], writes=[b_Va])
        else:
            S.add("dve", lambda dst=dst, src=src: nc.vector.tensor_copy(out=dst, in_=src), reads=[pb_], writes=[b_Va])

    ck(1, [("kT", kT, b_kT), ("qm", qm, b_qm), ("Va", Va, b_Va)])
    tin, b_tin = sb("tab_int", R_C + 41472, 10240)
    tin4 = tin.rearrange("p (a h q) -> p a h q", a=5, h=8)
    tsa, b_tsa = sb("tab_sa", R_C + 51712, 12288)
    tsa4 = tsa.rearrange("p (a h q) -> p a h q", a=6, h=8)
    assert R_C + 51712 + 12288 <= R_D
    o = R_D
    tsb, b_tsb = sb("tab_sb", R_A + 49152, 12288)
    tsb4 = tsb.rearrange("p (a h q) -> p a h q", a=6, h=8)
    PTb = []
    for i in range(3):
        PTb.append(sb(f"PT{i}", o, 3072)); o += 3072
    Sbb = []
    for i in range(2):
        Sbb.append(sb(f"Sb{i}", o, 6144, F32)); o += 6144
    ybt = []
    for i in range(2):
        ybt.append(sb(f"ybt{i}", o, 1024)); o += 1024
    assert o <= R_E
    y_bT, b_ybT = sb("y_bT", R_A + 32768, 16384)
    y_bT3 = y_bT.rearrange("p (c t) -> p c t", c=4)
    g_tin = SemGroup("tin", "slot")
    g_tsa = SemGroup("tsa", "slot")
    g_tsb = SemGroup("tsb", "slot")
    PW = 8 * 128

    def load_tab(dst4, dbuf, grp, p0, npat, eng="pool"):
        S.add("pool", lambda: nc.gpsimd.dma_start(
            out=dst4[:, 0:npat, :, :].rearrange("p a h q -> p a (h q)"),
            in_=btab_d[:, p0 * PW:(p0 + npat) * PW].rearrange("p (a x) -> p a x", a=npat)),
            writes=[dbuf], dma=grp)

    load_tab(tsa4, b_tsa, g_tsa, 0, 6)
    load_tab(tin4, b_tin, g_tin, 12, 5)
    load_tab(tsb4, b_tsb, g_tsb, 6, 6)

    S_ps = [pbank(f"S{i}", 3 * i, nbytes=6144) for i in range(2)]
    _pv = pbank("PV", 6, nbytes=4 * 65 * 4)
    PV_ps = [_pv, _pv]
    yTp_a, yTp_b = pbank("yTp", 7, dtype=BF16, nbytes=1024)
    yTp3 = yTp_a.rearrange("p (c t) -> p c t", c=4)

    def kst(s):
        if 2 <= s < 18:
            return s - 2
        return 16 + s if s < 2 else s

    def na_unit_info(il):
        if il == 0:
            offs = list(range(-2, 4)); tab = (tsa4, b_tsa)
        elif il == 1:
            offs = list(range(-2, 3)); tab = (tsb4, b_tsb)
        elif il == 14:
            offs = list(range(-2, 3)); tab = (tsa4, b_tsa)
        elif il == 15:
            offs = list(range(-3, 3)); tab = (tsb4, b_tsb)
        else:
            offs = list(range(-2, 3)); tab = (tin4, b_tin)
        slots = [il + 2 + o_ for o_ in offs]
        return slots, tab

    def na_S(il, c, u):
        slots, (t4, tbuf) = na_unit_info(il)
        ns = len(slots)
        sa, sb_ = S_ps[u % 2]
        pa, pb_ = PTb[u % 3]
        for n, s in enumerate(slots):
            j = kst(s)
            S.add("pe", lambda n=n, j=j: nc.tensor.matmul(
                sa[:, n * 256:(n + 1) * 256], lhsT=kT3[:, c, j * 128:(j + 1) * 128],
                rhs=qm4[:, :, c, il * 128:(il + 1) * 128], start=True, stop=False),
                reads=[b_kT, b_qm], writes=[sb_])
            S.add("pe", lambda n=n: nc.tensor.matmul(
                sa[:, n * 256:(n + 1) * 256], lhsT=ident,
                rhs=t4[:, n, 2 * c:2 * c + 2, :], start=False, stop=True),
                reads=[tbuf, b_ident], writes=[sb_])
        S.add("act", lambda: nc.scalar.activation(out=pa[:, 0:ns * 256], in_=sa[:, 0:ns * 256], func=AF.Exp),
              reads=[sb_], writes=[pb_])

    def na_PV(il, c, u):
        slots, _ = na_unit_info(il)
        pa, pb_ = PTb[u % 3]
        for hp in range(2):
            h = 2 * c + hp
            va, vb_ = PV_ps[h // 4]
            va3 = va.rearrange("p (h d) -> p h d", h=4)
            for n, s in enumerate(slots):
                j = kst(s)
                S.add("pe", lambda va3=va3, n=n, j=j, h=h, hp=hp, last=(n == len(slots) - 1): nc.tensor.matmul(
                    va3[:, h % 4, :], lhsT=pa[:, n * 256 + hp * 128:n * 256 + hp * 128 + 128], rhs=Va4[:, j, h, :],
                    start=(n == 0), stop=last), reads=[pb_, b_Va], writes=[vb_])
            if h % 4 == 3:
                hb4 = h // 4
                ya, yb_ = ybt[il % 2]
                ya3 = ya.rearrange("p (h d) -> p h d", h=8)
                rd = rden[:, (il % 2) * 8 + hb4 * 4:(il % 2) * 8 + hb4 * 4 + 4]
                S.add("dve", lambda rd=rd, va3=va3: nc.vector.reciprocal(out=rd, in_=va3[:, :, 64]),
                      reads=[vb_], writes=[b_rden_])
                S.add("dve", lambda ya3=ya3, va3=va3, rd=rd, hb4=hb4: nc.vector.tensor_tensor(
                    out=ya3[:, hb4 * 4:(hb4 + 1) * 4, :], in0=va3[:, :, 0:64],
                    in1=rd.unsqueeze(2).to_broadcast([128, 4, 64]), op=ALU.mult),
                    reads=[vb_, b_rden_], writes=[yb_])

    def na_T(il):
        ya, yb_ = ybt[il % 2]
        for c in range(4):
            S.add("pe", lambda c=c: nc.tensor.transpose(yTp3[:, c, :], ya[:, c * 128:(c + 1) * 128], ident),
                  reads=[yb_, b_ident], writes=[yTp_b])
        dst = y_bT3[:, :, il * 128:(il + 1) * 128]
        S.add("dve", lambda dst=dst: nc.vector.tensor_copy(out=dst, in_=yTp3), reads=[yTp_b], writes=[b_ybT])

    units = [(il, c) for il in range(NT) for c in range(4)]
    LAG = 2

    def na_issue_S(u):
        il2, c2 = units[u]
        if il2 == 3 and c2 == 0:
            load_tab(tsa4, b_tsa, g_tsa, 17, 6)
            load_tab(tsb4, b_tsb, g_tsb, 23, 6)
        na_S(il2, c2, u)

    for u in range(min(LAG, len(units))):
        na_issue_S(u)
    for u, (il, c) in enumerate(units):
        if u + LAG < len(units):
            na_issue_S(u + LAG)
        na_PV(il, c, u)
        if c == 0 and il > 0:
            na_T(il - 1)
    na_T(NT - 1)

    ck(2, [("ybT", y_bT, b_ybT)])
    uT = arena[:, (R_A + 49152) // 2:(R_A + 65536) // 2]
    j_uT = join_region("sb", R_A + 49152, 16384)
    b_uTg = [Buf(f"uTg{g}", "sb", R_A + 49152 + g * 4096, 4096, parent=j_uT) for g in range(4)]
    uT3 = uT.rearrange("p (c t) -> p c t", c=4)
    gtmp = []
    o = R_D
    for i in range(2):
        gtmp.append(sb(f"gtmp{i}", o, 2048, F32)); o += 2048
    o = R_D + 12288
    wst, b_wst = sb("wst", o, 1024); o += 1024
    bst, b_bst = sb("bst", o, 2048, F32); o += 2048
    lngt, b_lngt = sb("lngt", o, 2048, F32); o += 2048
    lnbt, b_lnbt = sb("lnbt", o, 2048, F32); o += 2048
    vtmp = []
    for i in range(3):
        vtmp.append(sb(f"vtmp{i}", o, 2048, F32)); o += 2048
    lnsm = []
    for i in range(3):
        lnsm.append(sb(f"lnsm{i}", o, 48, F32)); o += 48
    assert o <= R_D + 28672, o
    o = R_D + 28672
    wr2 = []
    for i in range(2):
        wr2.append(sb(f"wrb{i}", o, 4096)); o += 4096
    wvv, b_wvv = sb("wvv", o, 8192); o += 8192
    assert o <= R_E, o
    vn = arena[:, R_B // 2:(R_B + 16384) // 2]
    j_vn = join_region("sb", R_B, 16384)
    b_vnj = [Buf(f"vn{j}", "sb", R_B + j * 1024, 1024, parent=j_vn) for j in range(NT)]
    vn3 = vn.rearrange("p (j f) -> p j f", j=NT)
    g_c2 = SemGroup("const2", "all")
    S.add("sp", lambda: nc.sync.dma_start(out=lngt, in_=lng_d.partition_broadcast(128)), writes=[b_lngt], dma=g_c2)
    S.add("sp", lambda: nc.sync.dma_start(out=lnbt, in_=lnb_d.partition_broadcast(128)), writes=[b_lnbt], dma=g_c2)
    S.add("sp", lambda: nc.sync.dma_start(out=bst, in_=bs_d.partition_broadcast(128)), writes=[b_bst], dma=g_c2)
    g_c2p = SemGroup("const2_sw", "all")
    S.add("pool", lambda: nc.gpsimd.dma_start(out=wst, in_=wst_d), writes=[b_wst], dma=g_c2p)
    g_wr2 = [SemGroup(f"wrb{i}", "slot") for i in range(2)]
    g_wvv = SemGroup("wvv", "slot")

    mm2 = [pbank(f"mmb{i}", i) for i in range(4)]
    unit = 0
    u_w = []
    for cp in range(2):
        wa, wb = wr2[cp % 2]
        wa3 = wa.rearrange("p (k f) -> p k f", k=8)
        S.add("pool", lambda wa3=wa3, cp=cp: nc.gpsimd.dma_start(out=wa3, in_=w_in3[:, :, C_U + cp * 256:C_U + (cp + 1) * 256]),
              writes=[wb], dma=g_wr2[cp % 2])
        u_w.append((wa3, wb))

    def u_unit(i):
        tg, c = i // 4, i % 4
        cp, cc = c // 2, c % 2
        wa3, wb = u_w[cp]
        pa, pb_ = mm2[i % 2]
        for kc in range(8):
            S.add("pe", lambda kc=kc: nc.tensor.matmul(pa, lhsT=wa3[:, kc, cc * 128:(cc + 1) * 128], rhs=hT3[:, kc, tg * 512:(tg + 1) * 512],
                                                       start=(kc == 0), stop=(kc == 7)), reads=[wb, b_hTg[tg]], writes=[pb_])
        S.add("act", lambda: nc.scalar.activation(out=uT3[:, c, tg * 512:(tg + 1) * 512], in_=pa, func=AF.Gelu), reads=[pb_], writes=[b_uTg[tg]])

    wvv3 = wvv.rearrange("p (k f) -> p k f", k=8)
    S.add("pool", lambda: nc.gpsimd.dma_start(out=wvv3, in_=w_in3[:, :, C_V:C_V + 512]), writes=[b_wvv], dma=g_wvv)
    def v_A(j):
        pa, pb_ = mm2[2 + j % 2]
        va_, vb_ = vtmp[j % 3]
        sm, smb = lnsm[j % 3]
        st_, mv_, r_ = sm[:, 0:6], sm[:, 6:8], sm[:, 8:9]
        for kc in range(8):
            S.add("pe", lambda kc=kc: nc.tensor.matmul(pa, lhsT=hT3[:, kc, j * 128:(j + 1) * 128], rhs=wvv3[:, kc, :],
                                                       start=(kc == 0), stop=(kc == 7)),
                  reads=[b_wvv, b_hTg[j // 4]], writes=[pb_])
        S.add("act", lambda: nc.scalar.activation(out=va_, in_=pa, func=AF.Gelu), reads=[pb_], writes=[vb_])
        S.add("dve", lambda: nc.vector.bn_stats(out=st_, in_=va_), reads=[vb_], writes=[smb])
        S.add("dve", lambda: nc.vector.bn_aggr(out=mv_, in_=st_), reads=[smb], writes=[smb])
        S.add("pool", lambda: nc.gpsimd.tensor_scalar(out=r_, in0=mv_[:, 1:2], scalar1=LN_EPS, scalar2=None, op0=ALU.add),
              reads=[smb], writes=[smb])
        S.add("pool", lambda: nc.gpsimd.tensor_tensor(out=r_, in0=r_, in1=mhalf[:, 0:1], op=ALU.pow),
              reads=[smb, b_mhalf], writes=[smb])

    def v_B(j):
        va_, vb_ = vtmp[j % 3]
        sm, smb = lnsm[j % 3]
        mv_, r_ = sm[:, 6:8], sm[:, 8:9]
        S.add("dve", lambda: nc.vector.scalar_tensor_tensor(out=va_, in0=va_, scalar=mv_[:, 0:1], in1=lngt, op0=ALU.subtract, op1=ALU.mult),
              reads=[vb_, smb, b_lngt], writes=[vb_])
        S.add("dve", lambda: nc.vector.scalar_tensor_tensor(out=vn3[:, j, :], in0=va_, scalar=r_, in1=lnbt, op0=ALU.mult, op1=ALU.add),
              reads=[vb_, smb, b_lnbt], writes=[b_vnj[j]])

    wst3 = wst.rearrange("p (g q) -> p g q", g=4)
    mg = [pbank(f"mg{i}", 4 + i) for i in range(2)]

    def gmlp(j):
        pa, pb_ = mg[j % 2]
        ga, gb_ = gtmp[j % 2]
        for g in range(4):
            S.add("pe", lambda g=g: nc.tensor.matmul(
                pa[:, g * 128:(g + 1) * 128], lhsT=vn3[:, j, g * 128:(g + 1) * 128], rhs=wst3[:, g, :], start=True, stop=True),
                reads=[b_vnj[j], b_wst], writes=[pb_])
        S.add("dve", lambda: nc.vector.tensor_tensor(out=ga, in0=pa, in1=bst, op=ALU.add), reads=[pb_, b_bst], writes=[gb_])
        uv = uT3[:, :, j * 128:(j + 1) * 128]
        S.add("dve", lambda: nc.vector.tensor_tensor(out=uv, in0=ga.rearrange("p (g t) -> p g t", g=4), in1=uv, op=ALU.mult),
              reads=[gb_, b_uTg[j // 4]], writes=[b_uTg[j // 4]])

    woa, b_woa = sb("woa", R_C + 32768, 8192)
    wob, b_wob = sb("wob", R_C + 32768 + 8192, 8192)
    woa3 = woa.rearrange("p (k f) -> p k f", k=4)
    wob3 = wob.rearrange("p (k f) -> p k f", k=4)
    g_wo = SemGroup("wo", "all")
    wgA = [sb("wgA0", R_C + 0, 4096), sb("wgA1", R_C + 4096, 4096)]
    g_wgA = [SemGroup(f"wgA{i}", "slot") for i in range(2)]
    wgA_loaded = []
    for gi, cb in enumerate((C_G0, C_G1)):
        wa, wb = wgA[gi]
        wa3 = wa.rearrange("p (k f) -> p k f", k=8)
        S.add("pool", lambda wa3=wa3, cb=cb: nc.gpsimd.dma_start(out=wa3, in_=w_in3[:, :, cb:cb + 256]), writes=[wb], dma=g_wgA[gi])
        wgA_loaded.append((wa3, wb))
    S.add("pool", lambda: nc.gpsimd.dma_start(out=woa3, in_=w_o_a.rearrange("(k p) f -> p k f", p=128)), writes=[b_woa], dma=g_wo)
    S.add("pool", lambda: nc.gpsimd.dma_start(out=wob3, in_=w_o_b.rearrange("(k p) f -> p k f", p=128)), writes=[b_wob], dma=g_wo)
    wout_a, b_wout0 = sb("wout0", R_C + 49152, 8192)
    wout_b, b_wout1 = sb("wout1", R_C + 49152 + 8192, 8192)
    wout3 = arena[:, (R_C + 49152) // 2:(R_C + 49152 + 16384) // 2].rearrange("p (k f) -> p k f", k=8)
    g_wout = SemGroup("wout", "all")
    NXS = 4
    xs = [sb(f"xs{t}", R_C + 16384 + t * 4096, 4096, F32) for t in range(NXS)]
    g_xs = [SemGroup(f"xs{t}", "slot") for t in range(NXS)]
    for t in range(NXS):
        S.add("sp", lambda t=t: nc.sync.dma_start(out=xs[t][0], in_=x_ext[t * 128:(t + 1) * 128, :]), writes=[xs[t][1]], dma=g_xs[t])

    v_A(0)
    for j in range(NT):
        u_unit(j)
        if j + 1 < NT:
            v_A(j + 1)
        v_B(j)
        if j >= 4:
            gmlp(j - 4)
    for j in range(NT - 4, NT):
        gmlp(j)

    ck(3, [("uT", uT, b_uTg[3]), ("vn", vn, b_vnj[15])])
    y_aT3 = uT3

    ck(4, [("yaT", uT, b_uTg[3])])
    mT, b_mT = sb("mergedT", R_B, 32768)
    mT3 = mT.rearrange("p (k t) -> p k t", k=8)
    S.add("pool", lambda: nc.gpsimd.dma_start(out=wout3[:, 0:4, :], in_=w_out.rearrange("(k p) f -> p k f", p=128)[:, 0:4, :]),
          writes=[b_wout0], dma=g_wout)
    S.add("pool", lambda: nc.gpsimd.dma_start(out=wout3[:, 4:8, :], in_=w_out.rearrange("(k p) f -> p k f", p=128)[:, 4:8, :]),
          writes=[b_wout1], dma=g_wout)
    wgB = [sb("wgB0", R_D + 0, 4096), sb("wgB1", R_D + 4096, 4096)]
    wgC = [sb("wgC0", R_D + 36864, 4096), sb("wgC1", R_D + 40960, 4096)]
    g_wgB = [SemGroup(f"wgB{i}", "slot") for i in range(2)]
    g_wgC = [SemGroup(f"wgC{i}", "slot") for i in range(2)]
    tt = []
    o = R_D + 8192
    for i in range(4):
        tt.append(sb(f"tt{i}", o, 2048, F32)); o += 2048
    ffw = []
    g_ff = [(SemGroup(f"ff1_{i}", "slot"), SemGroup(f"ff2_{i}", "slot")) for i in range(3)]
    w_ff1_3 = w_ff1.rearrange("(k p) f -> p k f", p=128)
    w_ff2_3 = w_ff2.rearrange("(k p) f -> p k f", p=128)

    def load_ff(part):
        (w1a, w1b), (w2a, w2b) = ffw[part % 3]
        g1_, g2_ = g_ff[part % 3]
        S.add("pool", lambda: nc.gpsimd.dma_start(out=w1a.rearrange("p (k f) -> p k f", k=8),
                                                  in_=w_ff1_3[:, :, part * 512:(part + 1) * 512]), writes=[w1b], dma=g1_)
        S.add("pool", lambda: nc.gpsimd.dma_start(out=w2a.rearrange("p (k f) -> p k f", k=4),
                                                  in_=w_ff2_3[:, part * 4:(part + 1) * 4, :]), writes=[w2b], dma=g2_)

    gps = [[pbank(f"gp{i}_{k}", 4 * i + k) for k in range(4)] for i in range(2)]
    unit = 0
    for mp in range(DBG_MP):
        if mp == 0:
            was = wgA_loaded
        else:
            was = []
            bufs, grps = (wgB, g_wgB) if mp % 2 == 1 else (wgC, g_wgC)
            for gi, cb in enumerate((C_G0, C_G1)):
                wa, wb = bufs[gi]
                wa3 = wa.rearrange("p (k f) -> p k f", k=8)
                S.add("pool", lambda wa3=wa3, cb=cb, mp=mp: nc.gpsimd.dma_start(out=wa3, in_=w_in3[:, :, cb + mp * 256:cb + (mp + 1) * 256]),
                      writes=[wb], dma=grps[gi])
                was.append((wa3, wb))
        if mp == 2:
            ffw.append((sb("ff1_0", R_C, 8192), sb("ff2_0", R_C + 8192, 8192)))
            load_ff(0)
        for mm in range(2):
            m = mp * 2 + mm
            for tg in range(4):
                ps4 = gps[unit % 2]
                t0, t1, ta, tb = tt[0], tt[1], tt[2], tt[3]
                unit += 1
                tsl = slice(tg * 512, (tg + 1) * 512)
                for gi in range(2):
                    wa3, wb = was[gi]
                    pa, pb_ = ps4[gi]
                    for kc in range(8):
                        S.add("pe", lambda pa=pa, wa3=wa3, mm=mm, kc=kc, tsl=tsl: nc.tensor.matmul(
                            pa, lhsT=wa3[:, kc, mm * 128:(mm + 1) * 128], rhs=hT3[:, kc, tsl], start=(kc == 0), stop=(kc == 7)),
                            reads=[wb, b_hTg[tg]], writes=[pb_])
                pa, pb_ = ps4[2]
                for kc in range(4):
                    S.add("pe", lambda pa=pa, kc=kc, m=m, tsl=tsl: nc.tensor.matmul(
                        pa, lhsT=woa3[:, kc, m * 128:(m + 1) * 128], rhs=y_aT3[:, kc, tsl], start=(kc == 0), stop=(kc == 3)),
                        reads=[b_woa, b_uTg[tg]], writes=[pb_])
                pa, pb_ = ps4[3]
                for kc in range(4):
                    S.add("pe", lambda pa=pa, kc=kc, m=m, tsl=tsl: nc.tensor.matmul(
                        pa, lhsT=wob3[:, kc, m * 128:(m + 1) * 128], rhs=y_bT3[:, kc, tsl], start=(kc == 0), stop=(kc == 3)),
                        reads=[b_wob, b_ybT], writes=[pb_])
                S.add("act", lambda t0=t0, ps4=ps4, m=m: nc.scalar.activation(out=t0[0], in_=ps4[0][0], func=AF.Tanh, scale=0.5,
                                                                              bias=hbg[:, m:m + 1]),
                      reads=[ps4[0][1], b_hbg], writes=[t0[1]])
                S.add("act", lambda t1=t1, ps4=ps4, m=m: nc.scalar.activation(out=t1[0], in_=ps4[1][0], func=AF.Tanh, scale=0.5,
                                                                              bias=hbg[:, 8 + m:8 + m + 1]),
                      reads=[ps4[1][1], b_hbg], writes=[t1[1]])
                S.add("dve", lambda ta=ta, t0=t0, ps4=ps4: nc.vector.scalar_tensor_tensor(
                    out=ta[0], in0=t0[0], scalar=1.0, in1=ps4[2][0], op0=ALU.add, op1=ALU.mult),
                    reads=[t0[1], ps4[2][1]], writes=[ta[1]])
                S.add("dve", lambda tb=tb, t1=t1, ps4=ps4: nc.vector.scalar_tensor_tensor(
                    out=tb[0], in0=t1[0], scalar=1.0, in1=ps4[3][0], op0=ALU.add, op1=ALU.mult),
                    reads=[t1[1], ps4[3][1]], writes=[tb[1]])
                S.add("dve", lambda ta=ta, tb=tb, m=m, tsl=tsl: nc.vector.tensor_tensor(
                    out=mT3[:, m, tsl], in0=ta[0], in1=tb[0], op=ALU.add),
                    reads=[ta[1], tb[1]], writes=[b_mT])

    ck(5, [("mT", mT, b_mT)])
    j_RA = join_region("sb", R_A, 65536)
    x1 = []
    for t in range(NT):
        x1.append(sb(f"x1_{t}", R_A + t * 4096, 4096, F32, parent=j_RA))
    g_x1 = [SemGroup(f"x1_{t}", "slot") for t in range(NT)]

    def x_reload(t, after=None):
        xa, xb_ = x1[t]
        S.add("sp", lambda: nc.sync.dma_start(out=xa, in_=x_ext[t * 128:(t + 1) * 128, :]),
              reads=[after] if after is not None else [], writes=[xb_], dma=g_x1[t])

    for t in range(NXS, NXS + 3):
        x_reload(t)
    h2T = arena[:, R_D // 2:(R_D + 32768) // 2]
    j_RD = join_region("sb", R_D, 32768)
    b_h2Tg = [Buf(f"h2Tg{g}", "sb", R_D + g * 8192, 8192, parent=j_RD) for g in range(4)]
    h2T3 = h2T.rearrange("p (k t) -> p k t", k=8)
    o = R_D + 32768
    h2b = []
    for i in range(2):
        h2b.append(sb(f"h2b{i}", o, 2048)); o += 2048
    g2t, b_g2t = sb("g2t", o, 4096, F32); o += 4096
    junk2s = []
    for i in range(2):
        junk2s.append(sb(f"junk2_{i}", o, 2048)); o += 2048
    assert o <= R_E, o
    b_ss2c = [Buf(f"ss2_{t}", "sb", R_E + 888 + 4 * t, 4) for t in range(NT)]
    b_v2c = [Buf(f"v2_{t}", "sb", R_E + 1216 + 4 * t, 4) for t in range(NT)]
    b_rstd2c = [Buf(f"rstd2_{t}", "sb", R_E + 952 + 4 * t, 4) for t in range(NT)]
    g_c3 = SemGroup("const3", "all")
    S.add("sp", lambda: nc.sync.dma_start(out=g2t, in_=g2_d.partition_broadcast(128)), writes=[b_g2t], dma=g_c3)

    wo_ps = [pbank(f"wo{i}", i) for i in range(4)]
    pT2 = [pbank(f"pTb{i}", 4 + i, dtype=BF16) for i in range(2)]
    unit = 0

    def n2_A(t):
        xa, xb_ = x1[t]
        ha, hb_ = h2b[t % 2]
        ja, jb_ = junk2s[t % 2]
        S.add("act", lambda: nc.scalar.activation(out=ja, in_=xa, func=AF.Square, accum_out=ss2[:, t:t + 1]),
              reads=[xb_], writes=[jb_, b_ss2c[t]])
        S.add("pool", lambda: nc.gpsimd.tensor_scalar(out=v2t[:, t:t + 1], in0=ss2[:, t:t + 1], scalar1=1.0 / D, scalar2=RMS_EPS,
                                                      op0=ALU.mult, op1=ALU.add), reads=[b_ss2c[t]], writes=[b_v2c[t]])
        S.add("pool", lambda: nc.gpsimd.tensor_tensor(out=rstd2[:, t:t + 1], in0=v2t[:, t:t + 1], in1=mhalf[:, 0:1], op=ALU.pow),
              reads=[b_v2c[t], b_mhalf], writes=[b_rstd2c[t]])
        S.add("dve", lambda: nc.vector.scalar_tensor_tensor(out=ha, in0=xa, scalar=rstd2[:, t:t + 1], in1=g2t, op0=ALU.mult, op1=ALU.mult),
              reads=[xb_, b_rstd2c[t], b_g2t], writes=[hb_])

    def n2_B(t):
        ha, hb_ = h2b[t % 2]
        pa, pb_ = pT2[t % 2]
        pa3 = pa.rearrange("p (k t) -> p k t", k=8)
        for kc in range(8):
            S.add("pe", lambda kc=kc: nc.tensor.transpose(pa3[:, kc, :], ha[:, kc * 128:(kc + 1) * 128], ident),
                  reads=[hb_, b_ident], writes=[pb_])
        dst = h2T3[:, :, t * 128:(t + 1) * 128]
        S.add("act", lambda: nc.scalar.copy(out=dst, in_=pa3), reads=[pb_], writes=[b_h2Tg[t // 4]])

    for t in range(NT):
        xa, xb_ = x1[t]
        for dh in range(2):
            pa, pb_ = wo_ps[unit % 4]
            unit += 1
            for kc in range(8):
                S.add("pe", lambda pa=pa, kc=kc, t=t, dh=dh: nc.tensor.matmul(
                    pa, lhsT=mT3[:, kc, t * 128:(t + 1) * 128], rhs=wout3[:, kc, dh * 512:(dh + 1) * 512],
                    start=(kc == 0), stop=(kc == 7)), reads=[b_mT, b_wout0 if kc < 4 else b_wout1], writes=[pb_])
            if t < NXS:
                src, srcb = xs[t]
            else:
                src, srcb = xa, xb_
            S.add("dve", lambda pa=pa, xa=xa, dh=dh, src=src: nc.vector.scalar_tensor_tensor(
                out=xa[:, dh * 512:(dh + 1) * 512], in0=pa, scalar=0.5, in1=src[:, dh * 512:(dh + 1) * 512],
                op0=ALU.mult, op1=ALU.add), reads=[pb_, srcb], writes=[xb_])
        if t >= NXS and t + 3 < NT:
            x_reload(t + 3, after=xb_)
        if t >= 1:
            n2_A(t - 1)
        if t >= 2:
            n2_B(t - 2)
    def n2_tail():
        n2_A(NT - 1)
        n2_B(NT - 2)
        n2_B(NT - 1)

    ffw.append((sb("ff1_1", R_C + 16384, 8192), sb("ff2_1", R_C + 16384 + 8192, 8192)))
    ffw.append((sb("ff1_2", R_C + 2 * 16384, 8192), sb("ff2_2", R_C + 2 * 16384 + 8192, 8192)))
    load_ff(1)
    load_ff(2)
    ck(6, [("x1_%d" % t, x1[t][0], x1[t][1]) for t in range(NT)] + [("h2T", h2T, b_h2Tg[3])])
    o = R_B
    actT = []
    for i in range(2):
        actT.append(sb(f"actT{i}", o, 4096)); o += 4096
    sqf = []
    for i in range(2):
        sqf.append(sb(f"sqf{i}", o, 2048, F32)); o += 2048
    f1_ps = [pbank(f"f1p{i}", i) for i in range(3)]
    f2_ps = [pbank(f"f2p{i}", b) for i, b in enumerate((3, 6, 7))]
    g_out = SemGroup("out", "all")
    u1 = 0
    u2 = 0

    def ff1(part, tg, ui):
        nonlocal u1
        (w1a, w1b), _ = ffw[part % 3]
        w13 = w1a.rearrange("p (k f) -> p k f", k=8)
        aa, ab_ = actT[ui % 2]
        aa3 = aa.rearrange("p (c t) -> p c t", c=4)
        for fc in range(4):
            pa, pb_ = f1_ps[u1 % 3]
            qa, qb_ = sqf[u1 % 2]
            u1 += 1
            for kc in range(8):
                S.add("pe", lambda pa=pa, w13=w13, fc=fc, kc=kc, tg=tg: nc.tensor.matmul(
                    pa, lhsT=w13[:, kc, fc * 128:(fc + 1) * 128], rhs=h2T3[:, kc, tg * 512:(tg + 1) * 512],
                    start=(kc == 0), stop=(kc == 7)), reads=[w1b, b_h2Tg[tg]], writes=[pb_])
            S.add("act", lambda qa=qa, pa=pa: nc.scalar.activation(out=qa, in_=pa, func=AF.Square), reads=[pb_], writes=[qb_])
            S.add("dve", lambda aa3=aa3, fc=fc, pa=pa, qa=qa: nc.vector.scalar_tensor_tensor(
                out=aa3[:, fc, :], in0=pa, scalar=0.0, in1=qa, op0=ALU.is_gt, op1=ALU.mult),
                reads=[pb_, qb_], writes=[ab_])

    def ff2(part, tg, ui):
        nonlocal u2
        _, (w2a, w2b) = ffw[part % 3]
        w23 = w2a.rearrange("p (k f) -> p k f", k=4)
        aa, ab_ = actT[ui % 2]
        aa3 = aa.rearrange("p (c t) -> p c t", c=4)
        for tt_ in range(4):
            t = tg * 4 + tt_
            xa, xb_ = x1[t]
            for dh in range(2):
                pa, pb_ = f2_ps[u2 % 3]
                u2 += 1
                for fc in range(4):
                    S.add("pe", lambda pa=pa, aa3=aa3, w23=w23, fc=fc, tt_=tt_, dh=dh: nc.tensor.matmul(
                        pa, lhsT=aa3[:, fc, tt_ * 128:(tt_ + 1) * 128], rhs=w23[:, fc, dh * 512:(dh + 1) * 512],
                        start=(fc == 0), stop=(fc == 3)), reads=[ab_, w2b], writes=[pb_])
                S.add("dve", lambda pa=pa, xa=xa, dh=dh: nc.vector.tensor_tensor(
                    out=xa[:, dh * 512:(dh + 1) * 512], in0=pa, in1=xa[:, dh * 512:(dh + 1) * 512], op=ALU.add),
                    reads=[pb_, xb_], writes=[xb_])
            if part == 7:
                S.add("sp", lambda xa=xa, t=t: nc.sync.dma_start(out=out_d[t * 128:(t + 1) * 128, :], in_=xa),
                      reads=[xb_], dma=g_out)

    funits = [(p, tg) for p in range(8) for tg in range(4)]
    ff1(*funits[0], 0)
    for ui, (p, tg) in enumerate(funits):
        if ui + 1 < len(funits):
            ff1(*funits[ui + 1], ui + 1)
        ff2(p, tg, ui)
        if ui == 1:
            n2_tail()
        if tg == 3 and p + 3 < 8:
            load_ff(p + 3)

    S.frozen = False if stop is None else S.frozen
    S.emit([g_out] if stop is None else list(SemGroup.all_groups))
    return nc, S


def _bias_tables(rpb, half):
    def pattern(il, o):
        i = 16 * half + il
        j = i + o
        tab = np.full((8, 128, 128), NEG, dtype=np.float32)
        if j < 0 or j > 31:
            return tab
        a = np.arange(2)[:, None]; kc = np.arange(64)[None, :]
        kr = (2 * j + a + 0 * kc).reshape(-1)
        kcc = (0 * a + kc).reshape(-1)
        r = (2 * i + a + 0 * kc).reshape(-1)
        c = kcc.copy()
        r0 = np.clip(r - 4, 0, 56)
        cs = np.clip(c - 8, 0, 48)
        KR, R = kr[:, None], r[None, :]
        KC, C = kcc[:, None], c[None, :]
        valid = (KR >= r0[None, :]) & (KR < r0[None, :] + 8) & (KC >= cs[None, :]) & (KC < cs[None, :] + 16)
        dr = np.clip(KR - R + 7, 0, 14)
        dc = np.clip(KC - C + 15, 0, 30)
        g = rpb[:, dr, dc]
        return np.where(valid[None], g, tab)
    pats = []
    pats += [pattern(0, o) for o in range(-2, 4)]
    pats += [pattern(1, o) for o in range(-2, 3)] + [np.full((8, 128, 128), NEG, np.float32)]
    pats += [pattern(8, o) for o in range(-2, 3)]
    pats += [pattern(14, o) for o in range(-2, 3)] + [np.full((8, 128, 128), NEG, np.float32)]
    pats += [pattern(15, o) for o in range(-3, 3)]
    arr = np.stack(pats, 0)
    arr = arr.transpose(2, 0, 1, 3).reshape(128, NPAT * 8 * 128)
    return np.ascontiguousarray(arr)


_CACHE = {}


def kernel(x, norm1_g, w_in, b_gate, gmlp_ln_g, gmlp_ln_b, gmlp_w_s, gmlp_b_s,
           na_q_g, na_k_g, na_rpb, w_o_a, w_o_b, w_out, norm2_g, w_ff1, w_ff2):
    if "nc" not in _CACHE:
        _CACHE["nc"] = build_program()[0]
    nc = _CACHE["nc"]
    in_maps = make_in_maps(x, norm1_g, w_in, b_gate, gmlp_ln_g, gmlp_ln_b, gmlp_w_s, gmlp_b_s,
                           na_q_g, na_k_g, na_rpb, w_o_a, w_o_b, w_out, norm2_g, w_ff1, w_ff2)
    res = run_bass_kernel_spmd(nc, in_maps, core_ids=list(range(8)))
    out = np.empty((4, 2 * NTOK, D), np.float32)
    for core in range(8):
        b, hf = core // 2, core % 2
        out[b, hf * NTOK:(hf + 1) * NTOK] = res.results[core]["out"]
    return out


def make_in_maps(x, norm1_g, w_in, b_gate, gmlp_ln_g, gmlp_ln_b, gmlp_w_s, gmlp_b_s,
                 na_q_g, na_k_g, na_rpb, w_o_a, w_o_b, w_out, norm2_g, w_ff1, w_ff2):
    f = lambda a: np.ascontiguousarray(np.asarray(a, dtype=np.float32))
    x = f(x)
    shared = {
        "w_in": f(w_in[0]), "w_o_a": f(w_o_a[0]), "w_o_b": f(w_o_b[0]), "w_out": f(w_out[0]),
        "w_ff1": f(w_ff1[0]), "w_ff2": f(w_ff2[0]),
        "norm1_g": f(norm1_g[0]).reshape(1, D), "norm2_g": f(norm2_g[0]).reshape(1, D),
        "ln_g": f(gmlp_ln_g[0]).reshape(1, 512), "ln_b": f(gmlp_ln_b[0]).reshape(1, 512),
        "b_s": f(gmlp_b_s[0]).reshape(1, 512),
        "w_sT": f(np.transpose(np.asarray(gmlp_w_s[0]), (2, 0, 1)).reshape(128, 512)),
        "ident": np.eye(128, dtype=np.float32),
        "bones": np.kron(np.eye(2, dtype=np.float32), np.ones((64, 64), np.float32)),
    }
    qg = np.tile(np.asarray(na_q_g[0], np.float32), 2).reshape(128, 1)
    kg = np.tile(np.asarray(na_k_g[0], np.float32), 2).reshape(128, 1)
    bg = np.asarray(b_gate[0], np.float32).reshape(16, 128).T
    z64 = np.zeros((64, 1), np.float32)
    qg0 = np.concatenate([qg[:64], z64], axis=0)
    qg1 = np.concatenate([z64, qg[64:]], axis=0)
    shared["smalls"] = f(np.concatenate([qg, kg, bg, qg0, qg1], axis=1))
    tabs = [_bias_tables(np.asarray(na_rpb[0], np.float32), hf) for hf in range(2)]
    zeros = np.zeros((256, D), np.float32)
    in_maps = []
    for core in range(8):
        b, hf = core // 2, core % 2
        own = x[b, hf * NTOK:(hf + 1) * NTOK]
        before = x[b, NTOK - 256:NTOK] if hf == 1 else zeros
        after = x[b, NTOK:NTOK + 256] if hf == 0 else zeros
        m = dict(shared)
        m["x_ext"] = np.ascontiguousarray(np.concatenate([own, before, after], axis=0))
        m["btab"] = tabs[hf]
        in_maps.append(m)
    return in_maps
```

```python
import numpy as np
import concourse.bass as bass
import concourse.mybir as mybir
from concourse.bass_utils import run_bass_kernel_spmd

F32 = mybir.dt.float32
BF16 = mybir.dt.bfloat16
AF = mybir.ActivationFunctionType
ALU = mybir.AluOpType

D = 1024
NTOK = 2048
NT = 16
NSLOT = 20
RMS_EPS = 1e-6
LN_EPS = 1e-5
NEG = -30000.0
DBG_MP = 4
NPAT = 29

C_U, C_V, C_Q, C_K, C_VA, C_G0, C_G1 = 0, 512, 1024, 1536, 2048, 2560, 3584


def _compact(ops):
    out, last = [], {}
    for p in ops:
        if p.is_dma:
            out.append(p)
        elif p.eng not in last or last[p.eng].idx < p.idx:
            last[p.eng] = p
    return out + list(last.values())


class Buf:
    registry = {"sb": [], "ps": []}

    def __init__(self, name, space, start, size, parent=None):
        self.name, self.space, self.start, self.size = name, space, start, size
        self.last_w = None
        self.readers = []
        self.inherit = list(parent.inherit) if parent is not None else []
        self.dead = False
        for o in Buf.registry[space]:
            if o.start < start + size and start < o.start + o.size:
                if o.last_w is not None:
                    self.inherit.append(o.last_w)
                self.inherit.extend(o.readers)
                self.inherit.extend(o.inherit)
                o.dead = True
        self.inherit = _compact(self.inherit)
        Buf.registry[space].append(self)


def join_region(space, start, size):
    j = Buf("join", space, start, size)
    j.dead = True
    Buf.registry[space] = [o for o in Buf.registry[space] if o is not j]
    return j


class SemGroup:
    all_groups = []

    def __init__(self, name, kind):
        self.name, self.kind = name, kind
        self.n = 0
        self.sem = None
        SemGroup.all_groups.append(self)


class Op:
    __slots__ = ("eng", "fn", "is_dma", "group", "gidx", "waits", "need_inc", "count", "idx")


class Sched:
    ENGS = ("pe", "act", "dve", "pool", "sp")

    def __init__(self, nc):
        self.nc = nc
        self.ops = []
        self.eng_obj = {"pe": nc.tensor, "act": nc.scalar, "dve": nc.vector, "pool": nc.gpsimd, "sp": nc.sync}

    def add(self, eng, fn, reads=(), writes=(), dma=None):
        if getattr(self, "frozen", False):
            return None
        op = Op()
        op.eng, op.fn, op.is_dma, op.group = eng, fn, dma is not None, dma
        op.idx = len(self.ops)
        op.need_inc = False
        op.count = None
        if dma is not None:
            assert getattr(dma, "eng", eng) == eng, "a DMA semaphore group must be fed by a single queue"
            dma.eng = eng
            op.gidx = dma.n
            dma.n += 1
        deps = []
        for b in reads:
            assert not b.dead, f"read of dead buf {b.name}"
            if b.last_w is not None:
                deps.append(b.last_w)
            deps.extend(b.inherit) if b.last_w is None else None
        for b in writes:
            assert not b.dead, f"write of dead buf {b.name}"
            if b.last_w is not None:
                deps.append(b.last_w)
            deps.extend(b.readers)
            deps.extend(b.inherit)
        w = []
        seen = set()
        if dma is not None and dma.kind == "all":
            assert all(not (p.is_dma and p.group is dma) for p in deps), "intra-'all'-group dependency"
        for p in deps:
            if p.idx in seen or p is op:
                continue
            seen.add(p.idx)
            if (not p.is_dma) and (not op.is_dma) and p.eng == "pe" and op.eng == "pe":
                continue
            w.append(p)
        op.waits = w
        for b in reads:
            b.readers.append(op)
            if len(b.readers) > 1:
                b.readers = _compact(b.readers)
        for b in writes:
            b.last_w = op
            b.readers = []
            b.inherit = []
        self.ops.append(op)
        return op

    def emit(self, final_groups):
        nc = self.nc
        for op in self.ops:
            for p in op.waits:
                p.need_inc = True
        sems = {e: nc.alloc_semaphore("s_" + e) for e in ("pe", "act", "dve", "pool")}
        cnt = {e: 0 for e in sems}
        waited = {e: {} for e in self.ENGS}
        nwait = 0
        for op in self.ops:
            eo = self.eng_obj[op.eng]
            need = {}
            for p in op.waits:
                if p.is_dma:
                    g = p.group
                    sem = g.sem
                    val = 16 * (p.gidx + 1) if g.kind == "slot" else 16 * g.n
                else:
                    sem = sems[p.eng]
                    val = p.count
                key = id(sem)
                if key not in need or need[key][1] < val:
                    need[key] = (sem, val)
            for key, (sem, val) in need.items():
                if waited[op.eng].get(key, 0) >= val:
                    continue
                waited[op.eng][key] = val
                eo.wait_ge(sem, val)
                nwait += 1
            inst = op.fn()
            if op.is_dma:
                g = op.group
                if g.sem is None:
                    g.sem = nc.alloc_semaphore("d_" + g.name)
                inst.then_inc(g.sem, 16)
            elif op.need_inc:
                cnt[op.eng] += 1
                op.count = cnt[op.eng]
                inst.then_inc(sems[op.eng], 1)
        for g in final_groups:
            if g.sem is not None:
                nc.sync.wait_ge(g.sem, 16 * g.n)
        self.stats = dict(n_ops=len(self.ops), n_waits=nwait, counts=dict(cnt))


def build_program(stop=None):
    Buf.registry = {"sb": [], "ps": []}
    SemGroup.all_groups = []
    nc = bass.Bass("TRN2", target_bir_lowering=False)
    S = Sched(nc)
    g_dbg = SemGroup("dbg", "all")

    def ck(k, items):
        if stop != k or getattr(S, "frozen", False):
            return
        for name, ap, buf in items:
            dt_ = nc.dram_tensor("dbg_" + name, list(ap.shape), F32, kind="ExternalOutput").ap()
            S.add("pool", lambda dt_=dt_, ap=ap: nc.gpsimd.dma_start(out=dt_, in_=ap, max_dma_last_dim=2048), reads=[buf], dma=g_dbg)
        S.frozen = True

    def din(name, shape):
        return nc.dram_tensor(name, list(shape), F32, kind="ExternalInput").ap()

    x_ext = din("x_ext", [NSLOT * 128, D])
    w_in = din("w_in", [D, 4608])
    w_o_a = din("w_o_a", [512, D])
    w_o_b = din("w_o_b", [512, D])
    w_out = din("w_out", [D, D])
    w_ff1 = din("w_ff1", [D, 4096])
    w_ff2 = din("w_ff2", [4096, D])
    g1_d = din("norm1_g", [1, D])
    g2_d = din("norm2_g", [1, D])
    lng_d = din("ln_g", [1, 512])
    lnb_d = din("ln_b", [1, 512])
    bs_d = din("b_s", [1, 512])
    wst_d = din("w_sT", [128, 512])
    smalls_d = din("smalls", [128, 20])
    ident_d = din("ident", [128, 128])
    bones_d = din("bones", [128, 128])
    btab_d = din("btab", [128, NPAT * 8 * 128])
    out_d = nc.dram_tensor("out", [NTOK, D], F32, kind="ExternalOutput").ap()

    ARENA_ELEMS = 105472
    arena = nc.alloc_sbuf_tensor("arena", [128, ARENA_ELEMS], BF16)
    psum = nc.alloc_psum_tensor("psum", [128, 4096], F32)

    def sb(name, off, nbytes, dtype=BF16, parent=None):
        assert off % 4 == 0 and off + nbytes <= ARENA_ELEMS * 2, (name, off, nbytes)
        ap = arena[:, off // 2:(off + nbytes) // 2]
        if dtype == F32:
            ap = ap.bitcast(F32)
        return ap, Buf(name, "sb", off, nbytes, parent=parent)

    def pbank(name, bank, nbanks=1, dtype=F32, boff=0, nbytes=None):
        start = bank * 2048 + boff
        nb = nbanks * 2048 - boff if nbytes is None else nbytes
        ap = psum[:, start // 4:(start + nb) // 4]
        if dtype == BF16:
            ap = ap.bitcast(BF16)
        return ap, Buf(name, "ps", start, nb)

    R_A = 0
    R_B = 65536
    R_C = 98304
    R_D = 163840
    R_E = 208896
    END = ARENA_ELEMS * 2

    g_const = SemGroup("const_sw", "all")
    g_const_h = SemGroup("const_hw", "all")
    ident, b_ident = sb("ident", R_E, 256)
    bones, b_bones = sb("bones", R_E + 256, 256)
    smalls, b_smalls = sb("smalls", R_E + 1296, 80, F32)
    hbg, b_hbg = sb("hbg", R_E + 584, 64, F32)
    ss1, b_ss1 = sb("ss1", R_E + 648, 80, F32)
    std1, b_std1 = sb("std1", R_E + 728, 80, F32)
    rstd1, b_rstd1 = sb("rstd1", R_E + 808, 80, F32)
    ss2, b_ss2 = sb("ss2", R_E + 888, 64, F32)
    rstd2, b_rstd2 = sb("rstd2", R_E + 952, 64, F32)
    mhalf, b_mhalf = sb("mhalf", R_E + 1016, 64, F32)
    lnst, b_lnst = sb("lnst", R_E + 1080, 24 * 2, F32)
    lnmv, b_lnmv = sb("lnmv", R_E + 1128, 8 * 2, F32)
    lnr, b_lnr = sb("lnr", R_E + 1144, 4 * 2, F32)
    rden, b_rden_ = sb("rden", R_E + 1152, 32 * 2, F32)
    v2t, b_v2t = sb("v2t", R_E + 1216, 64, F32)
    epsc, b_epsc = sb("epsc", R_E + 1280, 16, F32)
    assert R_E + 1376 <= END
    S.add("pool", lambda: nc.gpsimd.memset(epsc[:, 0:1], RMS_EPS), writes=[b_epsc])
    S.add("pool", lambda: nc.gpsimd.memset(epsc[:, 1:2], 64 * RMS_EPS), writes=[b_epsc])

    S.add("pool", lambda: nc.gpsimd.dma_start(out=ident, in_=ident_d), writes=[b_ident], dma=g_const)
    S.add("pool", lambda: nc.gpsimd.dma_start(out=bones, in_=bones_d), writes=[b_bones], dma=g_const)
    S.add("sp", lambda: nc.sync.dma_start(out=smalls, in_=smalls_d), writes=[b_smalls], dma=g_const_h)
    S.add("pool", lambda: nc.gpsimd.memset(mhalf, -0.5), writes=[b_mhalf])
    S.add("dve", lambda: nc.vector.tensor_scalar(out=hbg, in0=smalls[:, 2:18], scalar1=0.5, scalar2=None,
                                                  op0=ALU.mult), reads=[b_smalls], writes=[b_hbg])

    hT = arena[:, R_A // 2:(R_A + 32768) // 2]
    b_hTg = [Buf(f"hTg{g}", "sb", R_A + g * 8192, 8192) for g in range(4)]
    hT3 = hT.rearrange("p (k t) -> p k t", k=8)
    hTh, b_hTh = sb("hTh", R_D, 8192)
    hTh3 = hTh.rearrange("p (k t) -> p k t", k=8)
    b_hT_g = [Buf(f"hT_g{g}", "sb", R_A + 0, 0) for g in range(0)]

    o = R_D + 8192
    NXR = 4
    xr = []
    for i in range(NXR):
        xr.append(sb(f"xr{i}", o, 4096, F32)); o += 4096
    hb = []
    for i in range(2):
        hb.append(sb(f"hb{i}", o, 2048)); o += 2048
    g1t, b_g1t = sb("g1t", o, 4096, F32); o += 4096
    junks = []
    for i in range(2):
        junks.append(sb(f"junk{i}", o, 2048)); o += 2048
    wrKa, b_wrK = sb("wrK", o, 8192); o += 8192
    wrK3 = wrKa.rearrange("p (k f) -> p k f", k=8)
    assert o <= R_E
    g_xr = [SemGroup(f"xr{i}", "slot") for i in range(NXR)]
    S.add("sp", lambda: nc.sync.dma_start(out=g1t, in_=g1_d.partition_broadcast(128)), writes=[b_g1t], dma=g_const_h)

    pT = [pbank(f"pT{i}", i, dtype=BF16) for i in range(2)]

    hT_gb = [Buf(f"hTg{g}", "sb", R_A + g * 1024, 1024) for g in range(0)]

    b_ss1c = [Buf(f"ss1_{t}", "sb", R_E + 648 + 4 * t, 4) for t in range(NSLOT)]
    b_std1c = [Buf(f"std1_{t}", "sb", R_E + 728 + 4 * t, 4) for t in range(NSLOT)]
    b_rstd1c = [Buf(f"rstd1_{t}", "sb", R_E + 808 + 4 * t, 4) for t in range(NSLOT)]
    p0_order = list(range(NT, NSLOT)) + list(range(NT))

    def p0_A(ti):
        t = p0_order[ti]
        xa, xb_ = xr[ti % NXR]
        ha, hb_ = hb[ti % 2]
        ja, jb_ = junks[ti % 2]
        S.add("sp", lambda: nc.sync.dma_start(out=xa, in_=x_ext[t * 128:(t + 1) * 128, :]), writes=[xb_], dma=g_xr[ti % NXR])
        S.add("act", lambda: nc.scalar.activation(out=ja, in_=xa, func=AF.Square, accum_out=ss1[:, t:t + 1]),
              reads=[xb_], writes=[jb_, b_ss1c[t]])
        S.add("pool", lambda: nc.gpsimd.tensor_scalar(out=std1[:, t:t + 1], in0=ss1[:, t:t + 1], scalar1=1.0 / D, scalar2=RMS_EPS,
                                                      op0=ALU.mult, op1=ALU.add), reads=[b_ss1c[t]], writes=[b_std1c[t]])
        S.add("pool", lambda: nc.gpsimd.tensor_tensor(out=rstd1[:, t:t + 1], in0=std1[:, t:t + 1], in1=mhalf[:, 0:1], op=ALU.pow),
              reads=[b_std1c[t], b_mhalf], writes=[b_rstd1c[t]])

    def p0_A2(ti):
        t = p0_order[ti]
        xa, xb_ = xr[ti % NXR]
        ha, hb_ = hb[ti % 2]
        S.add("dve", lambda: nc.vector.scalar_tensor_tensor(out=ha, in0=xa, scalar=rstd1[:, t:t + 1], in1=g1t, op0=ALU.mult, op1=ALU.mult),
              reads=[xb_, b_rstd1c[t], b_g1t], writes=[hb_])

    def p0_B(ti):
        t = p0_order[ti]
        ha, hb_ = hb[ti % 2]
        pa, pb_ = pT[ti % 2]
        pa3 = pa.rearrange("p (k t) -> p k t", k=8)
        for kc in range(8):
            S.add("pe", lambda kc=kc: nc.tensor.transpose(pa3[:, kc, :], ha[:, kc * 128:(kc + 1) * 128], ident),
                  reads=[hb_, b_ident], writes=[pb_])
        if t < NT:
            dst, dbuf = hT3[:, :, t * 128:(t + 1) * 128], b_hTg[t // 4]
        else:
            dst, dbuf = hTh3[:, :, (t - NT) * 128:(t - NT + 1) * 128], b_hTh
        S.add("dve", lambda: nc.vector.tensor_copy(out=dst, in_=pa3), reads=[pb_], writes=[dbuf])

    qm, b_qm = sb("qm", R_B, 32768)
    qm4 = qm.rearrange("p (h c t) -> p h c t", h=2, c=4)
    kT, b_kT = sb("kT", R_C, 20480)
    kT3 = kT.rearrange("p (c t) -> p c t", c=4)
    Va, b_Va = sb("Vaug", R_C + 20480, 20800)
    Va4 = Va.rearrange("p (s h d) -> p s h d", s=NSLOT, h=8)
    o = R_A + 32768
    sq = []
    for i in range(2):
        sq.append(sb(f"sq{i}", o, 1024)); o += 1024
    stdb = []
    for i in range(2):
        stdb.append(sb(f"std{i}", o, 2048, F32)); o += 2048
    rsb = []
    for i in range(2):
        rsb.append(sb(f"rs{i}", o, 2048, F32)); o += 2048
    assert o <= R_A + 49152
    wkey = {}
    g_wrK = SemGroup("wrK", "slot")
    g_wv = SemGroup("wv", "slot")
    w_in3 = w_in.rearrange("(k p) f -> p k f", p=128)

    S.add("pool", lambda: nc.gpsimd.memset(Va4[:, :, :, 64:65], 1.0), writes=[b_Va])

    mm_ps = [pbank(f"mm{i}", 2 + i) for i in range(4)]
    ss_ps = [pbank(f"ssp{i}", 6 + i) for i in range(2)]

    def load_w256(key, col0):
        (wa, wb), grp = wkey[key]
        wa3 = wa.rearrange("p (k f) -> p k f", k=8)
        S.add("pool", lambda: nc.gpsimd.dma_start(out=wa3, in_=w_in3[:, :, col0:col0 + 256]),
              writes=[wb], dma=grp)
        return wa3, wb

    def rhs_group(tg):
        if tg < 4:
            return (lambda kc: hT3[:, kc, tg * 512:(tg + 1) * 512]), b_hTg[tg]
        return (lambda kc: hTh3[:, kc, :]), b_hTh

    units_qk = []
    for tg in [4, 0, 1, 2, 3]:
        for cp in range(2):
            for cc in range(2):
                units_qk.append(("k", C_K, cp, cc, tg))
    for cp in range(2):
        for cc in range(2):
            for tg in [0, 1, 2, 3]:
                units_qk.append(("q", C_Q, cp, cc, tg))
    wcache = {}

    def qk_mm(u):
        kind, cbase, cp, cc, tg = units_qk[u]
        key = (kind, cp)
        if key not in wcache:
            wcache[key] = load_w256(key, cbase + cp * 256)
        wa3, wb = wcache[key]
        rf, rbuf = rhs_group(tg)
        pa, pb_ = mm_ps[u % 4]
        qa, qb_ = sq[u % 2]
        for kc in range(8):
            S.add("pe", lambda kc=kc: nc.tensor.matmul(pa, lhsT=wa3[:, kc, cc * 128:(cc + 1) * 128], rhs=rf(kc),
                                                       start=(kc == 0), stop=(kc == 7)), reads=[wb, rbuf], writes=[pb_])
        S.add("act", lambda: nc.scalar.activation(out=qa, in_=pa, func=AF.Square), reads=[pb_], writes=[qb_])

    def qk_fin(u):
        kind, cbase, cp, cc, tg = units_qk[u]
        c = cp * 2 + cc
        pa, pb_ = mm_ps[u % 4]
        sa, sb_ = ss_ps[u % 2]
        qa, qb_ = sq[u % 2]
        sta, stb_ = stdb[u % 2]
        ra, rb_ = rsb[u % 2]
        S.add("pe", lambda: nc.tensor.matmul(sa, lhsT=bones, rhs=qa, start=True, stop=True), reads=[qb_, b_bones], writes=[sb_])
        if kind == "k":
            S.add("act", lambda: nc.scalar.activation(out=sta, in_=sa, func=AF.Ln, scale=1.0 / 64, bias=RMS_EPS), reads=[sb_], writes=[stb_])
        else:
            S.add("act", lambda: nc.scalar.activation(out=sta, in_=sa, func=AF.Ln, scale=1.0, bias=64 * RMS_EPS), reads=[sb_], writes=[stb_])
        S.add("act", lambda: nc.scalar.activation(out=ra, in_=sta, func=AF.Exp, scale=-0.5), reads=[stb_], writes=[rb_])
        if kind == "k":
            tok0 = tg * 512
            S.add("dve", lambda: nc.vector.scalar_tensor_tensor(out=kT3[:, c, tok0:tok0 + 512], in0=pa, scalar=smalls[:, 1:2], in1=ra,
                                                                op0=ALU.mult, op1=ALU.mult), reads=[pb_, rb_, b_smalls], writes=[b_kT])
        else:
            for hp in range(2):
                S.add("dve", lambda hp=hp: nc.vector.scalar_tensor_tensor(
                    out=qm4[:, hp, c, tg * 512:(tg + 1) * 512], in0=pa, scalar=smalls[:, 18 + hp:19 + hp],
                    in1=ra, op0=ALU.mult, op1=ALU.mult), reads=[pb_, rb_, b_smalls], writes=[b_qm])

    qk_next = [0]

    def qk_step():
        u = qk_next[0]
        if u < len(units_qk):
            qk_mm(u)
        if u >= 1:
            qk_fin(u - 1)
        qk_next[0] = u + 1

    S.add("pool", lambda: nc.gpsimd.dma_start(out=wrK3, in_=w_in3[:, :, C_K:C_K + 512]), writes=[b_wrK], dma=g_wrK)
    wcache[("k", 0)] = (wrK3[:, :, 0:256], b_wrK)
    wcache[("k", 1)] = (wrK3[:, :, 256:512], b_wrK)
    p0_A(0)
    p0_A(1)
    p0_A2(0)
    for ti in range(NSLOT):
        if ti + 2 < NSLOT:
            p0_A(ti + 2)
        if ti + 1 < NSLOT:
            p0_A2(ti + 1)
        p0_B(ti)
        if ti >= 4:
            qk_step()
    ck(0, [("hT", hT, b_hTg[3]), ("hTh", hTh, b_hTh)])
    wkey[("q", 0)] = (sb("wrQ0", R_D + 8192, 4096), SemGroup("wrQ0", "slot"))
    wkey[("q", 1)] = (sb("wrQ1", R_D + 12288, 4096), SemGroup("wrQ1", "slot"))
    wv, b_wv = sb("wv", R_D + 16384, 8192)
    while qk_next[0] <= len(units_qk):
        qk_step()
    unit = len(units_qk)

    wv3 = wv.rearrange("p (k f) -> p k f", k=8)
    S.add("pool", lambda: nc.gpsimd.dma_start(out=wv3, in_=w_in3[:, :, C_VA:C_VA + 512]), writes=[b_wv], dma=g_wv)
    for j in range(NSLOT):
        pa, pb_ = mm_ps[unit % 4]
        unit += 1
        if j < NT:
            lf, lbuf = (lambda kc, j=j: hT3[:, kc, j * 128:(j + 1) * 128]), b_hTg[j // 4]
        else:
            lf, lbuf = (lambda kc, j=j: hTh3[:, kc, (j - NT) * 128:(j - NT + 1) * 128]), b_hTh
        for kc in range(8):
            S.add("pe", lambda pa=pa, lf=lf, kc=kc: nc.tensor.matmul(pa, lhsT=lf(kc), rhs=wv3[:, kc, :],
                                                                     start=(kc == 0), stop=(kc == 7)),
                  reads=[b_wv, lbuf], writes=[pb_])
        src = pa.rearrange("p (h d) -> p h d", h=8)
        dst = Va4[:, j, :, 0:64]
        if j % 2 == 0:
            S.add("act", lambda dst=dst, src=src: nc.scalar.copy(out=dst, in_=src), reads=[pb_], writes=[b_Va])
        else:
            S.add("dve", lambda dst=dst, src=src: nc.vector.tensor_copy(out=dst, in_=src), reads=[pb_], writes=[b_Va])

    ck(1, [("kT", kT, b_kT), ("qm", qm, b_qm), ("Va", Va, b_Va)])
    tin, b_tin = sb("tab_int", R_C + 41472, 10240)
    tin4 = tin.rearrange("p (a h q) -> p a h q", a=5, h=8)
    tsa, b_tsa = sb("tab_sa", R_C + 51712, 12288)
    tsa4 = tsa.rearrange("p (a h q) -> p a h q", a=6, h=8)
    assert R_C + 51712 + 12288 <= R_D
    o = R_D
    tsb, b_tsb = sb("tab_sb", R_A + 49152, 12288)
    tsb4 = tsb.rearrange("p (a h q) -> p a h q", a=6, h=8)
    PTb = []
    for i in range(3):
        PTb.append(sb(f"PT{i}", o, 3072)); o += 3072
    Sbb = []
    for i in range(2):
        Sbb.append(sb(f"Sb{i}", o, 6144, F32)); o += 6144
    ybt = []
    for i in range(2):
        ybt.append(sb(f"ybt{i}", o, 1024)); o += 1024
    assert o <= R_E
    y_bT, b_ybT = sb("y_bT", R_A + 32768, 16384)
    y_bT3 = y_bT.rearrange("p (c t) -> p c t", c=4)
    g_tin = SemGroup("tin", "slot")
    g_tsa = SemGroup("tsa", "slot")
    g_tsb = SemGroup("tsb", "slot")
    PW = 8 * 128

    def load_tab(dst4, dbuf, grp, p0, npat, eng="pool"):
        S.add("pool", lambda: nc.gpsimd.dma_start(
            out=dst4[:, 0:npat, :, :].rearrange("p a h q -> p a (h q)"),
            in_=btab_d[:, p0 * PW:(p0 + npat) * PW].rearrange("p (a x) -> p a x", a=npat)),
            writes=[dbuf], dma=grp)

    load_tab(tsa4, b_tsa, g_tsa, 0, 6)
    load_tab(tin4, b_tin, g_tin, 12, 5)
    load_tab(tsb4, b_tsb, g_tsb, 6, 6)

    S_ps = [pbank(f"S{i}", 3 * i, nbytes=6144) for i in range(2)]
    _pv = pbank("PV", 6, nbytes=4 * 65 * 4)
    PV_ps = [_pv, _pv]
    yTp_a, yTp_b = pbank("yTp", 7, dtype=BF16, nbytes=1024)
    yTp3 = yTp_a.rearrange("p (c t) -> p c t", c=4)

    def kst(s):
        if 2 <= s < 18:
            return s - 2
        return 16 + s if s < 2 else s

    def na_unit_info(il):
        if il == 0:
            offs = list(range(-2, 4)); tab = (tsa4, b_tsa)
        elif il == 1:
            offs = list(range(-2, 3)); tab = (tsb4, b_tsb)
        elif il == 14:
            offs = list(range(-2, 3)); tab = (tsa4, b_tsa)
        elif il == 15:
            offs = list(range(-3, 3)); tab = (tsb4, b_tsb)
        else:
            offs = list(range(-2, 3)); tab = (tin4, b_tin)
        slots = [il + 2 + o_ for o_ in offs]
        return slots, tab

    def na_S(il, c, u):
        slots, (t4, tbuf) = na_unit_info(il)
        ns = len(slots)
        sa, sb_ = S_ps[u % 2]
        pa, pb_ = PTb[u % 3]
        for n, s in enumerate(slots):
            j = kst(s)
            S.add("pe", lambda n=n, j=j: nc.tensor.matmul(
                sa[:, n * 256:(n + 1) * 256], lhsT=kT3[:, c, j * 128:(j + 1) * 128],
                rhs=qm4[:, :, c, il * 128:(il + 1) * 128], start=True, stop=False),
                reads=[b_kT, b_qm], writes=[sb_])
            S.add("pe", lambda n=n: nc.tensor.matmul(
                sa[:, n * 256:(n + 1) * 256], lhsT=ident,
                rhs=t4[:, n, 2 * c:2 * c + 2, :], start=False, stop=True),
                reads=[tbuf, b_ident], writes=[sb_])
        S.add("act", lambda: nc.scalar.activation(out=pa[:, 0:ns * 256], in_=sa[:, 0:ns * 256], func=AF.Exp),
              reads=[sb_], writes=[pb_])

    def na_PV(il, c, u):
        slots, _ = na_unit_info(il)
        pa, pb_ = PTb[u % 3]
        for hp in range(2):
            h = 2 * c + hp
            va, vb_ = PV_ps[h // 4]
            va3 = va.rearrange("p (h d) -> p h d", h=4)
            for n, s in enumerate(slots):
                j = kst(s)
                S.add("pe", lambda va3=va3, n=n, j=j, h=h, hp=hp, last=(n == len(slots) - 1): nc.tensor.matmul(
                    va3[:, h % 4, :], lhsT=pa[:, n * 256 + hp * 128:n * 256 + hp * 128 + 128], rhs=Va4[:, j, h, :],
                    start=(n == 0), stop=last), reads=[pb_, b_Va], writes=[vb_])
            if h % 4 == 3:
                hb4 = h // 4
                ya, yb_ = ybt[il % 2]
                ya3 = ya.rearrange("p (h d) -> p h d", h=8)
                rd = rden[:, (il % 2) * 8 + hb4 * 4:(il % 2) * 8 + hb4 * 4 + 4]
                S.add("dve", lambda rd=rd, va3=va3: nc.vector.reciprocal(out=rd, in_=va3[:, :, 64]),
                      reads=[vb_], writes=[b_rden_])
                S.add("dve", lambda ya3=ya3, va3=va3, rd=rd, hb4=hb4: nc.vector.tensor_tensor(
                    out=ya3[:, hb4 * 4:(hb4 + 1) * 4, :], in0=va3[:, :, 0:64],
                    in1=rd.unsqueeze(2).to_broadcast([128, 4, 64]), op=ALU.mult),
                    reads=[vb_, b_rden_], writes=[yb_])

    def na_T(il):
        ya, yb_ = ybt[il % 2]
        for c in range(4):
            S.add("pe", lambda c=c: nc.tensor.transpose(yTp3[:, c, :], ya[:, c * 128:(c + 1) * 128], ident),
                  reads=[yb_, b_ident], writes=[yTp_b])
        dst = y_bT3[:, :, il * 128:(il + 1) * 128]
        S.add("dve", lambda dst=dst: nc.vector.tensor_copy(out=dst, in_=yTp3), reads=[yTp_b], writes=[b_ybT])

    units = [(il, c) for il in range(NT) for c in range(4)]
    LAG = 2

    def na_issue_S(u):
        il2, c2 = units[u]
        if il2 == 3 and c2 == 0:
            load_tab(tsa4, b_tsa, g_tsa, 17, 6)
            load_tab(tsb4, b_tsb, g_tsb, 23, 6)
        na_S(il2, c2, u)

    for u in range(min(LAG, len(units))):
        na_issue_S(u)
    for u, (il, c) in enumerate(units):
        if u + LAG < len(units):
            na_issue_S(u + LAG)
        na_PV(il, c, u)
        if c == 0 and il > 0:
            na_T(il - 1)
    na_T(NT - 1)

    ck(2, [("ybT", y_bT, b_ybT)])
    uT = arena[:, (R_A + 49152) // 2:(R_A + 65536) // 2]
    j_uT = join_region("sb", R_A + 49152, 16384)
    b_uTg = [Buf(f"uTg{g}", "sb", R_A + 49152 + g * 4096, 4096, parent=j_uT) for g in range(4)]
    uT3 = uT.rearrange("p (c t) -> p c t", c=4)
    gtmp = []
    o = R_D
    NG = 5
    for i in range(NG):
        gtmp.append(sb(f"gtmp{i}", o, 2048, F32)); o += 2048
    o = R_D + 12288
    wst, b_wst = sb("wst", o, 1024); o += 1024
    bst, b_bst = sb("bst", o, 2048, F32); o += 2048
    lngt, b_lngt = sb("lngt", o, 2048, F32); o += 2048
    lnbt, b_lnbt = sb("lnbt", o, 2048, F32); o += 2048
    vtmp = []
    for i in range(3):
        vtmp.append(sb(f"vtmp{i}", o, 2048, F32)); o += 2048
    lnsm = []
    for i in range(3):
        lnsm.append(sb(f"lnsm{i}", o, 48, F32)); o += 48
    assert o <= R_D + 28672, o
    o = R_D + 28672
    wr2 = []
    for i in range(2):
        wr2.append(sb(f"wrb{i}", o, 4096)); o += 4096
    wvv, b_wvv = sb("wvv", o, 8192); o += 8192
    assert o <= R_E, o
    vn = arena[:, R_B // 2:(R_B + 16384) // 2]
    j_vn = join_region("sb", R_B, 16384)
    b_vnj = [Buf(f"vn{j}", "sb", R_B + j * 1024, 1024, parent=j_vn) for j in range(NT)]
    vn3 = vn.rearrange("p (j f) -> p j f", j=NT)
    g_c2 = SemGroup("const2", "all")
    S.add("sp", lambda: nc.sync.dma_start(out=lngt, in_=lng_d.partition_broadcast(128)), writes=[b_lngt], dma=g_c2)
    S.add("sp", lambda: nc.sync.dma_start(out=lnbt, in_=lnb_d.partition_broadcast(128)), writes=[b_lnbt], dma=g_c2)
    S.add("sp", lambda: nc.sync.dma_start(out=bst, in_=bs_d.partition_broadcast(128)), writes=[b_bst], dma=g_c2)
    g_c2p = SemGroup("const2_sw", "all")
    S.add("pool", lambda: nc.gpsimd.dma_start(out=wst, in_=wst_d), writes=[b_wst], dma=g_c2p)
    g_wr2 = [SemGroup(f"wrb{i}", "slot") for i in range(2)]
    g_wvv = SemGroup("wvv", "slot")

    mm2 = [pbank(f"mmb{i}", i) for i in range(4)]
    unit = 0
    u_w = []
    for cp in range(2):
        wa, wb = wr2[cp % 2]
        wa3 = wa.rearrange("p (k f) -> p k f", k=8)
        S.add("pool", lambda wa3=wa3, cp=cp: nc.gpsimd.dma_start(out=wa3, in_=w_in3[:, :, C_U + cp * 256:C_U + (cp + 1) * 256]),
              writes=[wb], dma=g_wr2[cp % 2])
        u_w.append((wa3, wb))

    def u_unit(i):
        tg, c = i // 4, i % 4
        cp, cc = c // 2, c % 2
        wa3, wb = u_w[cp]
        pa, pb_ = mm2[i % 2]
        for kc in range(8):
            S.add("pe", lambda kc=kc: nc.tensor.matmul(pa, lhsT=wa3[:, kc, cc * 128:(cc + 1) * 128], rhs=hT3[:, kc, tg * 512:(tg + 1) * 512],
                                                       start=(kc == 0), stop=(kc == 7)), reads=[wb, b_hTg[tg]], writes=[pb_])
        S.add("act", lambda: nc.scalar.activation(out=uT3[:, c, tg * 512:(tg + 1) * 512], in_=pa, func=AF.Gelu), reads=[pb_], writes=[b_uTg[tg]])

    wvv3 = wvv.rearrange("p (k f) -> p k f", k=8)
    S.add("pool", lambda: nc.gpsimd.dma_start(out=wvv3, in_=w_in3[:, :, C_V:C_V + 512]), writes=[b_wvv], dma=g_wvv)
    def v_A(j):
        pa, pb_ = mm2[2 + j % 2]
        va_, vb_ = vtmp[j % 3]
        sm, smb = lnsm[j % 3]
        st_, mv_, r_ = sm[:, 0:6], sm[:, 6:8], sm[:, 8:9]
        for kc in range(8):
            S.add("pe", lambda kc=kc: nc.tensor.matmul(pa, lhsT=hT3[:, kc, j * 128:(j + 1) * 128], rhs=wvv3[:, kc, :],
                                                       start=(kc == 0), stop=(kc == 7)),
                  reads=[b_wvv, b_hTg[j // 4]], writes=[pb_])
        S.add("act", lambda: nc.scalar.activation(out=va_, in_=pa, func=AF.Gelu), reads=[pb_], writes=[vb_])
        S.add("dve", lambda: nc.vector.bn_stats(out=st_, in_=va_), reads=[vb_], writes=[smb])
        S.add("dve", lambda: nc.vector.bn_aggr(out=mv_, in_=st_), reads=[smb], writes=[smb])
        S.add("pool", lambda: nc.gpsimd.tensor_scalar(out=r_, in0=mv_[:, 1:2], scalar1=LN_EPS, scalar2=None, op0=ALU.add),
              reads=[smb], writes=[smb])
        S.add("pool", lambda: nc.gpsimd.tensor_tensor(out=r_, in0=r_, in1=mhalf[:, 0:1], op=ALU.pow),
              reads=[smb, b_mhalf], writes=[smb])

    def v_B(j):
        va_, vb_ = vtmp[j % 3]
        sm, smb = lnsm[j % 3]
        mv_, r_ = sm[:, 6:8], sm[:, 8:9]
        S.add("dve", lambda: nc.vector.scalar_tensor_tensor(out=va_, in0=va_, scalar=mv_[:, 0:1], in1=lngt, op0=ALU.subtract, op1=ALU.mult),
              reads=[vb_, smb, b_lngt], writes=[vb_])
        S.add("dve", lambda: nc.vector.scalar_tensor_tensor(out=vn3[:, j, :], in0=va_, scalar=r_, in1=lnbt, op0=ALU.mult, op1=ALU.add),
              reads=[vb_, smb, b_lnbt], writes=[b_vnj[j]])

    wst3 = wst.rearrange("p (g q) -> p g q", g=4)
    mg = [pbank(f"mg{i}", 4 + i) for i in range(2)]

    def gmlp_mm(j):
        pa, pb_ = mg[j % 2]
        ga, gb_ = gtmp[j % NG]
        for g in range(4):
            S.add("pe", lambda g=g: nc.tensor.matmul(
                pa[:, g * 128:(g + 1) * 128], lhsT=vn3[:, j, g * 128:(g + 1) * 128], rhs=wst3[:, g, :], start=True, stop=True),
                reads=[b_vnj[j], b_wst], writes=[pb_])
        S.add("dve", lambda: nc.vector.tensor_tensor(out=ga, in0=pa, in1=bst, op=ALU.add), reads=[pb_, b_bst], writes=[gb_])

    def gmlp_mul(j):
        ga, gb_ = gtmp[j % NG]
        uv = uT3[:, :, j * 128:(j + 1) * 128]
        S.add("dve", lambda: nc.vector.tensor_tensor(out=uv, in0=ga.rearrange("p (g t) -> p g t", g=4), in1=uv, op=ALU.mult),
              reads=[gb_, b_uTg[j // 4]], writes=[b_uTg[j // 4]])

    woa, b_woa = sb("woa", R_C + 32768, 8192)
    wob, b_wob = sb("wob", R_C + 32768 + 8192, 8192)
    woa3 = woa.rearrange("p (k f) -> p k f", k=4)
    wob3 = wob.rearrange("p (k f) -> p k f", k=4)
    g_wo = SemGroup("wo", "all")
    wgA = [sb("wgA0", R_C + 0, 4096), sb("wgA1", R_C + 4096, 4096)]
    g_wgA = [SemGroup(f"wgA{i}", "slot") for i in range(2)]
    wgA_loaded = []
    for gi, cb in enumerate((C_G0, C_G1)):
        wa, wb = wgA[gi]
        wa3 = wa.rearrange("p (k f) -> p k f", k=8)
        S.add("pool", lambda wa3=wa3, cb=cb: nc.gpsimd.dma_start(out=wa3, in_=w_in3[:, :, cb:cb + 256]), writes=[wb], dma=g_wgA[gi])
        wgA_loaded.append((wa3, wb))
    S.add("pool", lambda: nc.gpsimd.dma_start(out=woa3, in_=w_o_a.rearrange("(k p) f -> p k f", p=128)), writes=[b_woa], dma=g_wo)
    S.add("pool", lambda: nc.gpsimd.dma_start(out=wob3, in_=w_o_b.rearrange("(k p) f -> p k f", p=128)), writes=[b_wob], dma=g_wo)
    wout_a, b_wout0 = sb("wout0", R_C + 49152, 8192)
    wout_b, b_wout1 = sb("wout1", R_C + 49152 + 8192, 8192)
    wout3 = arena[:, (R_C + 49152) // 2:(R_C + 49152 + 16384) // 2].rearrange("p (k f) -> p k f", k=8)
    g_wout = SemGroup("wout", "all")
    NXS = 4
    xs = [sb(f"xs{t}", R_C + 16384 + t * 4096, 4096, F32) for t in range(NXS)]
    g_xs = [SemGroup(f"xs{t}", "slot") for t in range(NXS)]
    for t in range(NXS):
        S.add("sp", lambda t=t: nc.sync.dma_start(out=xs[t][0], in_=x_ext[t * 128:(t + 1) * 128, :]), writes=[xs[t][1]], dma=g_xs[t])

    v_A(0)
    for j in range(NT):
        u_unit(j)
        if j + 1 < NT:
            v_A(j + 1)
        v_B(j)
        if j >= 1:
            gmlp_mm(j - 1)
        if j >= 4:
            gmlp_mul(j - 4)
    gmlp_mm(NT - 1)
    for j in range(NT - 4, NT):
        gmlp_mul(j)

    ck(3, [("uT", uT, b_uTg[3]), ("vn", vn, b_vnj[15])])
    y_aT3 = uT3

    ck(4, [("yaT", uT, b_uTg[3])])
    mT, b_mT = sb("mergedT", R_B, 32768)
    mT3 = mT.rearrange("p (k t) -> p k t", k=8)
    S.add("pool", lambda: nc.gpsimd.dma_start(out=wout3[:, 0:4, :], in_=w_out.rearrange("(k p) f -> p k f", p=128)[:, 0:4, :]),
          writes=[b_wout0], dma=g_wout)
    S.add("pool", lambda: nc.gpsimd.dma_start(out=wout3[:, 4:8, :], in_=w_out.rearrange("(k p) f -> p k f", p=128)[:, 4:8, :]),
          writes=[b_wout1], dma=g_wout)
    wgB = [sb("wgB0", R_D + 0, 4096), sb("wgB1", R_D + 4096, 4096)]
    wgC = [sb("wgC0", R_D + 36864, 4096), sb("wgC1", R_D + 40960, 4096)]
    g_wgB = [SemGroup(f"wgB{i}", "slot") for i in range(2)]
    g_wgC = [SemGroup(f"wgC{i}", "slot") for i in range(2)]
    tt = []
    o = R_D + 8192
    for i in range(4):
        tt.append(sb(f"tt{i}", o, 2048, F32)); o += 2048
    ffw = []
    g_ff = [(SemGroup(f"ff1_{i}", "slot"), SemGroup(f"ff2_{i}", "slot")) for i in range(3)]
    w_ff1_3 = w_ff1.rearrange("(k p) f -> p k f", p=128)
    w_ff2_3 = w_ff2.rearrange("(k p) f -> p k f", p=128)

    def load_ff(part):
        (w1a, w1b), (w2a, w2b) = ffw[part % 3]
        g1_, g2_ = g_ff[part % 3]
        S.add("pool", lambda: nc.gpsimd.dma_start(out=w1a.rearrange("p (k f) -> p k f", k=8),
                                                  in_=w_ff1_3[:, :, part * 512:(part + 1) * 512]), writes=[w1b], dma=g1_)
        S.add("pool", lambda: nc.gpsimd.dma_start(out=w2a.rearrange("p (k f) -> p k f", k=4),
                                                  in_=w_ff2_3[:, part * 4:(part + 1) * 4, :]), writes=[w2b], dma=g2_)

    gps = [[pbank(f"gp{i}_{k}", 4 * i + k) for k in range(4)] for i in range(2)]
    unit = 0
    for mp in range(DBG_MP):
        if mp == 0:
            was = wgA_loaded
        else:
            was = []
            bufs, grps = (wgB, g_wgB) if mp % 2 == 1 else (wgC, g_wgC)
            for gi, cb in enumerate((C_G0, C_G1)):
                wa, wb = bufs[gi]
                wa3 = wa.rearrange("p (k f) -> p k f", k=8)
                S.add("pool", lambda wa3=wa3, cb=cb, mp=mp: nc.gpsimd.dma_start(out=wa3, in_=w_in3[:, :, cb + mp * 256:cb + (mp + 1) * 256]),
                      writes=[wb], dma=grps[gi])
                was.append((wa3, wb))
        if mp == 2:
            ffw.append((sb("ff1_0", R_C, 8192), sb("ff2_0", R_C + 8192, 8192)))
            load_ff(0)
        for mm in range(2):
            m = mp * 2 + mm
            for tg in range(4):
                ps4 = gps[unit % 2]
                t0, t1, ta, tb = tt[0], tt[1], tt[2], tt[3]
                unit += 1
                tsl = slice(tg * 512, (tg + 1) * 512)
                for gi in range(2):
                    wa3, wb = was[gi]
                    pa, pb_ = ps4[gi]
                    for kc in range(8):
                        S.add("pe", lambda pa=pa, wa3=wa3, mm=mm, kc=kc, tsl=tsl: nc.tensor.matmul(
                            pa, lhsT=wa3[:, kc, mm * 128:(mm + 1) * 128], rhs=hT3[:, kc, tsl], start=(kc == 0), stop=(kc == 7)),
                            reads=[wb, b_hTg[tg]], writes=[pb_])
                pa, pb_ = ps4[2]
                for kc in range(4):
                    S.add("pe", lambda pa=pa, kc=kc, m=m, tsl=tsl: nc.tensor.matmul(
                        pa, lhsT=woa3[:, kc, m * 128:(m + 1) * 128], rhs=y_aT3[:, kc, tsl], start=(kc == 0), stop=(kc == 3)),
                        reads=[b_woa, b_uTg[tg]], writes=[pb_])
                pa, pb_ = ps4[3]
                for kc in range(4):
                    S.add("pe", lambda pa=pa, kc=kc, m=m, tsl=tsl: nc.tensor.matmul(
                        pa, lhsT=wob3[:, kc, m * 128:(m + 1) * 128], rhs=y_bT3[:, kc, tsl], start=(kc == 0), stop=(kc == 3)),
                        reads=[b_wob, b_ybT], writes=[pb_])
                S.add("act", lambda t0=t0, ps4=ps4, m=m: nc.scalar.activation(out=t0[0], in_=ps4[0][0], func=AF.Tanh, scale=0.5,
                                                                              bias=hbg[:, m:m + 1]),
                      reads=[ps4[0][1], b_hbg], writes=[t0[1]])
                S.add("act", lambda t1=t1, ps4=ps4, m=m: nc.scalar.activation(out=t1[0], in_=ps4[1][0], func=AF.Tanh, scale=0.5,
                                                                              bias=hbg[:, 8 + m:8 + m + 1]),
                      reads=[ps4[1][1], b_hbg], writes=[t1[1]])
                S.add("dve", lambda ta=ta, t0=t0, ps4=ps4: nc.vector.scalar_tensor_tensor(
                    out=ta[0], in0=t0[0], scalar=1.0, in1=ps4[2][0], op0=ALU.add, op1=ALU.mult),
                    reads=[t0[1], ps4[2][1]], writes=[ta[1]])
                S.add("dve", lambda tb=tb, t1=t1, ps4=ps4: nc.vector.scalar_tensor_tensor(
                    out=tb[0], in0=t1[0], scalar=1.0, in1=ps4[3][0], op0=ALU.add, op1=ALU.mult),
                    reads=[t1[1], ps4[3][1]], writes=[tb[1]])
                S.add("dve", lambda ta=ta, tb=tb, m=m, tsl=tsl: nc.vector.tensor_tensor(
                    out=mT3[:, m, tsl], in0=ta[0], in1=tb[0], op=ALU.add),
                    reads=[ta[1], tb[1]], writes=[b_mT])

    ck(5, [("mT", mT, b_mT)])
    j_RA = join_region("sb", R_A, 65536)
    x1 = []
    for t in range(NT):
        x1.append(sb(f"x1_{t}", R_A + t * 4096, 4096, F32, parent=j_RA))
    g_x1 = [SemGroup(f"x1_{t}", "slot") for t in range(NT)]

    def x_reload(t, after=None):
        xa, xb_ = x1[t]
        S.add("sp", lambda: nc.sync.dma_start(out=xa, in_=x_ext[t * 128:(t + 1) * 128, :]),
              reads=[after] if after is not None else [], writes=[xb_], dma=g_x1[t])

    for t in range(NXS, NXS + 3):
        x_reload(t)
    h2T = arena[:, R_D // 2:(R_D + 32768) // 2]
    j_RD = join_region("sb", R_D, 32768)
    b_h2Tg = [Buf(f"h2Tg{g}", "sb", R_D + g * 8192, 8192, parent=j_RD) for g in range(4)]
    h2T3 = h2T.rearrange("p (k t) -> p k t", k=8)
    o = R_D + 32768
    h2b = []
    for i in range(2):
        h2b.append(sb(f"h2b{i}", o, 2048)); o += 2048
    g2t, b_g2t = sb("g2t", o, 4096, F32); o += 4096
    junk2s = []
    for i in range(2):
        junk2s.append(sb(f"junk2_{i}", o, 2048)); o += 2048
    assert o <= R_E, o
    b_ss2c = [Buf(f"ss2_{t}", "sb", R_E + 888 + 4 * t, 4) for t in range(NT)]
    b_v2c = [Buf(f"v2_{t}", "sb", R_E + 1216 + 4 * t, 4) for t in range(NT)]
    b_rstd2c = [Buf(f"rstd2_{t}", "sb", R_E + 952 + 4 * t, 4) for t in range(NT)]
    g_c3 = SemGroup("const3", "all")
    S.add("sp", lambda: nc.sync.dma_start(out=g2t, in_=g2_d.partition_broadcast(128)), writes=[b_g2t], dma=g_c3)

    wo_ps = [pbank(f"wo{i}", i) for i in range(4)]
    pT2 = [pbank(f"pTb{i}", 4 + i, dtype=BF16) for i in range(2)]
    unit = 0

    def n2_A(t):
        xa, xb_ = x1[t]
        ha, hb_ = h2b[t % 2]
        ja, jb_ = junk2s[t % 2]
        S.add("act", lambda: nc.scalar.activation(out=ja, in_=xa, func=AF.Square, accum_out=ss2[:, t:t + 1]),
              reads=[xb_], writes=[jb_, b_ss2c[t]])
        S.add("pool", lambda: nc.gpsimd.tensor_scalar(out=v2t[:, t:t + 1], in0=ss2[:, t:t + 1], scalar1=1.0 / D, scalar2=RMS_EPS,
                                                      op0=ALU.mult, op1=ALU.add), reads=[b_ss2c[t]], writes=[b_v2c[t]])
        S.add("pool", lambda: nc.gpsimd.tensor_tensor(out=rstd2[:, t:t + 1], in0=v2t[:, t:t + 1], in1=mhalf[:, 0:1], op=ALU.pow),
              reads=[b_v2c[t], b_mhalf], writes=[b_rstd2c[t]])
        S.add("dve", lambda: nc.vector.scalar_tensor_tensor(out=ha, in0=xa, scalar=rstd2[:, t:t + 1], in1=g2t, op0=ALU.mult, op1=ALU.mult),
              reads=[xb_, b_rstd2c[t], b_g2t], writes=[hb_])

    def n2_B(t):
        ha, hb_ = h2b[t % 2]
        pa, pb_ = pT2[t % 2]
        pa3 = pa.rearrange("p (k t) -> p k t", k=8)
        for kc in range(8):
            S.add("pe", lambda kc=kc: nc.tensor.transpose(pa3[:, kc, :], ha[:, kc * 128:(kc + 1) * 128], ident),
                  reads=[hb_, b_ident], writes=[pb_])
        dst = h2T3[:, :, t * 128:(t + 1) * 128]
        S.add("act", lambda: nc.scalar.copy(out=dst, in_=pa3), reads=[pb_], writes=[b_h2Tg[t // 4]])

    for t in range(NT):
        xa, xb_ = x1[t]
        for dh in range(2):
            pa, pb_ = wo_ps[unit % 4]
            unit += 1
            for kc in range(8):
                S.add("pe", lambda pa=pa, kc=kc, t=t, dh=dh: nc.tensor.matmul(
                    pa, lhsT=mT3[:, kc, t * 128:(t + 1) * 128], rhs=wout3[:, kc, dh * 512:(dh + 1) * 512],
                    start=(kc == 0), stop=(kc == 7)), reads=[b_mT, b_wout0 if kc < 4 else b_wout1], writes=[pb_])
            if t < NXS:
                src, srcb = xs[t]
            else:
                src, srcb = xa, xb_
            S.add("dve", lambda pa=pa, xa=xa, dh=dh, src=src: nc.vector.scalar_tensor_tensor(
                out=xa[:, dh * 512:(dh + 1) * 512], in0=pa, scalar=0.5, in1=src[:, dh * 512:(dh + 1) * 512],
                op0=ALU.mult, op1=ALU.add), reads=[pb_, srcb], writes=[xb_])
        if t >= NXS and t + 3 < NT:
            x_reload(t + 3, after=xb_)
        if t >= 1:
            n2_A(t - 1)
        if t >= 2:
            n2_B(t - 2)
    def n2_tail():
        n2_A(NT - 1)
        n2_B(NT - 2)
        n2_B(NT - 1)

    ffw.append((sb("ff1_1", R_C + 16384, 8192), sb("ff2_1", R_C + 16384 + 8192, 8192)))
    ffw.append((sb("ff1_2", R_C + 2 * 16384, 8192), sb("ff2_2", R_C + 2 * 16384 + 8192, 8192)))
    load_ff(1)
    load_ff(2)
    ck(6, [("x1_%d" % t, x1[t][0], x1[t][1]) for t in range(NT)] + [("h2T", h2T, b_h2Tg[3])])
    o = R_B
    actT = []
    for i in range(2):
        actT.append(sb(f"actT{i}", o, 4096)); o += 4096
    sqf = []
    for i in range(2):
        sqf.append(sb(f"sqf{i}", o, 2048, F32)); o += 2048
    f1_ps = [pbank(f"f1p{i}", i) for i in range(3)]
    f2_ps = [pbank(f"f2p{i}", b) for i, b in enumerate((3, 6, 7))]
    g_out = SemGroup("out", "all")
    u1 = 0
    u2 = 0

    def ff1(part, tg, ui):
        nonlocal u1
        (w1a, w1b), _ = ffw[part % 3]
        w13 = w1a.rearrange("p (k f) -> p k f", k=8)
        aa, ab_ = actT[ui % 2]
        aa3 = aa.rearrange("p (c t) -> p c t", c=4)
        for fc in range(4):
            pa, pb_ = f1_ps[u1 % 3]
            qa, qb_ = sqf[u1 % 2]
            u1 += 1
            for kc in range(8):
                S.add("pe", lambda pa=pa, w13=w13, fc=fc, kc=kc, tg=tg: nc.tensor.matmul(
                    pa, lhsT=w13[:, kc, fc * 128:(fc + 1) * 128], rhs=h2T3[:, kc, tg * 512:(tg + 1) * 512],
                    start=(kc == 0), stop=(kc == 7)), reads=[w1b, b_h2Tg[tg]], writes=[pb_])
            S.add("act", lambda qa=qa, pa=pa: nc.scalar.activation(out=qa, in_=pa, func=AF.Square), reads=[pb_], writes=[qb_])
            S.add("dve", lambda aa3=aa3, fc=fc, pa=pa, qa=qa: nc.vector.scalar_tensor_tensor(
                out=aa3[:, fc, :], in0=pa, scalar=0.0, in1=qa, op0=ALU.is_gt, op1=ALU.mult),
                reads=[pb_, qb_], writes=[ab_])

    def ff2(part, tg, ui):
        nonlocal u2
        _, (w2a, w2b) = ffw[part % 3]
        w23 = w2a.rearrange("p (k f) -> p k f", k=4)
        aa, ab_ = actT[ui % 2]
        aa3 = aa.rearrange("p (c t) -> p c t", c=4)
        for tt_ in range(4):
            t = tg * 4 + tt_
            xa, xb_ = x1[t]
            for dh in range(2):
                pa, pb_ = f2_ps[u2 % 3]
                u2 += 1
                for fc in range(4):
                    S.add("pe", lambda pa=pa, aa3=aa3, w23=w23, fc=fc, tt_=tt_, dh=dh: nc.tensor.matmul(
                        pa, lhsT=aa3[:, fc, tt_ * 128:(tt_ + 1) * 128], rhs=w23[:, fc, dh * 512:(dh + 1) * 512],
                        start=(fc == 0), stop=(fc == 3)), reads=[ab_, w2b], writes=[pb_])
                S.add("dve", lambda pa=pa, xa=xa, dh=dh: nc.vector.tensor_tensor(
                    out=xa[:, dh * 512:(dh + 1) * 512], in0=pa, in1=xa[:, dh * 512:(dh + 1) * 512], op=ALU.add),
                    reads=[pb_, xb_], writes=[xb_])
            if part == 7:
                S.add("sp", lambda xa=xa, t=t: nc.sync.dma_start(out=out_d[t * 128:(t + 1) * 128, :], in_=xa),
                      reads=[xb_], dma=g_out)

    funits = [(p, tg) for p in range(8) for tg in range(4)]
    ff1(*funits[0], 0)
    for ui, (p, tg) in enumerate(funits):
        if ui + 1 < len(funits):
            ff1(*funits[ui + 1], ui + 1)
        ff2(p, tg, ui)
        if ui == 1:
            n2_tail()
        if tg == 3 and p + 3 < 8:
            load_ff(p + 3)

    S.frozen = False if stop is None else S.frozen
    S.emit([g_out] if stop is None else list(SemGroup.all_groups))
    return nc, S


def _bias_tables(rpb, half):
    def pattern(il, o):
        i = 16 * half + il
        j = i + o
        tab = np.full((8, 128, 128), NEG, dtype=np.float32)
        if j < 0 or j > 31:
            return tab
        a = np.arange(2)[:, None]; kc = np.arange(64)[None, :]
        kr = (2 * j + a + 0 * kc).reshape(-1)
        kcc = (0 * a + kc).reshape(-1)
        r = (2 * i + a + 0 * kc).reshape(-1)
        c = kcc.copy()
        r0 = np.clip(r - 4, 0, 56)
        cs = np.clip(c - 8, 0, 48)
        KR, R = kr[:, None], r[None, :]
        KC, C = kcc[:, None], c[None, :]
        valid = (KR >= r0[None, :]) & (KR < r0[None, :] + 8) & (KC >= cs[None, :]) & (KC < cs[None, :] + 16)
        dr = np.clip(KR - R + 7, 0, 14)
        dc = np.clip(KC - C + 15, 0, 30)
        g = rpb[:, dr, dc]
        return np.where(valid[None], g, tab)
    pats = []
    pats += [pattern(0, o) for o in range(-2, 4)]
    pats += [pattern(1, o) for o in range(-2, 3)] + [np.full((8, 128, 128), NEG, np.float32)]
    pats += [pattern(8, o) for o in range(-2, 3)]
    pats += [pattern(14, o) for o in range(-2, 3)] + [np.full((8, 128, 128), NEG, np.float32)]
    pats += [pattern(15, o) for o in range(-3, 3)]
    arr = np.stack(pats, 0)
    arr = arr.transpose(2, 0, 1, 3).reshape(128, NPAT * 8 * 128)
    return np.ascontiguousarray(arr)


_CACHE = {}


def kernel(x, norm1_g, w_in, b_gate, gmlp_ln_g, gmlp_ln_b, gmlp_w_s, gmlp_b_s,
           na_q_g, na_k_g, na_rpb, w_o_a, w_o_b, w_out, norm2_g, w_ff1, w_ff2):
    if "nc" not in _CACHE:
        _CACHE["nc"] = build_program()[0]
    nc = _CACHE["nc"]
    in_maps = make_in_maps(x, norm1_g, w_in, b_gate, gmlp_ln_g, gmlp_ln_b, gmlp_w_s, gmlp_b_s,
                           na_q_g, na_k_g, na_rpb, w_o_a, w_o_b, w_out, norm2_g, w_ff1, w_ff2)
    res = run_bass_kernel_spmd(nc, in_maps, core_ids=list(range(8)))
    out = np.empty((4, 2 * NTOK, D), np.float32)
    for core in range(8):
        b, hf = core // 2, core % 2
        out[b, hf * NTOK:(hf + 1) * NTOK] = res.results[core]["out"]
    return out


def make_in_maps(x, norm1_g, w_in, b_gate, gmlp_ln_g, gmlp_ln_b, gmlp_w_s, gmlp_b_s,
                 na_q_g, na_k_g, na_rpb, w_o_a, w_o_b, w_out, norm2_g, w_ff1, w_ff2):
    f = lambda a: np.ascontiguousarray(np.asarray(a, dtype=np.float32))
    x = f(x)
    shared = {
        "w_in": f(w_in[0]), "w_o_a": f(w_o_a[0]), "w_o_b": f(w_o_b[0]), "w_out": f(w_out[0]),
        "w_ff1": f(w_ff1[0]), "w_ff2": f(w_ff2[0]),
        "norm1_g": f(norm1_g[0]).reshape(1, D), "norm2_g": f(norm2_g[0]).reshape(1, D),
        "ln_g": f(gmlp_ln_g[0]).reshape(1, 512), "ln_b": f(gmlp_ln_b[0]).reshape(1, 512),
        "b_s": f(gmlp_b_s[0]).reshape(1, 512),
        "w_sT": f(np.transpose(np.asarray(gmlp_w_s[0]), (2, 0, 1)).reshape(128, 512)),
        "ident": np.eye(128, dtype=np.float32),
        "bones": np.kron(np.eye(2, dtype=np.float32), np.ones((64, 64), np.float32)),
    }
    qg = np.tile(np.asarray(na_q_g[0], np.float32), 2).reshape(128, 1)
    kg = np.tile(np.asarray(na_k_g[0], np.float32), 2).reshape(128, 1)
    bg = np.asarray(b_gate[0], np.float32).reshape(16, 128).T
    z64 = np.zeros((64, 1), np.float32)
    qg0 = np.concatenate([qg[:64], z64], axis=0)
    qg1 = np.concatenate([z64, qg[64:]], axis=0)
    shared["smalls"] = f(np.concatenate([qg, kg, bg, qg0, qg1], axis=1))
    tabs = [_bias_tables(np.asarray(na_rpb[0], np.float32), hf) for hf in range(2)]
    zeros = np.zeros((256, D), np.float32)
    in_maps = []
    for core in range(8):
        b, hf = core // 2, core % 2
        own = x[b, hf * NTOK:(hf + 1) * NTOK]
        before = x[b, NTOK - 256:NTOK] if hf == 1 else zeros
        after = x[b, NTOK:NTOK + 256] if hf == 0 else zeros
        m = dict(shared)
        m["x_ext"] = np.ascontiguousarray(np.concatenate([own, before, after], axis=0))
        m["btab"] = tabs[hf]
        in_maps.append(m)
    return in_maps
```

```python
import numpy as np
import concourse.bass as bass
import concourse.mybir as mybir
from concourse.bass_utils import run_bass_kernel_spmd

F32 = mybir.dt.float32
BF16 = mybir.dt.bfloat16
AF = mybir.ActivationFunctionType
ALU = mybir.AluOpType

D = 1024
NTOK = 2048
NT = 16
NSLOT = 20
RMS_EPS = 1e-6
LN_EPS = 1e-5
NEG = -30000.0
DBG_MP = 4
NPAT = 29

C_U, C_V, C_Q, C_K, C_VA, C_G0, C_G1 = 0, 512, 1024, 1536, 2048, 2560, 3584


def _compact(ops):
    out, last = [], {}
    for p in ops:
        if p.is_dma:
            out.append(p)
        elif p.eng not in last or last[p.eng].idx < p.idx:
            last[p.eng] = p
    return out + list(last.values())


class Buf:
    registry = {"sb": [], "ps": []}

    def __init__(self, name, space, start, size, parent=None):
        self.name, self.space, self.start, self.size = name, space, start, size
        self.last_w = None
        self.readers = []
        self.inherit = list(parent.inherit) if parent is not None else []
        self.dead = False
        for o in Buf.registry[space]:
            if o.start < start + size and start < o.start + o.size:
                if o.last_w is not None:
                    self.inherit.append(o.last_w)
                self.inherit.extend(o.readers)
                self.inherit.extend(o.inherit)
                o.dead = True
        self.inherit = _compact(self.inherit)
        Buf.registry[space].append(self)


def join_region(space, start, size):
    j = Buf("join", space, start, size)
    j.dead = True
    Buf.registry[space] = [o for o in Buf.registry[space] if o is not j]
    return j


class SemGroup:
    all_groups = []

    def __init__(self, name, kind):
        self.name, self.kind = name, kind
        self.n = 0
        self.sem = None
        SemGroup.all_groups.append(self)


class Op:
    __slots__ = ("eng", "fn", "is_dma", "group", "gidx", "waits", "need_inc", "count", "idx")


class Sched:
    ENGS = ("pe", "act", "dve", "pool", "sp")

    def __init__(self, nc):
        self.nc = nc
        self.ops = []
        self.eng_obj = {"pe": nc.tensor, "act": nc.scalar, "dve": nc.vector, "pool": nc.gpsimd, "sp": nc.sync}

    def add(self, eng, fn, reads=(), writes=(), dma=None):
        if getattr(self, "frozen", False):
            return None
        op = Op()
        op.eng, op.fn, op.is_dma, op.group = eng, fn, dma is not None, dma
        op.idx = len(self.ops)
        op.need_inc = False
        op.count = None
        if dma is not None:
            assert getattr(dma, "eng", eng) == eng, "a DMA semaphore group must be fed by a single queue"
            dma.eng = eng
            op.gidx = dma.n
            dma.n += 1
        deps = []
        for b in reads:
            assert not b.dead, f"read of dead buf {b.name}"
            if b.last_w is not None:
                deps.append(b.last_w)
            deps.extend(b.inherit) if b.last_w is None else None
        for b in writes:
            assert not b.dead, f"write of dead buf {b.name}"
            if b.last_w is not None:
                deps.append(b.last_w)
            deps.extend(b.readers)
            deps.extend(b.inherit)
        w = []
        seen = set()
        if dma is not None and dma.kind == "all":
            assert all(not (p.is_dma and p.group is dma) for p in deps), "intra-'all'-group dependency"
        for p in deps:
            if p.idx in seen or p is op:
                continue
            seen.add(p.idx)
            if (not p.is_dma) and (not op.is_dma) and p.eng == "pe" and op.eng == "pe":
                continue
            w.append(p)
        op.waits = w
        for b in reads:
            b.readers.append(op)
            if len(b.readers) > 1:
                b.readers = _compact(b.readers)
        for b in writes:
            b.last_w = op
            b.readers = []
            b.inherit = []
        self.ops.append(op)
        return op

    def emit(self, final_groups):
        nc = self.nc
        for op in self.ops:
            for p in op.waits:
                p.need_inc = True
        sems = {e: nc.alloc_semaphore("s_" + e) for e in ("pe", "act", "dve", "pool")}
        cnt = {e: 0 for e in sems}
        waited = {e: {} for e in self.ENGS}
        nwait = 0
        for op in self.ops:
            eo = self.eng_obj[op.eng]
            need = {}
            for p in op.waits:
                if p.is_dma:
                    g = p.group
                    sem = g.sem
                    val = 16 * (p.gidx + 1) if g.kind == "slot" else 16 * g.n
                else:
                    sem = sems[p.eng]
                    val = p.count
                key = id(sem)
                if key not in need or need[key][1] < val:
                    need[key] = (sem, val)
            for key, (sem, val) in need.items():
                if waited[op.eng].get(key, 0) >= val:
                    continue
                waited[op.eng][key] = val
                eo.wait_ge(sem, val)
                nwait += 1
            inst = op.fn()
            if op.is_dma:
                g = op.group
                if g.sem is None:
                    g.sem = nc.alloc_semaphore("d_" + g.name)
                inst.then_inc(g.sem, 16)
            elif op.need_inc:
                cnt[op.eng] += 1
                op.count = cnt[op.eng]
                inst.then_inc(sems[op.eng], 1)
        for g in final_groups:
            if g.sem is not None:
                nc.sync.wait_ge(g.sem, 16 * g.n)
        self.stats = dict(n_ops=len(self.ops), n_waits=nwait, counts=dict(cnt))


def build_program(stop=None):
    Buf.registry = {"sb": [], "ps": []}
    SemGroup.all_groups = []
    nc = bass.Bass("TRN2", target_bir_lowering=False)
    S = Sched(nc)
    g_dbg = SemGroup("dbg", "all")

    def ck(k, items):
        if stop != k or getattr(S, "frozen", False):
            return
        for name, ap, buf in items:
            dt_ = nc.dram_tensor("dbg_" + name, list(ap.shape), F32, kind="ExternalOutput").ap()
            S.add("pool", lambda dt_=dt_, ap=ap: nc.gpsimd.dma_start(out=dt_, in_=ap, max_dma_last_dim=2048), reads=[buf], dma=g_dbg)
        S.frozen = True

    def din(name, shape):
        return nc.dram_tensor(name, list(shape), F32, kind="ExternalInput").ap()

    x_ext = din("x_ext", [NSLOT * 128, D])
    w_in = din("w_in", [D, 4608])
    w_o_a = din("w_o_a", [512, D])
    w_o_b = din("w_o_b", [512, D])
    w_out = din("w_out", [D, D])
    w_ff1 = din("w_ff1", [D, 4096])
    w_ff2 = din("w_ff2", [4096, D])
    g1_d = din("norm1_g", [1, D])
    g2_d = din("norm2_g", [1, D])
    lng_d = din("ln_g", [1, 512])
    lnb_d = din("ln_b", [1, 512])
    bs_d = din("b_s", [1, 512])
    wst_d = din("w_sT", [128, 512])
    smalls_d = din("smalls", [128, 20])
    ident_d = din("ident", [128, 128])
    bones_d = din("bones", [128, 128])
    btab_d = din("btab", [128, NPAT * 8 * 128])
    out_d = nc.dram_tensor("out", [NTOK, D], F32, kind="ExternalOutput").ap()

    ARENA_ELEMS = 105472
    arena = nc.alloc_sbuf_tensor("arena", [128, ARENA_ELEMS], BF16)
    psum = nc.alloc_psum_tensor("psum", [128, 4096], F32)

    def sb(name, off, nbytes, dtype=BF16, parent=None):
        assert off % 4 == 0 and off + nbytes <= ARENA_ELEMS * 2, (name, off, nbytes)
        ap = arena[:, off // 2:(off + nbytes) // 2]
        if dtype == F32:
            ap = ap.bitcast(F32)
        return ap, Buf(name, "sb", off, nbytes, parent=parent)

    def pbank(name, bank, nbanks=1, dtype=F32, boff=0, nbytes=None):
        start = bank * 2048 + boff
        nb = nbanks * 2048 - boff if nbytes is None else nbytes
        ap = psum[:, start // 4:(start + nb) // 4]
        if dtype == BF16:
            ap = ap.bitcast(BF16)
        return ap, Buf(name, "ps", start, nb)

    R_A = 0
    R_B = 65536
    R_C = 98304
    R_D = 163840
    R_E = 208896
    END = ARENA_ELEMS * 2

    g_const = SemGroup("const_sw", "all")
    g_const_h = SemGroup("const_hw", "all")
    ident, b_ident = sb("ident", R_E, 256)
    bones, b_bones = sb("bones", R_E + 256, 256)
    smalls, b_smalls = sb("smalls", R_E + 1296, 80, F32)
    hbg, b_hbg = sb("hbg", R_E + 584, 64, F32)
    ss1, b_ss1 = sb("ss1", R_E + 648, 80, F32)
    std1, b_std1 = sb("std1", R_E + 728, 80, F32)
    rstd1, b_rstd1 = sb("rstd1", R_E + 808, 80, F32)
    ss2, b_ss2 = sb("ss2", R_E + 888, 64, F32)
    rstd2, b_rstd2 = sb("rstd2", R_E + 952, 64, F32)
    mhalf, b_mhalf = sb("mhalf", R_E + 1016, 64, F32)
    lnst, b_lnst = sb("lnst", R_E + 1080, 24 * 2, F32)
    lnmv, b_lnmv = sb("lnmv", R_E + 1128, 8 * 2, F32)
    lnr, b_lnr = sb("lnr", R_E + 1144, 4 * 2, F32)
    rden, b_rden_ = sb("rden", R_E + 1152, 32 * 2, F32)
    v2t, b_v2t = sb("v2t", R_E + 1216, 64, F32)
    epsc, b_epsc = sb("epsc", R_E + 1280, 16, F32)
    assert R_E + 1376 <= END
    S.add("pool", lambda: nc.gpsimd.memset(epsc[:, 0:1], RMS_EPS), writes=[b_epsc])
    S.add("pool", lambda: nc.gpsimd.memset(epsc[:, 1:2], 64 * RMS_EPS), writes=[b_epsc])
    warm, b_warm = sb("warm", R_E + 1376, 4, F32)
    S.add("act", lambda: nc.scalar.activation(out=warm, in_=epsc[:, 0:1], func=AF.Exp), reads=[b_epsc], writes=[b_warm])

    S.add("pool", lambda: nc.gpsimd.dma_start(out=ident, in_=ident_d), writes=[b_ident], dma=g_const)
    S.add("pool", lambda: nc.gpsimd.dma_start(out=bones, in_=bones_d), writes=[b_bones], dma=g_const)
    S.add("sp", lambda: nc.sync.dma_start(out=smalls, in_=smalls_d), writes=[b_smalls], dma=g_const_h)
    S.add("pool", lambda: nc.gpsimd.memset(mhalf, -0.5), writes=[b_mhalf])
    S.add("dve", lambda: nc.vector.tensor_scalar(out=hbg, in0=smalls[:, 2:18], scalar1=0.5, scalar2=None,
                                                  op0=ALU.mult), reads=[b_smalls], writes=[b_hbg])

    hT = arena[:, R_A // 2:(R_A + 32768) // 2]
    b_hTg = [Buf(f"hTg{g}", "sb", R_A + g * 8192, 8192) for g in range(4)]
    hT3 = hT.rearrange("p (k t) -> p k t", k=8)
    hTh, b_hTh = sb("hTh", R_D, 8192)
    hTh3 = hTh.rearrange("p (k t) -> p k t", k=8)
    b_hT_g = [Buf(f"hT_g{g}", "sb", R_A + 0, 0) for g in range(0)]

    o = R_D + 8192
    NXR = 4
    xr = []
    for i in range(NXR):
        xr.append(sb(f"xr{i}", o, 4096, F32)); o += 4096
    hb = []
    for i in range(2):
        hb.append(sb(f"hb{i}", o, 2048)); o += 2048
    g1t, b_g1t = sb("g1t", o, 4096, F32); o += 4096
    junks = []
    for i in range(2):
        junks.append(sb(f"junk{i}", o, 2048)); o += 2048
    wrKa, b_wrK = sb("wrK", o, 8192); o += 8192
    wrK3 = wrKa.rearrange("p (k f) -> p k f", k=8)
    assert o <= R_E
    g_xr = [SemGroup(f"xr{i}", "slot") for i in range(NXR)]
    S.add("sp", lambda: nc.sync.dma_start(out=g1t, in_=g1_d.partition_broadcast(128)), writes=[b_g1t], dma=g_const_h)

    pT = [pbank(f"pT{i}", i, dtype=BF16) for i in range(2)]

    hT_gb = [Buf(f"hTg{g}", "sb", R_A + g * 1024, 1024) for g in range(0)]

    b_ss1c = [Buf(f"ss1_{t}", "sb", R_E + 648 + 4 * t, 4) for t in range(NSLOT)]
    b_std1c = [Buf(f"std1_{t}", "sb", R_E + 728 + 4 * t, 4) for t in range(NSLOT)]
    b_rstd1c = [Buf(f"rstd1_{t}", "sb", R_E + 808 + 4 * t, 4) for t in range(NSLOT)]
    p0_order = list(range(NT, NSLOT)) + list(range(NT))

    def p0_A(ti):
        t = p0_order[ti]
        xa, xb_ = xr[ti % NXR]
        ha, hb_ = hb[ti % 2]
        ja, jb_ = junks[ti % 2]
        S.add("sp", lambda: nc.sync.dma_start(out=xa, in_=x_ext[t * 128:(t + 1) * 128, :]), writes=[xb_], dma=g_xr[ti % NXR])
        S.add("act", lambda: nc.scalar.activation(out=ja, in_=xa, func=AF.Square, accum_out=ss1[:, t:t + 1]),
              reads=[xb_], writes=[jb_, b_ss1c[t]])
        S.add("pool", lambda: nc.gpsimd.tensor_scalar(out=std1[:, t:t + 1], in0=ss1[:, t:t + 1], scalar1=1.0 / D, scalar2=RMS_EPS,
                                                      op0=ALU.mult, op1=ALU.add), reads=[b_ss1c[t]], writes=[b_std1c[t]])
        S.add("pool", lambda: nc.gpsimd.tensor_tensor(out=rstd1[:, t:t + 1], in0=std1[:, t:t + 1], in1=mhalf[:, 0:1], op=ALU.pow),
              reads=[b_std1c[t], b_mhalf], writes=[b_rstd1c[t]])

    def p0_A2(ti):
        t = p0_order[ti]
        xa, xb_ = xr[ti % NXR]
        ha, hb_ = hb[ti % 2]
        S.add("dve", lambda: nc.vector.scalar_tensor_tensor(out=ha, in0=xa, scalar=rstd1[:, t:t + 1], in1=g1t, op0=ALU.mult, op1=ALU.mult),
              reads=[xb_, b_rstd1c[t], b_g1t], writes=[hb_])

    def p0_B(ti):
        t = p0_order[ti]
        ha, hb_ = hb[ti % 2]
        pa, pb_ = pT[ti % 2]
        pa3 = pa.rearrange("p (k t) -> p k t", k=8)
        for kc in range(8):
            S.add("pe", lambda kc=kc: nc.tensor.transpose(pa3[:, kc, :], ha[:, kc * 128:(kc + 1) * 128], ident),
                  reads=[hb_, b_ident], writes=[pb_])
        if t < NT:
            dst, dbuf = hT3[:, :, t * 128:(t + 1) * 128], b_hTg[t // 4]
        else:
            dst, dbuf = hTh3[:, :, (t - NT) * 128:(t - NT + 1) * 128], b_hTh
        S.add("dve", lambda: nc.vector.tensor_copy(out=dst, in_=pa3), reads=[pb_], writes=[dbuf])

    qm, b_qm = sb("qm", R_B, 32768)
    qm4 = qm.rearrange("p (h c t) -> p h c t", h=2, c=4)
    kT, b_kT = sb("kT", R_C, 20480)
    kT3 = kT.rearrange("p (c t) -> p c t", c=4)
    Va, b_Va = sb("Vaug", R_C + 20480, 20800)
    Va4 = Va.rearrange("p (s h d) -> p s h d", s=NSLOT, h=8)
    o = R_A + 32768
    sq = []
    for i in range(2):
        sq.append(sb(f"sq{i}", o, 1024)); o += 1024
    stdb = []
    for i in range(2):
        stdb.append(sb(f"std{i}", o, 2048, F32)); o += 2048
    rsb = []
    for i in range(2):
        rsb.append(sb(f"rs{i}", o, 2048, F32)); o += 2048
    assert o <= R_A + 49152
    wkey = {}
    g_wrK = SemGroup("wrK", "slot")
    g_wv = SemGroup("wv", "slot")
    w_in3 = w_in.rearrange("(k p) f -> p k f", p=128)

    S.add("pool", lambda: nc.gpsimd.memset(Va4[:, :, :, 64:65], 1.0), writes=[b_Va])

    mm_ps = [pbank(f"mm{i}", 2 + i) for i in range(4)]
    ss_ps = [pbank(f"ssp{i}", 6 + i) for i in range(2)]

    def load_w256(key, col0):
        (wa, wb), grp = wkey[key]
        wa3 = wa.rearrange("p (k f) -> p k f", k=8)
        S.add("pool", lambda: nc.gpsimd.dma_start(out=wa3, in_=w_in3[:, :, col0:col0 + 256]),
              writes=[wb], dma=grp)
        return wa3, wb

    def rhs_group(tg):
        if tg < 4:
            return (lambda kc: hT3[:, kc, tg * 512:(tg + 1) * 512]), b_hTg[tg]
        return (lambda kc: hTh3[:, kc, :]), b_hTh

    units_qk = []
    for tg in [4, 0, 1, 2, 3]:
        for cp in range(2):
            for cc in range(2):
                units_qk.append(("k", C_K, cp, cc, tg))
    for cp in range(2):
        for cc in range(2):
            for tg in [0, 1, 2, 3]:
                units_qk.append(("q", C_Q, cp, cc, tg))
    wcache = {}

    def qk_mm(u):
        kind, cbase, cp, cc, tg = units_qk[u]
        key = (kind, cp)
        if key not in wcache:
            wcache[key] = load_w256(key, cbase + cp * 256)
        wa3, wb = wcache[key]
        rf, rbuf = rhs_group(tg)
        pa, pb_ = mm_ps[u % 4]
        qa, qb_ = sq[u % 2]
        for kc in range(8):
            S.add("pe", lambda kc=kc: nc.tensor.matmul(pa, lhsT=wa3[:, kc, cc * 128:(cc + 1) * 128], rhs=rf(kc),
                                                       start=(kc == 0), stop=(kc == 7)), reads=[wb, rbuf], writes=[pb_])
        S.add("act", lambda: nc.scalar.activation(out=qa, in_=pa, func=AF.Square), reads=[pb_], writes=[qb_])

    def qk_fin(u):
        kind, cbase, cp, cc, tg = units_qk[u]
        c = cp * 2 + cc
        pa, pb_ = mm_ps[u % 4]
        sa, sb_ = ss_ps[u % 2]
        qa, qb_ = sq[u % 2]
        sta, stb_ = stdb[u % 2]
        ra, rb_ = rsb[u % 2]
        S.add("pe", lambda: nc.tensor.matmul(sa, lhsT=bones, rhs=qa, start=True, stop=True), reads=[qb_, b_bones], writes=[sb_])
        if kind == "k":
            S.add("act", lambda: nc.scalar.activation(out=sta, in_=sa, func=AF.Ln, scale=1.0 / 64, bias=RMS_EPS), reads=[sb_], writes=[stb_])
        else:
            S.add("act", lambda: nc.scalar.activation(out=sta, in_=sa, func=AF.Ln, scale=1.0, bias=64 * RMS_EPS), reads=[sb_], writes=[stb_])
        S.add("act", lambda: nc.scalar.activation(out=ra, in_=sta, func=AF.Exp, scale=-0.5), reads=[stb_], writes=[rb_])
        if kind == "k":
            tok0 = tg * 512
            S.add("dve", lambda: nc.vector.scalar_tensor_tensor(out=kT3[:, c, tok0:tok0 + 512], in0=pa, scalar=smalls[:, 1:2], in1=ra,
                                                                op0=ALU.mult, op1=ALU.mult), reads=[pb_, rb_, b_smalls], writes=[b_kT])
        else:
            for hp in range(2):
                S.add("dve", lambda hp=hp: nc.vector.scalar_tensor_tensor(
                    out=qm4[:, hp, c, tg * 512:(tg + 1) * 512], in0=pa, scalar=smalls[:, 18 + hp:19 + hp],
                    in1=ra, op0=ALU.mult, op1=ALU.mult), reads=[pb_, rb_, b_smalls], writes=[b_qm])

    qk_next = [0]

    def qk_step():
        u = qk_next[0]
        if u < len(units_qk):
            qk_mm(u)
        if u >= 1:
            qk_fin(u - 1)
        qk_next[0] = u + 1

    S.add("pool", lambda: nc.gpsimd.dma_start(out=wrK3, in_=w_in3[:, :, C_K:C_K + 512]), writes=[b_wrK], dma=g_wrK)
    wcache[("k", 0)] = (wrK3[:, :, 0:256], b_wrK)
    wcache[("k", 1)] = (wrK3[:, :, 256:512], b_wrK)
    p0_A(0)
    p0_A(1)
    p0_A2(0)
    for ti in range(NSLOT):
        if ti + 2 < NSLOT:
            p0_A(ti + 2)
        if ti + 1 < NSLOT:
            p0_A2(ti + 1)
        p0_B(ti)
        if ti >= 4:
            qk_step()
    ck(0, [("hT", hT, b_hTg[3]), ("hTh", hTh, b_hTh)])
    wkey[("q", 0)] = (sb("wrQ0", R_D + 8192, 4096), SemGroup("wrQ0", "slot"))
    wkey[("q", 1)] = (sb("wrQ1", R_D + 12288, 4096), SemGroup("wrQ1", "slot"))
    wv, b_wv = sb("wv", R_D + 16384, 8192)
    while qk_next[0] <= len(units_qk):
        qk_step()
    unit = len(units_qk)

    wv3 = wv.rearrange("p (k f) -> p k f", k=8)
    S.add("pool", lambda: nc.gpsimd.dma_start(out=wv3, in_=w_in3[:, :, C_VA:C_VA + 512]), writes=[b_wv], dma=g_wv)
    for j in range(NSLOT):
        pa, pb_ = mm_ps[unit % 4]
        unit += 1
        if j < NT:
            lf, lbuf = (lambda kc, j=j: hT3[:, kc, j * 128:(j + 1) * 128]), b_hTg[j // 4]
        else:
            lf, lbuf = (lambda kc, j=j: hTh3[:, kc, (j - NT) * 128:(j - NT + 1) * 128]), b_hTh
        for kc in range(8):
            S.add("pe", lambda pa=pa, lf=lf, kc=kc: nc.tensor.matmul(pa, lhsT=lf(kc), rhs=wv3[:, kc, :],
                                                                     start=(kc == 0), stop=(kc == 7)),
                  reads=[b_wv, lbuf], writes=[pb_])
        src = pa.rearrange("p (h d) -> p h d", h=8)
        dst = Va4[:, j, :, 0:64]
        if j % 2 == 0:
            S.add("act", lambda dst=dst, src=src: nc.scalar.copy(out=dst, in_=src), reads=[pb_], writes=[b_Va])
        else:
            S.add("dve", lambda dst=dst, src=src: nc.vector.tensor_copy(out=dst, in_=src), reads=[pb_], writes=[b_Va])

    ck(1, [("kT", kT, b_kT), ("qm", qm, b_qm), ("Va", Va, b_Va)])
    tin, b_tin = sb("tab_int", R_C + 41472, 10240)
    tin4 = tin.rearrange("p (a h q) -> p a h q", a=5, h=8)
    tsa, b_tsa = sb("tab_sa", R_C + 51712, 12288)
    tsa4 = tsa.rearrange("p (a h q) -> p a h q", a=6, h=8)
    assert R_C + 51712 + 12288 <= R_D
    o = R_D
    tsb, b_tsb = sb("tab_sb", R_A + 49152, 12288)
    tsb4 = tsb.rearrange("p (a h q) -> p a h q", a=6, h=8)
    PTb = []
    for i in range(3):
        PTb.append(sb(f"PT{i}", o, 3072)); o += 3072
    Sbb = []
    for i in range(2):
        Sbb.append(sb(f"Sb{i}", o, 6144, F32)); o += 6144
    ybt = []
    for i in range(2):
        ybt.append(sb(f"ybt{i}", o, 1024)); o += 1024
    assert o <= R_E
    y_bT, b_ybT = sb("y_bT", R_A + 32768, 16384)
    y_bT3 = y_bT.rearrange("p (c t) -> p c t", c=4)
    g_tin = SemGroup("tin", "slot")
    g_tsa = SemGroup("tsa", "slot")
    g_tsb = SemGroup("tsb", "slot")
    PW = 8 * 128

    def load_tab(dst4, dbuf, grp, p0, npat, eng="pool"):
        S.add("pool", lambda: nc.gpsimd.dma_start(
            out=dst4[:, 0:npat, :, :].rearrange("p a h q -> p a (h q)"),
            in_=btab_d[:, p0 * PW:(p0 + npat) * PW].rearrange("p (a x) -> p a x", a=npat)),
            writes=[dbuf], dma=grp)

    load_tab(tsa4, b_tsa, g_tsa, 0, 6)
    load_tab(tin4, b_tin, g_tin, 12, 5)
    load_tab(tsb4, b_tsb, g_tsb, 6, 6)

    S_ps = [pbank(f"S{i}", 3 * i, nbytes=6144) for i in range(2)]
    _pv = pbank("PV", 6, nbytes=4 * 65 * 4)
    PV_ps = [_pv, _pv]
    yTp_a, yTp_b = pbank("yTp", 7, dtype=BF16, nbytes=1024)
    yTp3 = yTp_a.rearrange("p (c t) -> p c t", c=4)

    def kst(s):
        if 2 <= s < 18:
            return s - 2
        return 16 + s if s < 2 else s

    def na_unit_info(il):
        if il == 0:
            offs = list(range(-2, 4)); tab = (tsa4, b_tsa)
        elif il == 1:
            offs = list(range(-2, 3)); tab = (tsb4, b_tsb)
        elif il == 14:
            offs = list(range(-2, 3)); tab = (tsa4, b_tsa)
        elif il == 15:
            offs = list(range(-3, 3)); tab = (tsb4, b_tsb)
        else:
            offs = list(range(-2, 3)); tab = (tin4, b_tin)
        slots = [il + 2 + o_ for o_ in offs]
        return slots, tab

    def na_S(il, c, u):
        slots, (t4, tbuf) = na_unit_info(il)
        ns = len(slots)
        sa, sb_ = S_ps[u % 2]
        pa, pb_ = PTb[u % 3]
        for n, s in enumerate(slots):
            j = kst(s)
            S.add("pe", lambda n=n, j=j: nc.tensor.matmul(
                sa[:, n * 256:(n + 1) * 256], lhsT=kT3[:, c, j * 128:(j + 1) * 128],
                rhs=qm4[:, :, c, il * 128:(il + 1) * 128], start=True, stop=False),
                reads=[b_kT, b_qm], writes=[sb_])
            S.add("pe", lambda n=n: nc.tensor.matmul(
                sa[:, n * 256:(n + 1) * 256], lhsT=ident,
                rhs=t4[:, n, 2 * c:2 * c + 2, :], start=False, stop=True),
                reads=[tbuf, b_ident], writes=[sb_])
        S.add("act", lambda: nc.scalar.activation(out=pa[:, 0:ns * 256], in_=sa[:, 0:ns * 256], func=AF.Exp),
              reads=[sb_], writes=[pb_])

    def na_PV(il, c, u):
        slots, _ = na_unit_info(il)
        pa, pb_ = PTb[u % 3]
        for hp in range(2):
            h = 2 * c + hp
            va, vb_ = PV_ps[h // 4]
            va3 = va.rearrange("p (h d) -> p h d", h=4)
            for n, s in enumerate(slots):
                j = kst(s)
                S.add("pe", lambda va3=va3, n=n, j=j, h=h, hp=hp, last=(n == len(slots) - 1): nc.tensor.matmul(
                    va3[:, h % 4, :], lhsT=pa[:, n * 256 + hp * 128:n * 256 + hp * 128 + 128], rhs=Va4[:, j, h, :],
                    start=(n == 0), stop=last), reads=[pb_, b_Va], writes=[vb_])
            if h % 4 == 3:
                hb4 = h // 4
                ya, yb_ = ybt[il % 2]
                ya3 = ya.rearrange("p (h d) -> p h d", h=8)
                rd = rden[:, (il % 2) * 8 + hb4 * 4:(il % 2) * 8 + hb4 * 4 + 4]
                S.add("dve", lambda rd=rd, va3=va3: nc.vector.reciprocal(out=rd, in_=va3[:, :, 64]),
                      reads=[vb_], writes=[b_rden_])
                S.add("dve", lambda ya3=ya3, va3=va3, rd=rd, hb4=hb4: nc.vector.tensor_tensor(
                    out=ya3[:, hb4 * 4:(hb4 + 1) * 4, :], in0=va3[:, :, 0:64],
                    in1=rd.unsqueeze(2).to_broadcast([128, 4, 64]), op=ALU.mult),
                    reads=[vb_, b_rden_], writes=[yb_])

    def na_T(il):
        ya, yb_ = ybt[il % 2]
        for c in range(4):
            S.add("pe", lambda c=c: nc.tensor.transpose(yTp3[:, c, :], ya[:, c * 128:(c + 1) * 128], ident),
                  reads=[yb_, b_ident], writes=[yTp_b])
        dst = y_bT3[:, :, il * 128:(il + 1) * 128]
        S.add("dve", lambda dst=dst: nc.vector.tensor_copy(out=dst, in_=yTp3), reads=[yTp_b], writes=[b_ybT])

    units = [(il, c) for il in range(NT) for c in range(4)]
    LAG = 2

    def na_issue_S(u):
        il2, c2 = units[u]
        if il2 == 3 and c2 == 0:
            load_tab(tsa4, b_tsa, g_tsa, 17, 6)
            load_tab(tsb4, b_tsb, g_tsb, 23, 6)
        na_S(il2, c2, u)

    for u in range(min(LAG, len(units))):
        na_issue_S(u)
    for u, (il, c) in enumerate(units):
        if u + LAG < len(units):
            na_issue_S(u + LAG)
        na_PV(il, c, u)
        if c == 0 and il > 0:
            na_T(il - 1)
    na_T(NT - 1)

    ck(2, [("ybT", y_bT, b_ybT)])
    uT = arena[:, (R_A + 49152) // 2:(R_A + 65536) // 2]
    j_uT = join_region("sb", R_A + 49152, 16384)
    b_uTg = [Buf(f"uTg{g}", "sb", R_A + 49152 + g * 4096, 4096, parent=j_uT) for g in range(4)]
    uT3 = uT.rearrange("p (c t) -> p c t", c=4)
    gtmp = []
    o = R_D
    NG = 5
    for i in range(NG):
        gtmp.append(sb(f"gtmp{i}", o, 2048, F32)); o += 2048
    o = R_D + 12288
    wst, b_wst = sb("wst", o, 1024); o += 1024
    bst, b_bst = sb("bst", o, 2048, F32); o += 2048
    lngt, b_lngt = sb("lngt", o, 2048, F32); o += 2048
    lnbt, b_lnbt = sb("lnbt", o, 2048, F32); o += 2048
    vtmp = []
    for i in range(3):
        vtmp.append(sb(f"vtmp{i}", o, 2048, F32)); o += 2048
    lnsm = []
    for i in range(3):
        lnsm.append(sb(f"lnsm{i}", o, 48, F32)); o += 48
    assert o <= R_D + 28672, o
    o = R_D + 28672
    wr2 = []
    for i in range(2):
        wr2.append(sb(f"wrb{i}", o, 4096)); o += 4096
    wvv, b_wvv = sb("wvv", o, 8192); o += 8192
    assert o <= R_E, o
    vn = arena[:, R_B // 2:(R_B + 16384) // 2]
    j_vn = join_region("sb", R_B, 16384)
    b_vnj = [Buf(f"vn{j}", "sb", R_B + j * 1024, 1024, parent=j_vn) for j in range(NT)]
    vn3 = vn.rearrange("p (j f) -> p j f", j=NT)
    g_c2 = SemGroup("const2", "all")
    S.add("sp", lambda: nc.sync.dma_start(out=lngt, in_=lng_d.partition_broadcast(128)), writes=[b_lngt], dma=g_c2)
    S.add("sp", lambda: nc.sync.dma_start(out=lnbt, in_=lnb_d.partition_broadcast(128)), writes=[b_lnbt], dma=g_c2)
    S.add("sp", lambda: nc.sync.dma_start(out=bst, in_=bs_d.partition_broadcast(128)), writes=[b_bst], dma=g_c2)
    g_c2p = SemGroup("const2_sw", "all")
    S.add("pool", lambda: nc.gpsimd.dma_start(out=wst, in_=wst_d), writes=[b_wst], dma=g_c2p)
    g_wr2 = [SemGroup(f"wrb{i}", "slot") for i in range(2)]
    g_wvv = SemGroup("wvv", "slot")

    mm2 = [pbank(f"mmb{i}", i) for i in range(4)]
    unit = 0
    u_w = []
    for cp in range(2):
        wa, wb = wr2[cp % 2]
        wa3 = wa.rearrange("p (k f) -> p k f", k=8)
        S.add("pool", lambda wa3=wa3, cp=cp: nc.gpsimd.dma_start(out=wa3, in_=w_in3[:, :, C_U + cp * 256:C_U + (cp + 1) * 256]),
              writes=[wb], dma=g_wr2[cp % 2])
        u_w.append((wa3, wb))

    def u_unit(i):
        tg, c = i // 4, i % 4
        cp, cc = c // 2, c % 2
        wa3, wb = u_w[cp]
        pa, pb_ = mm2[i % 2]
        for kc in range(8):
            S.add("pe", lambda kc=kc: nc.tensor.matmul(pa, lhsT=wa3[:, kc, cc * 128:(cc + 1) * 128], rhs=hT3[:, kc, tg * 512:(tg + 1) * 512],
                                                       start=(kc == 0), stop=(kc == 7)), reads=[wb, b_hTg[tg]], writes=[pb_])
        S.add("act", lambda: nc.scalar.activation(out=uT3[:, c, tg * 512:(tg + 1) * 512], in_=pa, func=AF.Gelu), reads=[pb_], writes=[b_uTg[tg]])

    wvv3 = wvv.rearrange("p (k f) -> p k f", k=8)
    S.add("pool", lambda: nc.gpsimd.dma_start(out=wvv3, in_=w_in3[:, :, C_V:C_V + 512]), writes=[b_wvv], dma=g_wvv)
    def v_A(j):
        pa, pb_ = mm2[2 + j % 2]
        va_, vb_ = vtmp[j % 3]
        sm, smb = lnsm[j % 3]
        st_, mv_, r_ = sm[:, 0:6], sm[:, 6:8], sm[:, 8:9]
        for kc in range(8):
            S.add("pe", lambda kc=kc: nc.tensor.matmul(pa, lhsT=hT3[:, kc, j * 128:(j + 1) * 128], rhs=wvv3[:, kc, :],
                                                       start=(kc == 0), stop=(kc == 7)),
                  reads=[b_wvv, b_hTg[j // 4]], writes=[pb_])
        S.add("act", lambda: nc.scalar.activation(out=va_, in_=pa, func=AF.Gelu), reads=[pb_], writes=[vb_])
        S.add("dve", lambda: nc.vector.bn_stats(out=st_, in_=va_), reads=[vb_], writes=[smb])
        S.add("dve", lambda: nc.vector.bn_aggr(out=mv_, in_=st_), reads=[smb], writes=[smb])
        S.add("pool", lambda: nc.gpsimd.tensor_scalar(out=r_, in0=mv_[:, 1:2], scalar1=LN_EPS, scalar2=None, op0=ALU.add),
              reads=[smb], writes=[smb])
        S.add("pool", lambda: nc.gpsimd.tensor_tensor(out=r_, in0=r_, in1=mhalf[:, 0:1], op=ALU.pow),
              reads=[smb, b_mhalf], writes=[smb])

    def v_B(j):
        va_, vb_ = vtmp[j % 3]
        sm, smb = lnsm[j % 3]
        mv_, r_ = sm[:, 6:8], sm[:, 8:9]
        S.add("dve", lambda: nc.vector.scalar_tensor_tensor(out=va_, in0=va_, scalar=mv_[:, 0:1], in1=lngt, op0=ALU.subtract, op1=ALU.mult),
              reads=[vb_, smb, b_lngt], writes=[vb_])
        S.add("dve", lambda: nc.vector.scalar_tensor_tensor(out=vn3[:, j, :], in0=va_, scalar=r_, in1=lnbt, op0=ALU.mult, op1=ALU.add),
              reads=[vb_, smb, b_lnbt], writes=[b_vnj[j]])

    wst3 = wst.rearrange("p (g q) -> p g q", g=4)
    mg = [pbank(f"mg{i}", 4 + i) for i in range(2)]

    def gmlp_mm(j):
        pa, pb_ = mg[j % 2]
        ga, gb_ = gtmp[j % NG]
        for g in range(4):
            S.add("pe", lambda g=g: nc.tensor.matmul(
                pa[:, g * 128:(g + 1) * 128], lhsT=vn3[:, j, g * 128:(g + 1) * 128], rhs=wst3[:, g, :], start=True, stop=True),
                reads=[b_vnj[j], b_wst], writes=[pb_])
        S.add("dve", lambda: nc.vector.tensor_tensor(out=ga, in0=pa, in1=bst, op=ALU.add), reads=[pb_, b_bst], writes=[gb_])

    def gmlp_mul(j):
        ga, gb_ = gtmp[j % NG]
        uv = uT3[:, :, j * 128:(j + 1) * 128]
        S.add("dve", lambda: nc.vector.tensor_tensor(out=uv, in0=ga.rearrange("p (g t) -> p g t", g=4), in1=uv, op=ALU.mult),
              reads=[gb_, b_uTg[j // 4]], writes=[b_uTg[j // 4]])

    woa, b_woa = sb("woa", R_C + 32768, 8192)
    wob, b_wob = sb("wob", R_C + 32768 + 8192, 8192)
    woa3 = woa.rearrange("p (k f) -> p k f", k=4)
    wob3 = wob.rearrange("p (k f) -> p k f", k=4)
    g_wo = SemGroup("wo", "all")
    wgA = [sb("wgA0", R_C + 0, 4096), sb("wgA1", R_C + 4096, 4096)]
    g_wgA = [SemGroup(f"wgA{i}", "slot") for i in range(2)]
    wgA_loaded = []
    for gi, cb in enumerate((C_G0, C_G1)):
        wa, wb = wgA[gi]
        wa3 = wa.rearrange("p (k f) -> p k f", k=8)
        S.add("pool", lambda wa3=wa3, cb=cb: nc.gpsimd.dma_start(out=wa3, in_=w_in3[:, :, cb:cb + 256]), writes=[wb], dma=g_wgA[gi])
        wgA_loaded.append((wa3, wb))
    S.add("pool", lambda: nc.gpsimd.dma_start(out=woa3, in_=w_o_a.rearrange("(k p) f -> p k f", p=128)), writes=[b_woa], dma=g_wo)
    S.add("pool", lambda: nc.gpsimd.dma_start(out=wob3, in_=w_o_b.rearrange("(k p) f -> p k f", p=128)), writes=[b_wob], dma=g_wo)
    wout_a, b_wout0 = sb("wout0", R_C + 49152, 8192)
    wout_b, b_wout1 = sb("wout1", R_C + 49152 + 8192, 8192)
    wout3 = arena[:, (R_C + 49152) // 2:(R_C + 49152 + 16384) // 2].rearrange("p (k f) -> p k f", k=8)
    g_wout = SemGroup("wout", "all")
    NXS = 4
    xs = [sb(f"xs{t}", R_C + 16384 + t * 4096, 4096, F32) for t in range(NXS)]
    g_xs = [SemGroup(f"xs{t}", "slot") for t in range(NXS)]
    for t in range(NXS):
        S.add("sp", lambda t=t: nc.sync.dma_start(out=xs[t][0], in_=x_ext[t * 128:(t + 1) * 128, :]), writes=[xs[t][1]], dma=g_xs[t])

    v_A(0)
    for j in range(NT):
        u_unit(j)
        if j + 1 < NT:
            v_A(j + 1)
        v_B(j)
        if j >= 1:
            gmlp_mm(j - 1)
        if j >= 4:
            gmlp_mul(j - 4)
    gmlp_mm(NT - 1)
    for j in range(NT - 4, NT):
        gmlp_mul(j)

    ck(3, [("uT", uT, b_uTg[3]), ("vn", vn, b_vnj[15])])
    y_aT3 = uT3

    ck(4, [("yaT", uT, b_uTg[3])])
    mT, b_mT = sb("mergedT", R_B, 32768)
    mT3 = mT.rearrange("p (k t) -> p k t", k=8)
    S.add("pool", lambda: nc.gpsimd.dma_start(out=wout3[:, 0:4, :], in_=w_out.rearrange("(k p) f -> p k f", p=128)[:, 0:4, :]),
          writes=[b_wout0], dma=g_wout)
    S.add("pool", lambda: nc.gpsimd.dma_start(out=wout3[:, 4:8, :], in_=w_out.rearrange("(k p) f -> p k f", p=128)[:, 4:8, :]),
          writes=[b_wout1], dma=g_wout)
    wgB = [sb("wgB0", R_D + 0, 4096), sb("wgB1", R_D + 4096, 4096)]
    wgC = [sb("wgC0", R_D + 36864, 4096), sb("wgC1", R_D + 40960, 4096)]
    g_wgB = [SemGroup(f"wgB{i}", "slot") for i in range(2)]
    g_wgC = [SemGroup(f"wgC{i}", "slot") for i in range(2)]
    tt = []
    o = R_D + 8192
    for i in range(4):
        tt.append(sb(f"tt{i}", o, 2048, F32)); o += 2048
    ffw = []
    g_ff = [(SemGroup(f"ff1_{i}", "slot"), SemGroup(f"ff2_{i}", "slot")) for i in range(3)]
    w_ff1_3 = w_ff1.rearrange("(k p) f -> p k f", p=128)
    w_ff2_3 = w_ff2.rearrange("(k p) f -> p k f", p=128)

    def load_ff(part):
        (w1a, w1b), (w2a, w2b) = ffw[part % 3]
        g1_, g2_ = g_ff[part % 3]
        S.add("pool", lambda: nc.gpsimd.dma_start(out=w1a.rearrange("p (k f) -> p k f", k=8),
                                                  in_=w_ff1_3[:, :, part * 512:(part + 1) * 512]), writes=[w1b], dma=g1_)
        S.add("pool", lambda: nc.gpsimd.dma_start(out=w2a.rearrange("p (k f) -> p k f", k=4),
                                                  in_=w_ff2_3[:, part * 4:(part + 1) * 4, :]), writes=[w2b], dma=g2_)

    gps = [[pbank(f"gp{i}_{k}", 4 * i + k) for k in range(4)] for i in range(2)]
    unit = 0
    for mp in range(DBG_MP):
        if mp == 0:
            was = wgA_loaded
        else:
            was = []
            bufs, grps = (wgB, g_wgB) if mp % 2 == 1 else (wgC, g_wgC)
            for gi, cb in enumerate((C_G0, C_G1)):
                wa, wb = bufs[gi]
                wa3 = wa.rearrange("p (k f) -> p k f", k=8)
                S.add("pool", lambda wa3=wa3, cb=cb, mp=mp: nc.gpsimd.dma_start(out=wa3, in_=w_in3[:, :, cb + mp * 256:cb + (mp + 1) * 256]),
                      writes=[wb], dma=grps[gi])
                was.append((wa3, wb))
        if mp == 2:
            ffw.append((sb("ff1_0", R_C, 8192), sb("ff2_0", R_C + 8192, 8192)))
            load_ff(0)
        for mm in range(2):
            m = mp * 2 + mm
            for tg in range(4):
                ps4 = gps[unit % 2]
                t0, t1, ta, tb = tt[0], tt[1], tt[2], tt[3]
                unit += 1
                tsl = slice(tg * 512, (tg + 1) * 512)
                for gi in range(2):
                    wa3, wb = was[gi]
                    pa, pb_ = ps4[gi]
                    for kc in range(8):
                        S.add("pe", lambda pa=pa, wa3=wa3, mm=mm, kc=kc, tsl=tsl: nc.tensor.matmul(
                            pa, lhsT=wa3[:, kc, mm * 128:(mm + 1) * 128], rhs=hT3[:, kc, tsl], start=(kc == 0), stop=(kc == 7)),
                            reads=[wb, b_hTg[tg]], writes=[pb_])
                pa, pb_ = ps4[2]
                for kc in range(4):
                    S.add("pe", lambda pa=pa, kc=kc, m=m, tsl=tsl: nc.tensor.matmul(
                        pa, lhsT=woa3[:, kc, m * 128:(m + 1) * 128], rhs=y_aT3[:, kc, tsl], start=(kc == 0), stop=(kc == 3)),
                        reads=[b_woa, b_uTg[tg]], writes=[pb_])
                pa, pb_ = ps4[3]
                for kc in range(4):
                    S.add("pe", lambda pa=pa, kc=kc, m=m, tsl=tsl: nc.tensor.matmul(
                        pa, lhsT=wob3[:, kc, m * 128:(m + 1) * 128], rhs=y_bT3[:, kc, tsl], start=(kc == 0), stop=(kc == 3)),
                        reads=[b_wob, b_ybT], writes=[pb_])
                S.add("act", lambda t0=t0, ps4=ps4, m=m: nc.scalar.activation(out=t0[0], in_=ps4[0][0], func=AF.Tanh, scale=0.5,
                                                                              bias=hbg[:, m:m + 1]),
                      reads=[ps4[0][1], b_hbg], writes=[t0[1]])
                S.add("act", lambda t1=t1, ps4=ps4, m=m: nc.scalar.activation(out=t1[0], in_=ps4[1][0], func=AF.Tanh, scale=0.5,
                                                                              bias=hbg[:, 8 + m:8 + m + 1]),
                      reads=[ps4[1][1], b_hbg], writes=[t1[1]])
                S.add("dve", lambda ta=ta, t0=t0, ps4=ps4: nc.vector.scalar_tensor_tensor(
                    out=ta[0], in0=t0[0], scalar=1.0, in1=ps4[2][0], op0=ALU.add, op1=ALU.mult),
                    reads=[t0[1], ps4[2][1]], writes=[ta[1]])
                S.add("dve", lambda tb=tb, t1=t1, ps4=ps4: nc.vector.scalar_tensor_tensor(
                    out=tb[0], in0=t1[0], scalar=1.0, in1=ps4[3][0], op0=ALU.add, op1=ALU.mult),
                    reads=[t1[1], ps4[3][1]], writes=[tb[1]])
                S.add("dve", lambda ta=ta, tb=tb, m=m, tsl=tsl: nc.vector.tensor_tensor(
                    out=mT3[:, m, tsl], in0=ta[0], in1=tb[0], op=ALU.add),
                    reads=[ta[1], tb[1]], writes=[b_mT])

    ck(5, [("mT", mT, b_mT)])
    j_RA = join_region("sb", R_A, 65536)
    x1 = []
    for t in range(NT):
        x1.append(sb(f"x1_{t}", R_A + t * 4096, 4096, F32, parent=j_RA))
    g_x1 = [SemGroup(f"x1_{t}", "slot") for t in range(NT)]

    def x_reload(t, after=None):
        xa, xb_ = x1[t]
        S.add("sp", lambda: nc.sync.dma_start(out=xa, in_=x_ext[t * 128:(t + 1) * 128, :]),
              reads=[after] if after is not None else [], writes=[xb_], dma=g_x1[t])

    for t in range(NXS, NXS + 3):
        x_reload(t)
    h2T = arena[:, R_D // 2:(R_D + 32768) // 2]
    j_RD = join_region("sb", R_D, 32768)
    b_h2Tg = [Buf(f"h2Tg{g}", "sb", R_D + g * 8192, 8192, parent=j_RD) for g in range(4)]
    h2T3 = h2T.rearrange("p (k t) -> p k t", k=8)
    o = R_D + 32768
    h2b = []
    for i in range(2):
        h2b.append(sb(f"h2b{i}", o, 2048)); o += 2048
    g2t, b_g2t = sb("g2t", o, 4096, F32); o += 4096
    junk2s = []
    for i in range(2):
        junk2s.append(sb(f"junk2_{i}", o, 2048)); o += 2048
    assert o <= R_E, o
    b_ss2c = [Buf(f"ss2_{t}", "sb", R_E + 888 + 4 * t, 4) for t in range(NT)]
    b_v2c = [Buf(f"v2_{t}", "sb", R_E + 1216 + 4 * t, 4) for t in range(NT)]
    b_rstd2c = [Buf(f"rstd2_{t}", "sb", R_E + 952 + 4 * t, 4) for t in range(NT)]
    g_c3 = SemGroup("const3", "all")
    S.add("sp", lambda: nc.sync.dma_start(out=g2t, in_=g2_d.partition_broadcast(128)), writes=[b_g2t], dma=g_c3)

    wo_ps = [pbank(f"wo{i}", i) for i in range(4)]
    pT2 = [pbank(f"pTb{i}", 4 + i, dtype=BF16) for i in range(2)]
    unit = 0

    def n2_A(t):
        xa, xb_ = x1[t]
        ha, hb_ = h2b[t % 2]
        ja, jb_ = junk2s[t % 2]
        S.add("act", lambda: nc.scalar.activation(out=ja, in_=xa, func=AF.Square, accum_out=ss2[:, t:t + 1]),
              reads=[xb_], writes=[jb_, b_ss2c[t]])
        S.add("pool", lambda: nc.gpsimd.tensor_scalar(out=v2t[:, t:t + 1], in0=ss2[:, t:t + 1], scalar1=1.0 / D, scalar2=RMS_EPS,
                                                      op0=ALU.mult, op1=ALU.add), reads=[b_ss2c[t]], writes=[b_v2c[t]])
        S.add("pool", lambda: nc.gpsimd.tensor_tensor(out=rstd2[:, t:t + 1], in0=v2t[:, t:t + 1], in1=mhalf[:, 0:1], op=ALU.pow),
              reads=[b_v2c[t], b_mhalf], writes=[b_rstd2c[t]])
        S.add("dve", lambda: nc.vector.scalar_tensor_tensor(out=ha, in0=xa, scalar=rstd2[:, t:t + 1], in1=g2t, op0=ALU.mult, op1=ALU.mult),
              reads=[xb_, b_rstd2c[t], b_g2t], writes=[hb_])

    def n2_B(t):
        ha, hb_ = h2b[t % 2]
        pa, pb_ = pT2[t % 2]
        pa3 = pa.rearrange("p (k t) -> p k t", k=8)
        for kc in range(8):
            S.add("pe", lambda kc=kc: nc.tensor.transpose(pa3[:, kc, :], ha[:, kc * 128:(kc + 1) * 128], ident),
                  reads=[hb_, b_ident], writes=[pb_])
        dst = h2T3[:, :, t * 128:(t + 1) * 128]
        S.add("act", lambda: nc.scalar.copy(out=dst, in_=pa3), reads=[pb_], writes=[b_h2Tg[t // 4]])

    for t in range(NT):
        xa, xb_ = x1[t]
        for dh in range(2):
            pa, pb_ = wo_ps[unit % 4]
            unit += 1
            for kc in range(8):
                S.add("pe", lambda pa=pa, kc=kc, t=t, dh=dh: nc.tensor.matmul(
                    pa, lhsT=mT3[:, kc, t * 128:(t + 1) * 128], rhs=wout3[:, kc, dh * 512:(dh + 1) * 512],
                    start=(kc == 0), stop=(kc == 7)), reads=[b_mT, b_wout0 if kc < 4 else b_wout1], writes=[pb_])
            if t < NXS:
                src, srcb = xs[t]
            else:
                src, srcb = xa, xb_
            S.add("dve", lambda pa=pa, xa=xa, dh=dh, src=src: nc.vector.scalar_tensor_tensor(
                out=xa[:, dh * 512:(dh + 1) * 512], in0=pa, scalar=0.5, in1=src[:, dh * 512:(dh + 1) * 512],
                op0=ALU.mult, op1=ALU.add), reads=[pb_, srcb], writes=[xb_])
        if t >= NXS and t + 3 < NT:
            x_reload(t + 3, after=xb_)
        if t >= 1:
            n2_A(t - 1)
        if t >= 2:
            n2_B(t - 2)
    def n2_tail():
        n2_A(NT - 1)
        n2_B(NT - 2)
        n2_B(NT - 1)

    ffw.append((sb("ff1_1", R_C + 16384, 8192), sb("ff2_1", R_C + 16384 + 8192, 8192)))
    ffw.append((sb("ff1_2", R_C + 2 * 16384, 8192), sb("ff2_2", R_C + 2 * 16384 + 8192, 8192)))
    load_ff(1)
    load_ff(2)
    ck(6, [("x1_%d" % t, x1[t][0], x1[t][1]) for t in range(NT)] + [("h2T", h2T, b_h2Tg[3])])
    o = R_B
    actT = []
    for i in range(2):
        actT.append(sb(f"actT{i}", o, 4096)); o += 4096
    sqf = []
    for i in range(2):
        sqf.append(sb(f"sqf{i}", o, 2048, F32)); o += 2048
    f1_ps = [pbank(f"f1p{i}", i) for i in range(3)]
    f2_ps = [pbank(f"f2p{i}", b) for i, b in enumerate((3, 6, 7))]
    g_out = SemGroup("out", "all")
    u1 = 0
    u2 = 0

    def ff1(part, tg, ui):
        nonlocal u1
        (w1a, w1b), _ = ffw[part % 3]
        w13 = w1a.rearrange("p (k f) -> p k f", k=8)
        aa, ab_ = actT[ui % 2]
        aa3 = aa.rearrange("p (c t) -> p c t", c=4)
        for fc in range(4):
            pa, pb_ = f1_ps[u1 % 3]
            qa, qb_ = sqf[u1 % 2]
            u1 += 1
            for kc in range(8):
                S.add("pe", lambda pa=pa, w13=w13, fc=fc, kc=kc, tg=tg: nc.tensor.matmul(
                    pa, lhsT=w13[:, kc, fc * 128:(fc + 1) * 128], rhs=h2T3[:, kc, tg * 512:(tg + 1) * 512],
                    start=(kc == 0), stop=(kc == 7)), reads=[w1b, b_h2Tg[tg]], writes=[pb_])
            S.add("act", lambda qa=qa, pa=pa: nc.scalar.activation(out=qa, in_=pa, func=AF.Square), reads=[pb_], writes=[qb_])
            S.add("dve", lambda aa3=aa3, fc=fc, pa=pa, qa=qa: nc.vector.scalar_tensor_tensor(
                out=aa3[:, fc, :], in0=pa, scalar=0.0, in1=qa, op0=ALU.is_gt, op1=ALU.mult),
                reads=[pb_, qb_], writes=[ab_])

    def ff2(part, tg, ui):
        nonlocal u2
        _, (w2a, w2b) = ffw[part % 3]
        w23 = w2a.rearrange("p (k f) -> p k f", k=4)
        aa, ab_ = actT[ui % 2]
        aa3 = aa.rearrange("p (c t) -> p c t", c=4)
        for tt_ in range(4):
            t = tg * 4 + tt_
            xa, xb_ = x1[t]
            for dh in range(2):
                pa, pb_ = f2_ps[u2 % 3]
                u2 += 1
                for fc in range(4):
                    S.add("pe", lambda pa=pa, aa3=aa3, w23=w23, fc=fc, tt_=tt_, dh=dh: nc.tensor.matmul(
                        pa, lhsT=aa3[:, fc, tt_ * 128:(tt_ + 1) * 128], rhs=w23[:, fc, dh * 512:(dh + 1) * 512],
                        start=(fc == 0), stop=(fc == 3)), reads=[ab_, w2b], writes=[pb_])
                S.add("dve", lambda pa=pa, xa=xa, dh=dh: nc.vector.tensor_tensor(
                    out=xa[:, dh * 512:(dh + 1) * 512], in0=pa, in1=xa[:, dh * 512:(dh + 1) * 512], op=ALU.add),
                    reads=[pb_, xb_], writes=[xb_])
            if part == 7:
                S.add("sp", lambda xa=xa, t=t: nc.sync.dma_start(out=out_d[t * 128:(t + 1) * 128, :], in_=xa),
                      reads=[xb_], dma=g_out)

    funits = [(p, tg) for p in range(8) for tg in range(4)]
    ff1(*funits[0], 0)
    for ui, (p, tg) in enumerate(funits):
        if ui + 1 < len(funits):
            ff1(*funits[ui + 1], ui + 1)
        ff2(p, tg, ui)
        if ui == 1:
            n2_tail()
        if tg == 3 and p + 3 < 8:
            load_ff(p + 3)

    S.frozen = False if stop is None else S.frozen
    S.emit([g_out] if stop is None else list(SemGroup.all_groups))
    return nc, S


def _bias_tables(rpb, half):
    def pattern(il, o):
        i = 16 * half + il
        j = i + o
        tab = np.full((8, 128, 128), NEG, dtype=np.float32)
        if j < 0 or j > 31:
            return tab
        a = np.arange(2)[:, None]; kc = np.arange(64)[None, :]
        kr = (2 * j + a + 0 * kc).reshape(-1)
        kcc = (0 * a + kc).reshape(-1)
        r = (2 * i + a + 0 * kc).reshape(-1)
        c = kcc.copy()
        r0 = np.clip(r - 4, 0, 56)
        cs = np.clip(c - 8, 0, 48)
        KR, R = kr[:, None], r[None, :]
        KC, C = kcc[:, None], c[None, :]
        valid = (KR >= r0[None, :]) & (KR < r0[None, :] + 8) & (KC >= cs[None, :]) & (KC < cs[None, :] + 16)
        dr = np.clip(KR - R + 7, 0, 14)
        dc = np.clip(KC - C + 15, 0, 30)
        g = rpb[:, dr, dc]
        return np.where(valid[None], g, tab)
    pats = []
    pats += [pattern(0, o) for o in range(-2, 4)]
    pats += [pattern(1, o) for o in range(-2, 3)] + [np.full((8, 128, 128), NEG, np.float32)]
    pats += [pattern(8, o) for o in range(-2, 3)]
    pats += [pattern(14, o) for o in range(-2, 3)] + [np.full((8, 128, 128), NEG, np.float32)]
    pats += [pattern(15, o) for o in range(-3, 3)]
    arr = np.stack(pats, 0)
    arr = arr.transpose(2, 0, 1, 3).reshape(128, NPAT * 8 * 128)
    return np.ascontiguousarray(arr)


_CACHE = {}


def kernel(x, norm1_g, w_in, b_gate, gmlp_ln_g, gmlp_ln_b, gmlp_w_s, gmlp_b_s,
           na_q_g, na_k_g, na_rpb, w_o_a, w_o_b, w_out, norm2_g, w_ff1, w_ff2):
    if "nc" not in _CACHE:
        _CACHE["nc"] = build_program()[0]
    nc = _CACHE["nc"]
    in_maps = make_in_maps(x, norm1_g, w_in, b_gate, gmlp_ln_g, gmlp_ln_b, gmlp_w_s, gmlp_b_s,
                           na_q_g, na_k_g, na_rpb, w_o_a, w_o_b, w_out, norm2_g, w_ff1, w_ff2)
    res = run_bass_kernel_spmd(nc, in_maps, core_ids=list(range(8)))
    out = np.empty((4, 2 * NTOK, D), np.float32)
    for core in range(8):
        b, hf = core // 2, core % 2
        out[b, hf * NTOK:(hf + 1) * NTOK] = res.results[core]["out"]
    return out


def make_in_maps(x, norm1_g, w_in, b_gate, gmlp_ln_g, gmlp_ln_b, gmlp_w_s, gmlp_b_s,
                 na_q_g, na_k_g, na_rpb, w_o_a, w_o_b, w_out, norm2_g, w_ff1, w_ff2):
    f = lambda a: np.ascontiguousarray(np.asarray(a, dtype=np.float32))
    x = f(x)
    shared = {
        "w_in": f(w_in[0]), "w_o_a": f(w_o_a[0]), "w_o_b": f(w_o_b[0]), "w_out": f(w_out[0]),
        "w_ff1": f(w_ff1[0]), "w_ff2": f(w_ff2[0]),
        "norm1_g": f(norm1_g[0]).reshape(1, D), "norm2_g": f(norm2_g[0]).reshape(1, D),
        "ln_g": f(gmlp_ln_g[0]).reshape(1, 512), "ln_b": f(gmlp_ln_b[0]).reshape(1, 512),
        "b_s": f(gmlp_b_s[0]).reshape(1, 512),
        "w_sT": f(np.transpose(np.asarray(gmlp_w_s[0]), (2, 0, 1)).reshape(128, 512)),
        "ident": np.eye(128, dtype=np.float32),
        "bones": np.kron(np.eye(2, dtype=np.float32), np.ones((64, 64), np.float32)),
    }
    qg = np.tile(np.asarray(na_q_g[0], np.float32), 2).reshape(128, 1)
    kg = np.tile(np.asarray(na_k_g[0], np.float32), 2).reshape(128, 1)
    bg = np.asarray(b_gate[0], np.float32).reshape(16, 128).T
    z64 = np.zeros((64, 1), np.float32)
    qg0 = np.concatenate([qg[:64], z64], axis=0)
    qg1 = np.concatenate([z64, qg[64:]], axis=0)
    shared["smalls"] = f(np.concatenate([qg, kg, bg, qg0, qg1], axis=1))
    tabs = [_bias_tables(np.asarray(na_rpb[0], np.float32), hf) for hf in range(2)]
    zeros = np.zeros((256, D), np.float32)
    in_maps = []
    for core in range(8):
        b, hf = core // 2, core % 2
        own = x[b, hf * NTOK:(hf + 1) * NTOK]
        before = x[b, NTOK - 256:NTOK] if hf == 1 else zeros
        after = x[b, NTOK:NTOK + 256] if hf == 0 else zeros
        m = dict(shared)
        m["x_ext"] = np.ascontiguousarray(np.concatenate([own, before, after], axis=0))
        m["btab"] = tabs[hf]
        in_maps.append(m)
    return in_maps
```
